# Optimizing a Trainium2 kernel written in Bass

```python
import jax, jax.numpy as jnp
from jax import lax
import numpy as np

D_MODEL = 1024
BATCH = 2
SEQ = 8192
DEPTH = 2
DEC_BATCH = 128
DEC_SEQ = 8
PAST_LEN = 2048
PAGE_SIZE = 128

N_A = DEPTH // 2
N_B = DEPTH - N_A
D_CONV = D_MODEL
CONV_W = 3
N_HEADS = 16
HEAD_DIM = D_MODEL // N_HEADS
N_KV_HEADS = 4
GROUP = N_HEADS // N_KV_HEADS
CMP_BLK = 32
CMP_STRIDE = 16
CMP_RATIO = CMP_BLK // CMP_STRIDE
CMP_HID = HEAD_DIM
SEL_BLK = 64
TOP_N = 16
WINDOW = 512
Q_BLK = 64
D_FF = ((8 * D_MODEL // 3 + 255) // 256) * 256
RMS_EPS = 1e-6
BIG = 1e9
NEG = -1e30

kernel_name = 'yoco_shortconv_nsa_decode_step'


def rmsnorm(x, w):
    xf = x.astype(jnp.float32)
    y = xf * lax.rsqrt(jnp.mean(xf * xf, axis=-1, keepdims=True) + RMS_EPS)
    return (y * w.astype(jnp.float32)).astype(x.dtype)


def swiglu(x, w_in, w_out):
    g, u = jnp.split(x @ w_in, 2, axis=-1)
    return (jax.nn.silu(g) * u) @ w_out


def short_conv_mixer(xn, w_in, conv_w, w_out, prefix):
    b, c, u = jnp.split(xn @ w_in, 3, axis=-1)
    v = c * u
    vp = jnp.concatenate([prefix.astype(v.dtype), v], axis=1)
    T = v.shape[1]
    conv = conv_w[0] * vp[:, 0:T]
    for j in range(1, CONV_W):
        conv = conv + conv_w[j] * vp[:, j:j + T]
    return (b * conv) @ w_out, vp[:, -(CONV_W - 1):]


def masked_softmax(s, mask):
    s = jnp.where(mask, s, NEG)
    m = jnp.max(s, axis=-1, keepdims=True)
    e = jnp.where(mask, jnp.exp(s - m), 0.0)
    return e / jnp.maximum(jnp.sum(e, axis=-1, keepdims=True), 1e-30)


def shared_kv(h, kv_norm_w, kv_w):
    N, T, _ = h.shape
    return (rmsnorm(h, kv_norm_w) @ kv_w).reshape(N, T, 3, 2, N_KV_HEADS, HEAD_DIM)


def compress(kv, cmp_pe, cmp_w1, cmp_w2):
    N, L = kv.shape[:2]
    kv = jnp.pad(kv, ((0, 0), (0, (-L) % CMP_STRIDE), (0, 0), (0, 0), (0, 0)))
    C = kv.shape[1] // CMP_STRIDE
    NC = C - CMP_RATIO + 1
    chunks = kv.reshape(N, C, CMP_STRIDE, 2, N_KV_HEADS, HEAD_DIM)
    w1 = cmp_w1.reshape(2, CMP_RATIO, CMP_STRIDE, HEAD_DIM, CMP_HID)
    part = jnp.einsum('ncjkgd,krjdh->rnckgh', chunks, w1)
    pre = part[0, :, 0:NC]
    for r in range(1, CMP_RATIO):
        pre = pre + part[r, :, r:r + NC]
    pe_bias = jnp.einsum('kjd,kjdh->kh', cmp_pe, cmp_w1)
    hid = jax.nn.silu(pre + pe_bias[:, None, :])
    out = jnp.einsum('nckgh,khe->nckge', hid, cmp_w2)
    ends = jnp.arange(NC) * CMP_STRIDE + CMP_BLK - 1
    return out, ends


def sel_blocks(kv):
    N, L = kv.shape[:2]
    kv = jnp.pad(kv, ((0, 0), (0, (-L) % SEL_BLK), (0, 0), (0, 0), (0, 0)))
    kv = kv.reshape(N, -1, SEL_BLK, 2, N_KV_HEADS, HEAD_DIM)
    return jnp.moveaxis(kv, 4, 1)


def query_side(xn, w_in):
    N, T, _ = xn.shape
    proj = xn @ w_in
    q = proj[..., :N_HEADS * HEAD_DIM].reshape(N, T, N_KV_HEADS, GROUP, HEAD_DIM) * (HEAD_DIM ** -0.5)
    g = jax.nn.sigmoid(proj[..., N_HEADS * HEAD_DIM:].astype(jnp.float32)).astype(xn.dtype)
    return q, g.reshape(N, T, N_KV_HEADS, GROUP, 3)


def nsa_block(q, g, q_pos, kvc, cmp_end, kvs, kvw, kw_pos):
    N, T = q.shape[:2]
    s_c = jnp.einsum('ntgrd,ncgd->ntgrc', q, kvc[:, :, 0]).astype(jnp.float32)
    mask_c = (cmp_end[None, :] <= q_pos[:, None])[None, :, None, None, :]
    p_c = masked_softmax(s_c, mask_c)
    o_c = jnp.einsum('ntgrc,ncgd->ntgrd', p_c.astype(kvc.dtype), kvc[:, :, 1])
    NC = kvc.shape[1]
    NSEL = kvs.shape[2]
    c_start = jnp.arange(NC) * CMP_STRIDE
    s_start = jnp.arange(NSEL) * SEL_BLK
    overlap = jnp.clip(jnp.minimum(c_start[:, None] + CMP_BLK, s_start[None, :] + SEL_BLK)
                       - jnp.maximum(c_start[:, None], s_start[None, :]), 0, None)
    agg = overlap.astype(jnp.float32) / CMP_STRIDE
    imp = jnp.einsum('ntgrc,cj->ntgj', p_c, agg)
    valid = s_start[None, :] <= q_pos[:, None]
    cur = (q_pos // SEL_BLK)[:, None]
    j = jnp.arange(NSEL)[None, :]
    forced = (j == 0) | (j == cur) | (j == cur - 1)
    score = jnp.where((valid & forced)[None, :, None, :], BIG,
                      jnp.where(valid[None, :, None, :], imp, -BIG))
    _, idx = lax.top_k(score, min(TOP_N, NSEL))
    K = idx.shape[-1]
    n_ix = jnp.arange(N)[:, None, None, None]
    g_ix = jnp.arange(N_KV_HEADS)[None, None, :, None]
    sel = kvs[n_ix, g_ix, idx]
    sel = sel.reshape(N, T, N_KV_HEADS, K * SEL_BLK, 2, HEAD_DIM)
    tok = (idx[..., None] * SEL_BLK + jnp.arange(SEL_BLK)).reshape(N, T, N_KV_HEADS, K * SEL_BLK)
    mask_s = (tok <= q_pos[None, :, None, None])[:, :, :, None, :]
    s_s = jnp.einsum('ntgrd,ntgkd->ntgrk', q, sel[..., 0, :]).astype(jnp.float32)
    p_s = masked_softmax(s_s, mask_s)
    o_s = jnp.einsum('ntgrk,ntgkd->ntgrd', p_s.astype(sel.dtype), sel[..., 1, :])
    s_w = jnp.einsum('ntgrd,nkgd->ntgrk', q, kvw[:, :, 0]).astype(jnp.float32)
    mask_w = ((kw_pos[None, :] <= q_pos[:, None]) & (kw_pos[None, :] >= q_pos[:, None] - WINDOW)
              & (kw_pos[None, :] >= 0))[None, :, None, None, :]
    p_w = masked_softmax(s_w, mask_w)
    o_w = jnp.einsum('ntgrk,nkgd->ntgrd', p_w.astype(kvw.dtype), kvw[:, :, 1])
    return g[..., 0:1] * o_c + g[..., 1:2] * o_s + g[..., 2:3] * o_w


def nsa_prompt(xn, w_in, w_out, ctx):
    kvc, cmp_end, kvs, kvw = ctx
    N, T, _ = xn.shape
    q, g = query_side(xn, w_in)
    nb = T // Q_BLK
    kvw_pad = jnp.pad(kvw, ((0, 0), (WINDOW, 0), (0, 0), (0, 0), (0, 0)))

    def block(args):
        qb, gb, start = args
        q_pos = start + jnp.arange(Q_BLK)
        kw = lax.dynamic_slice_in_dim(kvw_pad, start, WINDOW + Q_BLK, axis=1)
        kw_pos = start - WINDOW + jnp.arange(WINDOW + Q_BLK)
        return nsa_block(qb, gb, q_pos, kvc, cmp_end, kvs, kw, kw_pos)

    def to_blocks(a):
        return a.reshape(N, nb, Q_BLK, *a.shape[2:]).swapaxes(0, 1)

    o = lax.map(block, (to_blocks(q), to_blocks(g), jnp.arange(nb) * Q_BLK))
    o = o.swapaxes(0, 1).reshape(N, T, N_HEADS * HEAD_DIM)
    return o @ w_out


def nsa_sample(xn, w_in, w_out, ctx):
    kvc, cmp_end, kvs, kvw, kw_pos = ctx
    N, T, _ = xn.shape
    q, g = query_side(xn, w_in)

    def step(args):
        qt, gt, pos = args
        return nsa_block(qt[:, None], gt[:, None], pos[None], kvc, cmp_end, kvs, kvw, kw_pos)[:, 0]

    o = lax.map(step, (q.swapaxes(0, 1), g.swapaxes(0, 1), PAST_LEN + jnp.arange(T)))
    o = o.swapaxes(0, 1).reshape(N, T, N_HEADS * HEAD_DIM)
    return o @ w_out


def prompt_kv_side(h, kv_norm_w, kv_w, cmp_pe, cmp_w1, cmp_w2):
    kv = shared_kv(h, kv_norm_w, kv_w)
    kvc, ends = compress(kv[:, :, 0], cmp_pe, cmp_w1, cmp_w2)
    kvs = sel_blocks(kv[:, :, 1])
    kvw = kv[:, :, 2]
    keep = min(WINDOW, h.shape[1])
    return (kvc, ends, kvs, kvw), (kv[:, :, 0], kv[:, :, 1], kvw[:, -keep:])


def sample_kv_side(h, cache_cmp_kv, cache_slc_kv, state_win_kv, page_table, kv_norm_w, kv_w, cmp_pe, cmp_w1, cmp_w2):
    N, T, _ = h.shape
    kv = shared_kv(h, kv_norm_w, kv_w)

    def past(pool):
        return pool[page_table].reshape(N, -1, 2, N_KV_HEADS, HEAD_DIM)

    full_cmp = jnp.concatenate([past(cache_cmp_kv), kv[:, :, 0]], axis=1)
    full_slc = jnp.concatenate([past(cache_slc_kv), kv[:, :, 1]], axis=1)
    kvc, ends = compress(full_cmp, cmp_pe, cmp_w1, cmp_w2)
    kvs = sel_blocks(full_slc)
    win = jnp.concatenate([state_win_kv, kv[:, :, 2]], axis=1)
    w_len = state_win_kv.shape[1]
    kw_pos = PAST_LEN - w_len + jnp.arange(w_len + T)
    keep = min(WINDOW, PAST_LEN + T)
    return (kvc, ends, kvs, win, kw_pos), (kv[:, :, 0], kv[:, :, 1], win[:, -keep:])


def trunk(x, conv_prefix, kv_side, nsa_mixer, norm_w, final_norm_w, a_in_w, a_conv_w, a_out_w,
          b_in_w, b_out_w, ffn_in_w, ffn_out_w):
    h = x
    conv_states = []
    ctx, kv_rows = None, None
    for l in range(DEPTH):
        xn = rmsnorm(h, norm_w[l, 0])
        if l < N_A:
            y, st = short_conv_mixer(xn, a_in_w[l], a_conv_w[l], a_out_w[l], conv_prefix[l])
            conv_states.append(st)
        else:
            if l == N_A:
                ctx, kv_rows = kv_side(h)
            y = nsa_mixer(xn, b_in_w[l - N_A], b_out_w[l - N_A], ctx)
        h = h + y
        h = h + swiglu(rmsnorm(h, norm_w[l, 1]), ffn_in_w[l], ffn_out_w[l])
    return rmsnorm(h, final_norm_w), jnp.stack(conv_states), kv_rows


def setup_inputs(seed: int = 0) -> dict:
    key = jax.random.key(seed)
    ks = jax.random.split(key, 24)
    f32 = jnp.float32

    def nrm(k, shape, scale):
        return jax.random.normal(k, shape, f32) * scale

    n_pages = PAST_LEN // PAGE_SIZE
    n_used = DEC_BATCH * n_pages
    n_pool = n_used + max(1, n_used // 4)
    page_table = jax.random.permutation(ks[0], n_pool)[:n_used].reshape(DEC_BATCH, n_pages).astype(jnp.int32)
    kv_row = (2, N_KV_HEADS, HEAD_DIM)
    qg = N_HEADS * HEAD_DIM + 3 * N_HEADS
    return {
        'x_prompt': nrm(ks[1], (BATCH, SEQ, D_MODEL), 1.0),
        'x_sample': nrm(ks[2], (DEC_BATCH, DEC_SEQ, D_MODEL), 1.0),
        'cache_cmp_kv': nrm(ks[3], (n_pool, PAGE_SIZE) + kv_row, 1.0),
        'cache_slc_kv': nrm(ks[4], (n_pool, PAGE_SIZE) + kv_row, 1.0),
        'state_win_kv': nrm(ks[5], (DEC_BATCH, min(WINDOW, PAST_LEN)) + kv_row, 1.0),
        'state_conv': nrm(ks[6], (N_A, DEC_BATCH, CONV_W - 1, D_CONV), 1.0),
        'page_table': page_table,
        'norm_w': 1.0 + nrm(ks[7], (DEPTH, 2, D_MODEL), 0.02),
        'final_norm_w': 1.0 + nrm(ks[8], (D_MODEL,), 0.02),
        'a_in_w': nrm(ks[9], (N_A, D_MODEL, 3 * D_CONV), D_MODEL ** -0.5),
        'a_conv_w': nrm(ks[10], (N_A, CONV_W, D_CONV), CONV_W ** -0.5),
        'a_out_w': nrm(ks[11], (N_A, D_CONV, D_MODEL), D_CONV ** -0.5),
        'b_in_w': nrm(ks[12], (N_B, D_MODEL, qg), D_MODEL ** -0.5),
        'b_out_w': nrm(ks[13], (N_B, N_HEADS * HEAD_DIM, D_MODEL), (N_HEADS * HEAD_DIM) ** -0.5),
        'kv_norm_w': 1.0 + nrm(ks[14], (D_MODEL,), 0.02),
        'kv_w': nrm(ks[15], (D_MODEL, 3 * 2 * N_KV_HEADS * HEAD_DIM), D_MODEL ** -0.5),
        'cmp_pe': nrm(ks[16], (2, CMP_BLK, HEAD_DIM), 0.1),
        'cmp_w1': nrm(ks[17], (2, CMP_BLK, HEAD_DIM, CMP_HID), (CMP_BLK * HEAD_DIM) ** -0.5),
        'cmp_w2': nrm(ks[18], (2, CMP_HID, HEAD_DIM), CMP_HID ** -0.5),
        'ffn_in_w': nrm(ks[19], (DEPTH, D_MODEL, 2 * D_FF), D_MODEL ** -0.5),
        'ffn_out_w': nrm(ks[20], (DEPTH, D_FF, D_MODEL), D_FF ** -0.5),
    }


def reference(x_prompt, x_sample, cache_cmp_kv, cache_slc_kv, state_win_kv, state_conv, page_table,
              norm_w, final_norm_w, a_in_w, a_conv_w, a_out_w, b_in_w, b_out_w, kv_norm_w, kv_w,
              cmp_pe, cmp_w1, cmp_w2, ffn_in_w, ffn_out_w):
    conv_zero = jnp.zeros((N_A, x_prompt.shape[0], CONV_W - 1, D_CONV), x_prompt.dtype)
    y_prompt, conv_p, kv_rows_p = trunk(
        x_prompt, conv_zero,
        lambda h: prompt_kv_side(h, kv_norm_w, kv_w, cmp_pe, cmp_w1, cmp_w2),
        nsa_prompt, norm_w, final_norm_w, a_in_w, a_conv_w, a_out_w, b_in_w, b_out_w, ffn_in_w, ffn_out_w)
    y_sample, conv_s, kv_rows_s = trunk(
        x_sample, state_conv,
        lambda h: sample_kv_side(h, cache_cmp_kv, cache_slc_kv, state_win_kv, page_table,
                                 kv_norm_w, kv_w, cmp_pe, cmp_w1, cmp_w2),
        nsa_sample, norm_w, final_norm_w, a_in_w, a_conv_w, a_out_w, b_in_w, b_out_w, ffn_in_w, ffn_out_w)
    cmp_p, slc_p, win_p = kv_rows_p
    cmp_s, slc_s, win_s = kv_rows_s
    return (y_prompt, y_sample, conv_p, conv_s, cmp_p, cmp_s, slc_p, slc_s, win_p, win_s)
```

```python
import contextlib
import os
DBG = os.environ.get('KDBG', '')
import numpy as np
import concourse.bass as bass
import concourse.mybir as mybir
from concourse.bass_utils import run_bass_kernel_spmd

F32 = mybir.dt.float32
BF16 = mybir.dt.bfloat16
I32 = mybir.dt.int32
AF = mybir.ActivationFunctionType
ALU = mybir.AluOpType

NCORES = 8
D = 1024
NT = 17
NPT = 16
DFF = 2816
NJ = 22
EPS = 1e-6


class Buf:
    __slots__ = ("name", "last_w", "readers")

    def __init__(self, name):
        self.name = name
        self.last_w = None
        self.readers = {}


class Phase:
    ENGS = ["pe", "act", "dve", "pool", "sp"]
    NDMA = 12

    def __init__(self, nc, es, tag):
        self.nc = nc
        self.tag = tag
        self.csem = {e: es.enter_context(nc.semaphore(f"{tag}_{e}")) for e in ["pe", "act", "dve", "pool"]}
        self.dsem = {q: [es.enter_context(nc.semaphore(f"{tag}_{q}d{i}")) for i in range(self.NDMA)]
                     for q in ["sp", "pool", "cc"]}
        self.ccnt = {e: 0 for e in ["pe", "act", "dve", "pool"]}
        self.dcnt = {q: [0] * self.NDMA for q in ["sp", "pool", "cc"]}
        self.dnext = {q: 0 for q in ["sp", "pool", "cc"]}
        self.waited = {e: {} for e in self.ENGS}
        self.prog = {e: [] for e in self.ENGS}
        self.rot = {}

    def sem(self, key):
        if key[0] == "c":
            return self.csem[key[1]]
        return self.dsem[key[1]][key[2]]

    def _deps(self, eng, reads, writes):
        deps = {}

        def add(k, v):
            if deps.get(k, 0) < v:
                deps[k] = v

        for r in reads:
            if r.last_w is not None:
                add(*r.last_w)
        for w in writes:
            if w.last_w is not None:
                add(*w.last_w)
            for k, v in w.readers.items():
                add(k, v)
        waits = []
        for k, v in deps.items():
            if eng == "pe" and k == ("c", "pe"):
                continue
            if self.waited[eng].get(k, 0) < v:
                self.waited[eng][k] = v
                waits.append((k, v))
        return waits

    def _commit(self, key, val, reads, writes):
        for r in reads:
            if r.readers.get(key, 0) < val:
                r.readers[key] = val
        for w in writes:
            w.last_w = (key, val)
            w.readers = {}

    def op(self, eng, fn, reads=(), writes=()):
        excl = [r for r in reads if r.name.startswith(("mm", "tr"))]
        if excl and eng != "pe":
            reads = [r for r in reads if r not in excl]
            writes = list(writes) + excl
        waits = self._deps(eng, reads, writes)
        self.ccnt[eng] += 1
        key = ("c", eng)
        self.prog[eng].append((waits, fn, key, 1))
        self._commit(key, self.ccnt[eng], reads, writes)

    def dma(self, q, fn, reads=(), writes=(), semq=None, inc=16):
        waits = self._deps(q, reads, writes)
        sq = semq or q
        slot = self.dnext[sq]
        self.dnext[sq] = (slot + 1) % self.NDMA
        key = ("d", sq, slot)
        prev = self.dcnt[sq][slot]
        if prev > 0 and self.waited[q].get(key, 0) < prev:
            self.waited[q][key] = prev
            waits.append((key, prev))
        self.dcnt[sq][slot] = prev + inc
        self.prog[q].append((waits, fn, key, inc))
        self._commit(key, prev + inc, reads, writes)

    def rotbuf(self, name, tensors):
        if name not in self.rot:
            self.rot[name] = [0, [(t, Buf(f"{name}{i}")) for i, t in enumerate(tensors)]]
        st = self.rot[name]
        r = st[1][st[0] % len(st[1])]
        st[0] += 1
        return r

    def emit(self):
        nc = self.nc
        finals = [(("c", e), self.ccnt[e]) for e in self.ccnt if self.ccnt[e] > 0]
        for q in self.dcnt:
            for s in range(self.NDMA):
                if self.dcnt[q][s] > 0:
                    finals.append((("d", q, s), self.dcnt[q][s]))
        prog = self.prog
        self.prog = {e: [] for e in self.ENGS}
        self.rot = {}
        with nc.Block() as block:
            for e, reg in [("pe", block.tensor), ("act", block.scalar), ("dve", block.vector),
                           ("pool", block.gpsimd), ("sp", block.sync)]:
                def body(engine, e=e):
                    for waits, fn, key, inc in prog[e]:
                        for k, v in waits:
                            engine.wait_ge(self.sem(k), v)
                        ins = fn(engine)
                        if ins is not None:
                            ins.then_inc(self.sem(key), inc)
                    for k, v in finals:
                        engine.wait_ge(self.sem(k), v)
                        if self.waited[e].get(k, 0) < v:
                            self.waited[e][k] = v
                reg(body)


def build_nc(mode="full", groups=None, npool=2560):
    nc = bass.Bass("TRN2", target_bir_lowering=False)
    din = lambda name, shape, dt=F32: nc.dram_tensor(name, list(shape), dt, kind="ExternalInput").ap()
    dout = lambda name, shape, dt=F32: nc.dram_tensor(name, list(shape), dt, kind="ExternalOutput").ap()
    xp = din("xp", [NPT * 128, D])
    xh = din("xh", [32, D])
    xs = din("xs", [128, D])
    sconv = din("sconv", [32, D])
    swin = din("swin", [16 * 512, 512])
    nwc = din("nwc", [128, 48])
    fnw = din("fnw", [1, D])
    convw = din("convw", [128, 24])
    a_in = din("a_in", [D, 3 * D])
    a_out = din("a_out", [D, D])
    ffn_in = din("ffn_in", [2, D, 2 * DFF])
    ffn_out = din("ffn_out", [2, DFF, D])
    kv_w = din("kv_w", [D, 1536])
    b_in = din("b_in", [D, 1072])
    b_out = din("b_out", [D, D])
    cmp_w1 = din("cmp_w1", [2, 32, 64, 64])
    cmp_w2 = din("cmp_w2", [2, 64, 64])
    cmp_pe = din("cmp_pe", [2, 32, 64])
    meta = din("meta", [128, 32])
    pcmp = din("pcmp", [npool * 128, 512])
    pslc = din("pslc", [npool * 128, 512])
    ptab = din("ptab", [1, 256], I32)

    y_o = dout("y_o", [NT * 128, D])
    conv_o = dout("conv_o", [34, D])
    cmp_o = dout("cmp_o", [NT * 128, 512])
    slc_o = dout("slc_o", [NT * 128, 512])
    winp_o = dout("winp_o", [128, 512])
    wins_o = dout("wins_o", [16 * 512, 512])

    ftl = [nc.dram_tensor(f"ft_s{c}", [2, 128, NPT * 128], BF16).ap() for c in range(4)]
    vtl = [nc.dram_tensor(f"vt_s{c}", [NPT * 128, 256], BF16).ap() for c in range(2)]
    if mode == "samp_test":
        skv_in = din("skv_in", [128, 2, 512], BF16)
    if mode in ("attn_test", "samp_test"):
        qt_s = din("qt_s", [8, 128, NT * 128], BF16)
        gft = din("gft", [4, 8, 128, NPT * 128], BF16)
        gvt = din("gvt", [4, 2, NPT * 128, 256], BF16)
        gftl = [gft[:, 2 * c:2 * c + 2] for c in range(4)]
        gvtl = [gvt[:, c] for c in range(2)]
        gates_in = din("gates_in", [128, NT * 48])
    else:
        qt_s = nc.dram_tensor("qt_s", [8, 128, NT * 128], BF16).ap()
        gftl = [nc.dram_tensor(f"gft{c}", [4, 2, 128, NPT * 128], BF16).ap() for c in range(4)]
        gvtl = [nc.dram_tensor(f"gvt{c}", [4, NPT * 128, 256], BF16).ap() for c in range(2)]

    es = contextlib.ExitStack()
    with es:
        sb = lambda name, shape, dt=F32: es.enter_context(nc.sbuf_tensor(name, list(shape), dt))
        ps = lambda name, shape, dt=F32: es.enter_context(nc.psum_tensor(name, list(shape), dt))
        h = sb("h", [128, NT, D])
        hb = [Buf(f"h{i}") for i in range(NT)]
        gates = sb("gates", [128, NT, 48])
        gates_b = Buf("gates")
        skv = sb("skv", [128, 2, 512], BF16)
        skv_b = Buf("skv")
        ident = sb("ident", [128, 128], BF16)
        identf = sb("identf", [128, 128], F32)
        ident_b = Buf("ident")
        nwc_t = sb("nwc_t", [128, 48])
        convw_t = sb("convw_t", [128, 24])
        fnw_t = sb("fnw_t", [128, D])
        const_b = Buf("consts")

        mmb = [ps(f"mm{i}", [128, 512]) for i in range(6)]
        trb = [ps(f"tr{i}", [128, 1024], BF16) for i in range(2)]

        ph = Phase(nc, es, "k")

        def dense_phase(tag, stage, groups):
          es1 = contextlib.ExitStack()
          with es1:
            sb1 = lambda name, shape, dt=F32: es1.enter_context(nc.sbuf_tensor(f"{tag}_{name}", list(shape), dt))
            xhalo = sb1("xhalo", [32, D])
            xhalo_b = Buf("xhalo")
            sconv_t = sb1("sconv_t", [34, D])
            sconvT = sb1("sconvT", [128, 8, 32])
            sconvT_b = Buf("sconvT")
            vh = sb1("vh", [128, 8, 32])
            vh_b = Buf("vh")
            cstage = sb1("cstage", [128, 8, 34])
            cstage_b = Buf("cstage")
            cout = sconv_t
            xT = sb1("xT", [128, 8, 544], BF16)
            xT_b = Buf("xT")
            junk = sb1("junk", [128, D], BF16)
            junk_b = Buf("junk")
            xr = [sb1(f"xr{i}", [128, D], BF16) for i in range(2)]
            ss = [sb1(f"ss{i}", [128, 2]) for i in range(4)]
            wp = [sb1(f"wp{i}", [128, 8, 384], BF16) for i in range(3)]
            wres = [sb1(f"wres{i}", [128, 6, D], BF16) for i in range(2)]
            csb = [sb1(f"csb{i}", [128, 512]) for i in range(2)]
            vbuf = [sb1(f"vbuf{i}", [128, 520]) for i in range(2)]
            t1b = [sb1(f"t1b{i}", [128, 512]) for i in range(2)]
            bcT = sb1("bcT", [128, 8, 512], BF16)
            bcT_b = Buf("bcT")
            hidT = bcT
            hidT_b = bcT_b
            xT2 = bcT
            xT2_b = bcT_b
            kvrow = [sb1(f"kvrow{i}", [128, 512]) for i in range(3)]
            vst = [sb1(f"vst{i}", [128, 256], BF16) for i in range(2)]
            fst = [sb1(f"fst{i}", [128, 512], BF16) for i in range(3)]

            def mmbank():
                return ph.rotbuf("mm", mmb)

            def trbank():
                return ph.rotbuf("tr", trb)

            sconv_b = Buf("sconv")
            wins_b = Buf("wins")
            if stage == "p1":
                ph.op("pool", lambda e: e.memset(identf[:], 0.0), writes=[ident_b])
                ph.op("pool", lambda e: e.affine_select(out=identf[:], in_=identf[:], pattern=[[-1, 128]],
                                                        compare_op=ALU.not_equal, fill=1.0, base=0,
                                                        channel_multiplier=1),
                      reads=[ident_b], writes=[ident_b])
                ph.op("pool", lambda e: e.tensor_copy(out=ident[:], in_=identf[:]), reads=[ident_b], writes=[ident_b])
                ph.dma("sp", lambda e: e.dma_start(out=nwc_t[:], in_=nwc[:, :]), writes=[const_b])
                ph.dma("sp", lambda e: e.dma_start(out=convw_t[:], in_=convw[:, :]), writes=[const_b])
                ph.dma("sp", lambda e: e.dma_start(out=fnw_t[:], in_=fnw.partition_broadcast(128)), writes=[const_b])
                ph.dma("sp", lambda e: e.dma_start(out=xhalo[:], in_=xh[:, :]), writes=[xhalo_b])
                ph.dma("sp", lambda e: e.dma_start(out=sconv_t[0:32, :], in_=sconv[:, :]), writes=[sconv_b])
                ph.dma("sp", lambda e: e.dma_start(
                    out=wins_o.rearrange("(s r) c -> s r c", r=512)[:, 0:504, :],
                    in_=swin.rearrange("(s r) c -> s r c", r=512)[:, 8:512, :]), writes=[wins_b])
                for ti in range(NT):
                    src = xp[ti * 128:(ti + 1) * 128, :] if ti < NPT else xs[:, :]
                    ph.dma("sp", lambda e, ti=ti, src=src: e.dma_start(out=h[:, ti, :], in_=src), writes=[hb[ti]])
                for half in range(2):
                    bank, bb = mmbank()
                    for jj in range(4):
                        j = half * 4 + jj
                        ph.op("pe", lambda e, j=j, jj=jj, bank=bank: e.transpose(
                            out=bank[:, jj * 32:(jj + 1) * 32], in_=sconv_t[0:32, j * 128:(j + 1) * 128],
                            identity=identf[0:32, 0:32]), reads=[sconv_b, ident_b], writes=[bb])
                    ph.op("act", lambda e, half=half, bank=bank: e.activation(
                        out=sconvT[:, half * 4:(half + 1) * 4, :],
                        in_=bank[:, 0:128].rearrange("p (j c) -> p j c", c=32), func=AF.Copy),
                        reads=[bb], writes=[sconvT_b])

            def rstd_of(src_ap, src_b, P):
                st, stb = ph.rotbuf("ss", ss)
                ph.op("act", lambda e: e.activation(out=junk[0:P, :], in_=src_ap, func=AF.Square,
                                                    accum_out=st[0:P, 0:1]),
                      reads=[src_b], writes=[junk_b, stb])
                ph.op("dve", lambda e: e.tensor_scalar(out=st[0:P, 1:2], in0=st[0:P, 0:1], scalar1=1.0 / D,
                                                       scalar2=EPS, op0=ALU.mult, op1=ALU.add),
                      reads=[stb], writes=[stb])
                ph.op("act", lambda e: e.activation(out=st[0:P, 1:2], in_=st[0:P, 1:2], func=AF.Sqrt),
                      reads=[stb], writes=[stb])
                ph.op("dve", lambda e: e.reciprocal(out=st[0:P, 1:2], in_=st[0:P, 1:2]),
                      reads=[stb], writes=[stb])
                return st, stb

            def norm_T(src_ap, src_b, P, dsts):
                st, stb = rstd_of(src_ap, src_b, P)
                x_r, xrb = ph.rotbuf("xr", xr)
                ph.op("dve", lambda e: e.tensor_scalar(out=x_r[0:P, :], in0=src_ap, scalar1=st[0:P, 1:2],
                                                       scalar2=None, op0=ALU.mult),
                      reads=[src_b, stb], writes=[xrb])
                bank, bb = trbank()
                for k in range(8):
                    ph.op("pe", lambda e, k=k: e.transpose(out=bank[:, k * 128:k * 128 + P],
                                                           in_=x_r[0:P, k * 128:(k + 1) * 128],
                                                           identity=ident[0:P, 0:P]),
                          reads=[xrb, ident_b], writes=[bb])
                for (dt_, db, c0, ni) in dsts:
                    ph.op("dve", lambda e, dt_=dt_, c0=c0, ni=ni: e.tensor_tensor(
                        out=dt_[:, :, c0:c0 + P],
                        in0=bank[:, :].rearrange("p (k c) -> p k c", c=128)[:, :, 0:P],
                        in1=nwc_t[:, ni * 8:(ni + 1) * 8].unsqueeze(2).to_broadcast([128, 8, P]),
                        op=ALU.mult), reads=[bb, const_b], writes=[db])

            def splits_of(n):
                return [(0, min(512, n))] + ([(512, n)] if n > 512 else [])

            def ffn(l, tiles, ni):
                n = 128 * len(tiles)
                for t, ti in enumerate(tiles):
                    norm_T(h[:, ti, :], hb[ti], 128, [(xT, xT_b, t * 128, ni)])
                for (j0, j1) in [(0, 6), (6, 12), (12, 17), (17, 22)]:
                    nj = j1 - j0
                    wr, wrb = ph.rotbuf("wres", wres)
                    ph.dma("pool", lambda e, wr=wr, j0=j0, j1=j1, nj=nj: e.dma_start(
                        out=wr[:, 0:nj, :],
                        in_=ffn_out[l, j0 * 128:j1 * 128, :].rearrange("(j p) n -> p j n", p=128)),
                        writes=[wrb])
                    for jj in range(nj):
                        j = j0 + jj
                        w, wb = ph.rotbuf("wp", wp)
                        for t2 in range(2):
                            ph.dma("pool", lambda e, w=w, j=j, t2=t2: e.dma_start(
                                out=w[:, :, t2 * 128:(t2 + 1) * 128],
                                in_=ffn_in[l, :, t2 * DFF + j * 128:t2 * DFF + (j + 1) * 128].rearrange("(k p) f -> p k f", p=128)),
                                writes=[wb])
                        bg, bgb = mmbank()
                        bu, bub = mmbank()
                        for k in range(8):
                            ph.op("pe", lambda e, k=k, w=w, bg=bg: e.matmul(
                                bg[:, 0:n], lhsT=w[:, k, 0:128], rhs=xT[:, k, 0:n], start=(k == 0), stop=(k == 7)),
                                reads=[wb, xT_b], writes=[bgb])
                        for k in range(8):
                            ph.op("pe", lambda e, k=k, w=w, bu=bu: e.matmul(
                                bu[:, 0:n], lhsT=w[:, k, 128:256], rhs=xT[:, k, 0:n], start=(k == 0), stop=(k == 7)),
                                reads=[wb, xT_b], writes=[bub])
                        sg, sgb = ph.rotbuf("csb", csb)
                        ph.op("act", lambda e, sg=sg, bg=bg: e.activation(out=sg[:, 0:n], in_=bg[:, 0:n], func=AF.Silu),
                              reads=[bgb], writes=[sgb])
                        ph.op("dve", lambda e, sg=sg, bu=bu, jj=jj: e.tensor_tensor(
                            out=hidT[:, jj, 0:n], in0=bu[:, 0:n], in1=sg[:, 0:n], op=ALU.mult),
                            reads=[bub, sgb], writes=[hidT_b])
                    for t, ti in enumerate(tiles):
                        for half in range(2):
                            bo, bob = mmbank()
                            for jj in range(nj):
                                ph.op("pe", lambda e, jj=jj, bo=bo, wr=wr, t=t, half=half, nj=nj: e.matmul(
                                    bo[:, :], lhsT=hidT[:, jj, t * 128:(t + 1) * 128],
                                    rhs=wr[:, jj, half * 512:(half + 1) * 512], start=(jj == 0), stop=(jj == nj - 1)),
                                    reads=[hidT_b, wrb], writes=[bob])
                            ph.op("dve", lambda e, bo=bo, ti=ti, half=half: e.tensor_tensor(
                                out=h[:, ti, half * 512:(half + 1) * 512], in0=bo[:, :],
                                in1=h[:, ti, half * 512:(half + 1) * 512], op=ALU.add),
                                reads=[bob, hb[ti]], writes=[hb[ti]])
            def mixer(gi, tiles, with_halo):
                sample = tiles[0] == NPT
                n_main = 128 * len(tiles)
                for t, ti in enumerate(tiles):
                    norm_T(h[:, ti, :], hb[ti], 128, [(xT, xT_b, t * 128, 0)])
                if with_halo:
                    norm_T(xhalo[0:32, :], xhalo_b, 32, [(xT, xT_b, 512, 0)])
                spl = ([(512, 544)] if with_halo else []) + [(0, n_main)]
                for j in range(8):
                    w, wb = ph.rotbuf("wp", wp)
                    for t3 in range(3):
                        ph.dma("pool", lambda e, w=w, j=j, t3=t3: e.dma_start(
                            out=w[:, :, t3 * 128:(t3 + 1) * 128],
                            in_=a_in[:, t3 * D + j * 128:t3 * D + (j + 1) * 128].rearrange("(k p) f -> p k f", p=128)),
                            writes=[wb])
                    for (c0, c1) in spl:
                        n = c1 - c0
                        halo = c0 == 512
                        banks = []
                        for t3 in range(3):
                            if halo and t3 == 0:
                                banks.append((None, None))
                                continue
                            bk, bkb = mmbank()
                            for k in range(8):
                                ph.op("pe", lambda e, k=k, w=w, bk=bk, t3=t3, c0=c0, c1=c1, n=n: e.matmul(
                                    bk[:, 0:n], lhsT=w[:, k, t3 * 128:(t3 + 1) * 128], rhs=xT[:, k, c0:c1],
                                    start=(k == 0), stop=(k == 7)), reads=[wb, xT_b], writes=[bkb])
                            banks.append((bk, bkb))
                        (pb, pbb), (pc, pcb), (pu, pub) = banks
                        cs, csb_ = ph.rotbuf("csb", csb)
                        ph.op("act", lambda e, cs=cs, pc=pc, n=n: e.activation(out=cs[:, 0:n], in_=pc[:, 0:n], func=AF.Copy),
                              reads=[pcb], writes=[csb_])
                        if halo:
                            ph.op("dve", lambda e, cs=cs, pu=pu, j=j: e.tensor_tensor(
                                out=vh[:, j, :], in0=pu[:, 0:32], in1=cs[:, 0:32], op=ALU.mult),
                                reads=[pub, csb_], writes=[vh_b])
                            continue
                        vb, vbb = ph.rotbuf("vbuf", vbuf)
                        if sample:
                            nb, L = 16, 8
                        else:
                            nb, L = len(tiles), 128
                        W = L + 2
                        vv = vb[:, 0:nb * W].rearrange("p (b c) -> p b c", c=W)
                        ph.op("dve", lambda e, vv=vv, pu=pu, cs=cs, L=L, n=n: e.tensor_tensor(
                            out=vv[:, :, 2:2 + L], in0=pu[:, 0:n].rearrange("p (b c) -> p b c", c=L),
                            in1=cs[:, 0:n].rearrange("p (b c) -> p b c", c=L), op=ALU.mult),
                            reads=[pub, csb_], writes=[vbb])
                        if sample:
                            ph.op("pool", lambda e, vv=vv, j=j: e.tensor_copy(
                                out=vv[:, :, 0:2], in_=sconvT[:, j, :].rearrange("p (s r) -> p s r", r=2)),
                                reads=[sconvT_b], writes=[vbb])
                            ph.op("pool", lambda e, vv=vv, j=j: e.tensor_copy(
                                out=cstage[:, j, 0:32].rearrange("p (s r) -> p s r", r=2), in_=vv[:, :, 8:10]),
                                reads=[vbb], writes=[cstage_b])
                        else:
                            g0 = tiles[0]
                            ph.op("pool", lambda e, vv=vv, j=j, g0=g0, nb=nb: e.tensor_copy(
                                out=vv[:, :, 0:2], in_=vh[:, j, 2 * g0:2 * (g0 + nb)].rearrange("p (s r) -> p s r", r=2)),
                                reads=[vh_b], writes=[vbb])
                            if tiles[-1] == NPT - 1:
                                ph.op("pool", lambda e, vv=vv, j=j, nb=nb: e.tensor_copy(
                                    out=cstage[:, j, 32:34], in_=vv[:, nb - 1, 128:130]),
                                    reads=[vbb], writes=[cstage_b])
                        t1, t1bb = ph.rotbuf("t1b", t1b)
                        t1v = t1[:, 0:n].rearrange("p (b c) -> p b c", c=L)
                        cw0, cw1, cw2 = [convw_t[:, tap * 8 + j:tap * 8 + j + 1] for tap in range(3)]
                        ph.op("dve", lambda e, t1v=t1v, vv=vv, L=L, cw0=cw0: e.tensor_scalar(
                            out=t1v, in0=vv[:, :, 0:L], scalar1=cw0, scalar2=None, op0=ALU.mult),
                            reads=[vbb, const_b], writes=[t1bb])
                        ph.op("dve", lambda e, t1v=t1v, vv=vv, L=L, cw1=cw1: e.scalar_tensor_tensor(
                            out=t1v, in0=vv[:, :, 1:1 + L], scalar=cw1, in1=t1v, op0=ALU.mult, op1=ALU.add),
                            reads=[vbb, const_b, t1bb], writes=[t1bb])
                        ph.op("dve", lambda e, t1v=t1v, vv=vv, L=L, cw2=cw2: e.scalar_tensor_tensor(
                            out=t1v, in0=vv[:, :, 2:2 + L], scalar=cw2, in1=t1v, op0=ALU.mult, op1=ALU.add),
                            reads=[vbb, const_b, t1bb], writes=[t1bb])
                        ph.op("dve", lambda e, t1=t1, pb=pb, j=j, n=n: e.tensor_tensor(
                            out=bcT[:, j, 0:n], in0=pb[:, 0:n], in1=t1[:, 0:n], op=ALU.mult),
                            reads=[pbb, t1bb], writes=[bcT_b])
                for half in range(2):
                    wr, wrb = ph.rotbuf("wres", wres)
                    wr_v = wr[:, :, :].rearrange("p a b -> p (a b)")[:, 0:4096].rearrange("p (j n) -> p j n", j=8)
                    ph.dma("pool", lambda e, wr_v=wr_v, half=half: e.dma_start(
                        out=wr_v, in_=a_out[:, half * 512:(half + 1) * 512].rearrange("(j p) n -> p j n", p=128)),
                        writes=[wrb])
                    for t, ti in enumerate(tiles):
                        bo, bob = mmbank()
                        for j in range(8):
                            ph.op("pe", lambda e, j=j, bo=bo, wr_v=wr_v, t=t: e.matmul(
                                bo[:, :], lhsT=bcT[:, j, t * 128:(t + 1) * 128],
                                rhs=wr_v[:, j, :], start=(j == 0), stop=(j == 7)),
                                reads=[bcT_b, wrb], writes=[bob])
                        ph.op("dve", lambda e, bo=bo, ti=ti, half=half: e.tensor_tensor(
                            out=h[:, ti, half * 512:(half + 1) * 512], in0=bo[:, :],
                            in1=h[:, ti, half * 512:(half + 1) * 512], op=ALU.add),
                            reads=[bob, hb[ti]], writes=[hb[ti]])

            def kvside(tiles):
                sample = tiles[0] == NPT
                n = 128 * len(tiles)
                t0 = tiles[0]
                for t, ti in enumerate(tiles):
                    norm_T(h[:, ti, :], hb[ti], 128, [(xT, xT_b, t * 128, 5), (xT2, xT2_b, t * 128, 2)])
                fm_cols = {0: [0, 128, 256, 384], 1: [0, 128], 2: [0, 128]}
                fm_idx = {0: [0, 1, 2, 3], 1: [4, 5], 2: [6, 7]}
                for c in range(3):
                    wr, wrb = ph.rotbuf("wres", wres)
                    wv = wr[:, :, :].rearrange("p a b -> p (a b)")[:, 0:4096].rearrange("p (k n) -> p k n", k=8)
                    ph.dma("pool", lambda e, wv=wv, c=c: e.dma_start(
                        out=wv, in_=kv_w[:, c * 512:(c + 1) * 512].rearrange("(k p) n -> p k n", p=128)),
                        writes=[wrb])
                    for t, ti in enumerate(tiles):
                        bk, bkb = mmbank()
                        for k in range(8):
                            ph.op("pe", lambda e, k=k, bk=bk, t=t, wv=wv: e.matmul(
                                bk[:, :], lhsT=xT[:, k, t * 128:(t + 1) * 128], rhs=wv[:, k, :],
                                start=(k == 0), stop=(k == 7)), reads=[xT_b, wrb], writes=[bkb])
                        kr, krb = ph.rotbuf("kvrow", kvrow)
                        ph.op("act", lambda e, kr=kr, bk=bk: e.activation(out=kr[:, :], in_=bk[:, :], func=AF.Copy),
                              reads=[bkb], writes=[krb])
                        if c == 0:
                            ph.dma("sp", lambda e, kr=kr, ti=ti: e.dma_start(
                                out=cmp_o[ti * 128:(ti + 1) * 128, :], in_=kr[:, :]), reads=[krb])
                        elif c == 1:
                            ph.dma("sp", lambda e, kr=kr, ti=ti: e.dma_start(
                                out=slc_o[ti * 128:(ti + 1) * 128, :], in_=kr[:, :]), reads=[krb])
                            if sample:
                                ph.op("pool", lambda e, kr=kr: e.tensor_copy(out=skv[:, 0, :], in_=kr[:, :]),
                                      reads=[krb], writes=[skv_b])
                        else:
                            if ti == NPT - 1:
                                ph.dma("sp", lambda e, kr=kr: e.dma_start(out=winp_o[:, :], in_=kr[:, :]), reads=[krb])
                            if sample:
                                ph.op("pool", lambda e, kr=kr: e.tensor_copy(out=skv[:, 1, :], in_=kr[:, :]),
                                      reads=[krb], writes=[skv_b])
                                for s in range(16):
                                    ph.dma("sp", lambda e, kr=kr, s=s: e.dma_start(
                                        out=wins_o[s * 512 + 504:s * 512 + 512, :],
                                        in_=kr[s * 8:(s + 1) * 8, :]), reads=[krb, wins_b], writes=[wins_b])
                        if c >= 1 and not sample:
                            vs, vsb = ph.rotbuf("vst", vst)
                            ph.op("dve", lambda e, vs=vs, bk=bk: e.tensor_copy(out=vs[:, :], in_=bk[:, 256:512]),
                                  reads=[bkb], writes=[vsb])
                            if 'noscr' not in DBG:
                              ph.dma("sp", lambda e, vs=vs, ti=ti, c=c: e.dma_start(
                                out=vtl[c - 1][ti * 128:(ti + 1) * 128, :], in_=vs[:, :]), reads=[vsb])
                    if not sample:
                        for col0, fi in zip(fm_cols[c], fm_idx[c]):
                            bk, bkb = mmbank()
                            for k in range(8):
                                ph.op("pe", lambda e, k=k, bk=bk, col0=col0, wv=wv: e.matmul(
                                    bk[:, 0:n], lhsT=wv[:, k, col0:col0 + 128], rhs=xT[:, k, 0:n],
                                    start=(k == 0), stop=(k == 7)), reads=[xT_b, wrb], writes=[bkb])
                            fs, fsb = ph.rotbuf("fst", fst)
                            ph.op("act", lambda e, fs=fs, bk=bk: e.activation(out=fs[:, 0:n], in_=bk[:, 0:n], func=AF.Copy),
                                  reads=[bkb], writes=[fsb])
                            if 'noscr' not in DBG:
                              ph.dma("sp", lambda e, fs=fs, fi=fi: e.dma_start(
                                out=ftl[fi // 2][fi % 2, :, t0 * 128:t0 * 128 + n], in_=fs[:, 0:n]), reads=[fsb])
                for c in range(3):
                    ncol = 512 if c < 2 else 48
                    wr, wrb = ph.rotbuf("wres", wres)
                    wv = wr[:, :, :].rearrange("p a b -> p (a b)")[:, 0:8 * ncol].rearrange("p (k n) -> p k n", k=8)
                    ph.dma("pool", lambda e, wv=wv, c=c, ncol=ncol: e.dma_start(
                        out=wv, in_=b_in[:, c * 512:c * 512 + ncol].rearrange("(k p) n -> p k n", p=128)),
                        writes=[wrb])
                    if c < 2:
                        for f4 in range(4):
                            fi = c * 4 + f4
                            bk, bkb = mmbank()
                            for k in range(8):
                                ph.op("pe", lambda e, k=k, bk=bk, f4=f4, wv=wv: e.matmul(
                                    bk[:, 0:n], lhsT=wv[:, k, f4 * 128:(f4 + 1) * 128], rhs=xT2[:, k, 0:n],
                                    start=(k == 0), stop=(k == 7)), reads=[xT2_b, wrb], writes=[bkb])
                            fs, fsb = ph.rotbuf("fst", fst)
                            ph.op("act", lambda e, fs=fs, bk=bk: e.activation(out=fs[:, 0:n], in_=bk[:, 0:n],
                                                                             func=AF.Copy, scale=0.125),
                                  reads=[bkb], writes=[fsb])
                            if 'noscr' not in DBG:
                              ph.dma("sp", lambda e, fs=fs, fi=fi: e.dma_start(
                                out=qt_s[fi, :, t0 * 128:t0 * 128 + n], in_=fs[:, 0:n]), reads=[fsb])
                    else:
                        for t, ti in enumerate(tiles):
                            bk, bkb = mmbank()
                            for k in range(8):
                                ph.op("pe", lambda e, k=k, bk=bk, t=t, wv=wv: e.matmul(
                                    bk[:, 0:48], lhsT=xT2[:, k, t * 128:(t + 1) * 128], rhs=wv[:, k, :],
                                    start=(k == 0), stop=(k == 7)), reads=[xT2_b, wrb], writes=[bkb])
                            ph.op("act", lambda e, bk=bk, ti=ti: e.activation(out=gates[:, ti, :], in_=bk[:, 0:48],
                                                                              func=AF.Sigmoid),
                                  reads=[bkb], writes=[gates_b])

            def final_out(tiles):
                for ti in tiles:
                    st, stb = rstd_of(h[:, ti, :], hb[ti], 128)
                    ph.op("dve", lambda e, ti=ti, st=st: e.scalar_tensor_tensor(
                        out=h[:, ti, :], in0=h[:, ti, :], scalar=st[:, 1:2], in1=fnw_t[:, :], op0=ALU.mult, op1=ALU.mult),
                        reads=[hb[ti], stb, const_b], writes=[hb[ti]])
                    ph.dma("sp", lambda e, ti=ti: e.dma_start(out=y_o[ti * 128:(ti + 1) * 128, :], in_=h[:, ti, :]),
                           reads=[hb[ti]])

            if stage == "p1":
                for gi, tiles in enumerate(groups):
                    mixer(gi, tiles, with_halo=(tiles[0] == 0))
                    ffn(0, tiles, 1)
                    kvside(tiles)
                for half in range(2):
                    bank, bb = mmbank()
                    for jj in range(4):
                        j = half * 4 + jj
                        ph.op("pe", lambda e, j=j, jj=jj, bank=bank: e.transpose(
                            out=bank[0:34, jj * 128:(jj + 1) * 128], in_=cstage[:, j, :], identity=identf[:, :]),
                            reads=[cstage_b, ident_b], writes=[bb])
                    ph.op("act", lambda e, half=half, bank=bank: e.activation(
                        out=cout[0:34, half * 512:(half + 1) * 512], in_=bank[0:34, :], func=AF.Copy),
                        reads=[bb], writes=[sconv_b])
                ph.dma("sp", lambda e: e.dma_start(out=conv_o[:, :], in_=cout[0:34, :]), reads=[sconv_b])
            else:
                for gi, tiles in enumerate(groups):
                    ffn(1, tiles, 3)
                    final_out(tiles)
            ph.emit()

        if groups is None:
            groups = [[0, 1, 2, 3], [4, 5, 6, 7], [8, 9, 10, 11], [12, 13, 14, 15], [16]]
        def attn_phase():
          es3 = contextlib.ExitStack()
          with es3:
            sb3 = lambda name, shape, dt=F32: es3.enter_context(nc.sbuf_tensor(f"p3_{name}", list(shape), dt))
            NEGM = -30000.0
            BIGV = 1.0e9
            KE = sb3("KE", [128, 8192], BF16); KE_b = Buf("KE"); KEe_b = Buf("KEe")
            KW = sb3("KW", [128, 8192], BF16); KW_b = Buf("KW")
            VS = sb3("VS", [128, 64, 65], BF16); VS_b = Buf("VS")
            VW = sb3("VW", [128, 64, 65], BF16); VW_b = Buf("VW")
            KCT = sb3("KCT", [64, 512], BF16); KCT_b = Buf("KCT")
            VC = sb3("VC", [128, 4, 65], BF16); VC_b = Buf("VC")
            hidc = sb3("hidc", [64, 512], BF16); hidc_b = Buf("hidc")
            w1a = sb3("w1a", [128, 2, 32, 64], BF16)
            w1b = sb3("w1b", [128, 2, 16, 64], BF16)
            pe_t = sb3("pe_t", [128, 2, 16], BF16)
            w2_t = sb3("w2_t", [64, 2, 64], BF16)
            peb = sb3("peb", [64, 2])
            cw_b = Buf("cmpw")
            wbo = sb3("wbo", [128, 2, D], BF16); wbo_b = Buf("wbo")
            meta_t = sb3("meta_t", [128, 32])
            itmp = sb3("itmp", [128, 128], I32)
            jrow = sb3("jrow", [128, 128])
            T16 = sb3("T16", [128, 128])
            dqp = sb3("dqp", [128, 128])
            tri01 = sb3("tri01", [128, 128])
            atri01 = sb3("atri01", [128, 128])
            j0row = sb3("j0row", [128, 128])
            aggm = sb3("aggm", [128, 4, 128], BF16)
            dmask = sb3("dmask", [128, 4, 128], BF16)
            wmask = sb3("wmask", [128, 8, 128], BF16)
            addm = sb3("addm", [128, NPT, 128])
            k_b = Buf("aconst")
            ta = sb3("ta", [128, 128]); tb = sb3("tb", [128, 128]); tmp_b = Buf("tatb")
            Xm = sb3("Xm", [128, 256], BF16); Xm_b = Buf("Xm")
            QA = [sb3(f"QA{i}", [128, 512], BF16) for i in range(2)]
            QB = [sb3(f"QB{i}", [128, 512], BF16) for i in range(2)]
            PT = [sb3(f"PT{i}", [128, 512], BF16) for i in range(6)]
            cmk = [sb3(f"cmk{i}", [128, 128], BF16) for i in range(2)]
            osb = [sb3(f"osb{i}", [65, 512], BF16) for i in range(2)]
            rl = [sb3(f"rl{i}", [128, 8]) for i in range(3)]
            accs = [(sb3(f"acc{i}", [128, 256]), Buf(f"acc{i}")) for i in range(2)]
            acct = sb3("acct", [128, 256]); acct_b = Buf("acct")
            accbs = [(sb3(f"accb{i}", [128, 256], BF16), Buf(f"accb{i}")) for i in range(2)]
            accT = sb3("accT", [128, 2, 128], BF16); accT_b = Buf("accT")
            score = sb3("score", [128, 128]); score_b = Buf("score")
            stmp = sb3("stmp", [128, 128])
            m8 = sb3("m8", [128, 16])

            def mmbank():
                return ph.rotbuf("mm", mmb[0:3])

            def obank():
                return ph.rotbuf("ob", mmb[3:5])
            rbank, rbank_b = mmb[5], Buf("mmR")

            def trbank():
                return ph.rotbuf("tr", trb)

            ph.dma("sp", lambda e: e.dma_start(out=meta_t[:], in_=meta[:, :]), writes=[k_b])
            def iota_f(dst, pattern, cm, base=0):
                ph.op("pool", lambda e: e.iota(itmp[:], pattern=pattern, base=base, channel_multiplier=cm), writes=[tmp_b])
                ph.op("pool", lambda e: e.tensor_copy(out=dst[:], in_=itmp[:]), reads=[tmp_b], writes=[k_b])
            iota_f(jrow, [[1, 128]], 0)
            iota_f(T16, [[-1, 128]], 16)
            iota_f(dqp, [[1, 128]], -1)
            ph.op("dve", lambda e: e.tensor_scalar(out=tri01[:], in0=dqp[:], scalar1=0.0, scalar2=None, op0=ALU.is_ge),
                  reads=[k_b], writes=[k_b])
            ph.op("dve", lambda e: e.tensor_scalar(out=atri01[:], in0=dqp[:], scalar1=0.0, scalar2=None, op0=ALU.is_le),
                  reads=[k_b], writes=[k_b])
            ph.op("dve", lambda e: e.tensor_scalar(out=j0row[:], in0=jrow[:], scalar1=0.0, scalar2=None, op0=ALU.is_equal),
                  reads=[k_b], writes=[k_b])
            for cidx in range(4):
                iota_f(ta, [[-4, 128]], 1, base=128 * cidx)
                ph.op("dve", lambda e: e.tensor_scalar(out=tb[:], in0=ta[:], scalar1=-1.0, scalar2=None, op0=ALU.is_ge),
                      reads=[k_b], writes=[tmp_b])
                ph.op("dve", lambda e: e.scalar_tensor_tensor(out=tb[:], in0=ta[:], scalar=3.0, in1=tb[:], op0=ALU.is_le,
                                                              op1=ALU.mult), reads=[k_b, tmp_b], writes=[tmp_b])
                ph.op("dve", lambda e: e.tensor_scalar(out=stmp[:], in0=ta[:], scalar1=0.0, scalar2=None, op0=ALU.is_ge),
                      reads=[k_b], writes=[score_b])
                ph.op("dve", lambda e: e.scalar_tensor_tensor(out=stmp[:], in0=ta[:], scalar=2.0, in1=stmp[:], op0=ALU.is_le,
                                                              op1=ALU.mult), reads=[k_b, score_b], writes=[score_b])
                ph.op("dve", lambda e, cidx=cidx: e.tensor_tensor(out=aggm[:, cidx, :], in0=tb[:], in1=stmp[:], op=ALU.add),
                      reads=[tmp_b, score_b], writes=[k_b])
            for j in range(4):
                ph.op("dve", lambda e, j=j: e.tensor_scalar(out=dmask[:, j, :], in0=tri01[:], scalar1=meta_t[:, 2 + j:3 + j],
                                                            scalar2=None, op0=ALU.max), reads=[k_b], writes=[k_b])
            for d in range(8):
                ph.op("dve", lambda e, d=d: e.tensor_scalar(out=ta[:], in0=tri01[:], scalar1=meta_t[:, 14 + d:15 + d],
                                                            scalar2=meta_t[:, 6 + d:7 + d], op0=ALU.mult, op1=ALU.add),
                      reads=[k_b], writes=[tmp_b])
                ph.op("dve", lambda e, d=d: e.scalar_tensor_tensor(out=wmask[:, d, :], in0=atri01[:],
                                                                   scalar=meta_t[:, 22 + d:23 + d], in1=ta[:],
                                                                   op0=ALU.mult, op1=ALU.add),
                      reads=[k_b, tmp_b], writes=[k_b])
            for iq in range(NPT):
                ph.op("dve", lambda e, iq=iq: e.tensor_scalar(out=ta[:], in0=jrow[:], scalar1=meta_t[:, 0:1],
                                                              scalar2=float(8 * iq), op0=ALU.subtract, op1=ALU.subtract),
                      reads=[k_b], writes=[tmp_b])
                ph.op("dve", lambda e: e.tensor_scalar(out=tb[:], in0=ta[:], scalar1=0.0, scalar2=None, op0=ALU.is_le),
                      reads=[tmp_b], writes=[tmp_b])
                ph.op("dve", lambda e: e.scalar_tensor_tensor(out=ta[:], in0=ta[:], scalar=-1.0, in1=tb[:], op0=ALU.is_ge,
                                                              op1=ALU.mult), reads=[tmp_b], writes=[tmp_b])
                ph.op("dve", lambda e: e.tensor_tensor(out=ta[:], in0=ta[:], in1=j0row[:], op=ALU.max),
                      reads=[tmp_b, k_b], writes=[tmp_b])
                ph.op("dve", lambda e: e.tensor_tensor(out=ta[:], in0=ta[:], in1=tb[:], op=ALU.add),
                      reads=[tmp_b], writes=[tmp_b])
                ph.op("dve", lambda e, iq=iq: e.tensor_scalar(out=addm[:, iq, :], in0=ta[:], scalar1=-1.0, scalar2=BIGV,
                                                              op0=ALU.add, op1=ALU.mult), reads=[tmp_b], writes=[k_b])
            ph.op("pool", lambda e: e.memset(KE[64:128, :], 1.0), writes=[KEe_b])
            ph.op("pool", lambda e: e.affine_select(out=KE[64:128, :], in_=KE[64:128, :],
                                                    pattern=[[0, 2], [1, 64], [0, 64]], compare_op=ALU.is_equal,
                                                    fill=0.0, base=0, channel_multiplier=-1),
                  reads=[KEe_b], writes=[KEe_b])
            ph.op("pool", lambda e: e.memset(VS[:, :, 64:65], 1.0), writes=[VS_b])
            ph.op("pool", lambda e: e.memset(VW[:, :, 64:65], 1.0), writes=[VW_b])
            ph.op("pool", lambda e: e.memset(VC[:, :, 64:65], 1.0), writes=[VC_b])
            ph.op("pool", lambda e: e.memset(Xm[:], 0.0), writes=[Xm_b])
            ph.op("pool", lambda e: e.memset(hidc[:, 511:512], 0.0), writes=[hidc_b])
            for k in range(2):
                for c2 in range(2):
                    ph.dma("pool", lambda e, k=k, c2=c2: e.dma_start(
                        out=w1a[c2 * 64:(c2 + 1) * 64, k, :, :], in_=cmp_w1[k].rearrange("j d h -> d j h")), writes=[cw_b])
                    ph.dma("pool", lambda e, k=k, c2=c2: e.dma_start(
                        out=w1b[c2 * 64:(c2 + 1) * 64, k, :, :],
                        in_=cmp_w1[k].rearrange("(jj j2) d h -> j2 d jj h", j2=2)[c2]), writes=[cw_b])
                    ph.dma("pool", lambda e, k=k, c2=c2: e.dma_start(
                        out=pe_t[c2 * 64:(c2 + 1) * 64, k, :],
                        in_=cmp_pe[k].rearrange("(jj j2) d -> j2 d jj", j2=2)[c2], allow_slow_non_contiguous=True),
                        writes=[cw_b])
                ph.dma("pool", lambda e, k=k: e.dma_start(out=w2_t[:, k, :], in_=cmp_w2[k]), writes=[cw_b])
            for k in range(2):
                bk, bkb = mmbank()
                for jj in range(16):
                    ph.op("pe", lambda e, k=k, jj=jj, bk=bk: e.matmul(
                        bk[0:64, 0:1], lhsT=w1b[:, k, jj, :], rhs=pe_t[:, k, jj:jj + 1], start=(jj == 0), stop=(jj == 15)),
                        reads=[cw_b], writes=[bkb])
                ph.op("act", lambda e, k=k, bk=bk: e.activation(out=peb[:, k:k + 1], in_=bk[0:64, 0:1], func=AF.Copy),
                      reads=[bkb], writes=[cw_b])

            def load_ft(dst, dst_b, fc, p0, np_, dp0):
                for r in range(4):
                    ph.dma("sp", lambda e, r=r: e.dma_start(
                        out=dst[dp0:dp0 + np_, :].rearrange("p (i r c) -> p r i c", r=4, c=128)[:, r],
                        in_=gftl[fc // 2][r, fc % 2, p0:p0 + np_, :].rearrange("p (i c) -> p i c", c=128)), writes=[dst_b])

            def load_v(dst, dst_b, which, g):
                for r in range(4):
                    ph.dma("sp", lambda e, r=r: e.dma_start(
                        out=dst[:, :, 0:64].rearrange("p (i r) d -> p r i d", r=4)[:, r],
                        in_=gvtl[which][r].rearrange("(i p) d -> p i d", p=128)[:, :, g * 64:(g + 1) * 64]),
                        writes=[dst_b])

            def finish(ob, obb, branch, g, iq, first, last=False):
                acc, acc_b = accs[iq % 2]
                accb, accb_b = accbs[iq % 2]
                os_, osb_b = ph.rotbuf("osb", osb)
                ph.op("act", lambda e: e.activation(out=os_[0:65, :], in_=ob[0:65, :], func=AF.Copy),
                      reads=[obb], writes=[osb_b])
                tb_, tbb = trbank()
                for r4 in range(4):
                    ph.op("pe", lambda e, r4=r4: e.transpose(out=tb_[:, r4 * 66:r4 * 66 + 65],
                                                             in_=os_[0:65, r4 * 128:(r4 + 1) * 128],
                                                             identity=ident[0:65, 0:65]),
                          reads=[osb_b, ident_b], writes=[tbb])
                tv = tb_[:, 0:264].rearrange("p (r c) -> p r c", c=66)
                r_, rb = ph.rotbuf("rl", rl)
                ph.op("dve", lambda e: e.tensor_scalar(out=r_[:, 0:4], in0=tv[:, :, 64], scalar1=1e-30, scalar2=None,
                                                       op0=ALU.max), reads=[tbb], writes=[rb])
                ph.op("dve", lambda e: e.reciprocal(out=r_[:, 0:4], in_=r_[:, 0:4]), reads=[rb], writes=[rb])
                gv = gates[:, iq, g * 12:(g + 1) * 12].rearrange("p (r b) -> p r b", b=3)[:, :, branch]
                ph.op("dve", lambda e: e.tensor_tensor(out=r_[:, 4:8], in0=r_[:, 0:4], in1=gv, op=ALU.mult),
                      reads=[rb, gates_b], writes=[rb])
                glb = r_[:, 4:8].unsqueeze(2).to_broadcast([128, 4, 64])
                if first:
                    ph.op("dve", lambda e: e.tensor_tensor(out=acc[:, :].rearrange("p (r d) -> p r d", d=64),
                                                           in0=tv[:, :, 0:64], in1=glb, op=ALU.mult),
                          reads=[tbb, rb], writes=[acc_b])
                else:
                    ph.op("dve", lambda e: e.tensor_tensor(out=acct[:, :].rearrange("p (r d) -> p r d", d=64),
                                                           in0=tv[:, :, 0:64], in1=glb, op=ALU.mult),
                          reads=[tbb, rb], writes=[acct_b])
                    if last:
                        ph.op("dve", lambda e: e.tensor_tensor(out=accb[:, :], in0=acc[:, :], in1=acct[:, :], op=ALU.add),
                              reads=[acct_b, acc_b], writes=[accb_b])
                    else:
                        ph.op("dve", lambda e: e.tensor_tensor(out=acc[:, :], in0=acc[:, :], in1=acct[:, :], op=ALU.add),
                              reads=[acct_b, acc_b], writes=[acc_b])
                return r_, rb

            def outproj(g, iq):
                accb, accb_b = accbs[iq % 2]
                trk, trkb = trbank()
                for a in range(2):
                    ph.op("pe", lambda e, a=a, trk=trk: e.transpose(out=trk[:, a * 128:(a + 1) * 128],
                                                                   in_=accb[:, a * 128:(a + 1) * 128], identity=ident[:, :]),
                          reads=[accb_b, ident_b], writes=[trkb])
                ph.op("act", lambda e, trk=trk: e.activation(out=accT[:, :, :],
                                                             in_=trk[:, 0:256].rearrange("p (a q) -> p a q", q=128),
                                                             func=AF.Copy), reads=[trkb], writes=[accT_b])
                for half in range(2):
                    bo, bob = mmbank()
                    for a in range(2):
                        ph.op("pe", lambda e, a=a, bo=bo, half=half: e.matmul(
                            bo[:, :], lhsT=accT[:, a, :], rhs=wbo[:, a, half * 512:(half + 1) * 512],
                            start=(a == 0), stop=(a == 1)), reads=[accT_b, wbo_b], writes=[bob])
                    ph.op("dve", lambda e, bo=bo, half=half, iq=iq: e.tensor_tensor(
                        out=h[:, iq, half * 512:(half + 1) * 512], in0=bo[:, :],
                        in1=h[:, iq, half * 512:(half + 1) * 512], op=ALU.add),
                        reads=[bob, hb[iq]], writes=[hb[iq]])

            prev_c1 = prev_c2 = None
            for g in range(4):
                po = (g % 2) * 64
                for k in range(2):
                    load_ft(KW, KW_b, k * 2 + g // 2, 0, 128, 0)
                    bk, bkb = mmbank()
                    for j in range(32):
                        ph.op("pe", lambda e, k=k, j=j, bk=bk, po=po: e.matmul(
                            bk[0:64, 0:511], lhsT=w1a[po:po + 64, k, j, :], rhs=KW[po:po + 64, j:j + 8161:16],
                            start=(j == 0), stop=(j == 31)), reads=[cw_b, KW_b], writes=[bkb])
                    ph.op("act", lambda e, k=k, bk=bk: e.activation(out=hidc[:, 0:511], in_=bk[0:64, 0:511], func=AF.Silu,
                                                                    bias=peb[:, k:k + 1]),
                          reads=[bkb, cw_b], writes=[hidc_b])
                    if k == 0:
                        b2, b2b = mmbank()
                        ph.op("pe", lambda e, b2=b2: e.matmul(b2[0:64, 0:511], lhsT=w2_t[:, 0, :], rhs=hidc[:, 0:511],
                                                              start=True, stop=True), reads=[cw_b, hidc_b], writes=[b2b])
                        ph.op("act", lambda e, b2=b2: e.activation(out=KCT[:, 0:511], in_=b2[0:64, 0:511], func=AF.Copy),
                              reads=[b2b], writes=[KCT_b])
                    else:
                        b2, b2b = mmbank()
                        for cidx in range(4):
                            M = 128
                            ph.op("pe", lambda e, b2=b2, cidx=cidx, M=M: e.matmul(
                                b2[0:M, cidx * 64:(cidx + 1) * 64], lhsT=hidc[:, cidx * 128:cidx * 128 + M],
                                rhs=w2_t[:, 1, :], start=True, stop=True), reads=[cw_b, hidc_b], writes=[b2b])
                        ph.op("act", lambda e, b2=b2: e.activation(
                            out=VC[:, :, 0:64], in_=b2[:, 0:256].rearrange("p (c d) -> p c d", d=64), func=AF.Copy),
                            reads=[b2b], writes=[VC_b])
                load_ft(KE, KE_b, 4 + g // 2, po, 64, 0)
                load_ft(KW, KW_b, 6 + g // 2, po, 64, 0)
                load_v(VS, VS_b, 0, g)
                load_v(VW, VW_b, 1, g)
                ph.dma("pool", lambda e, g=g: e.dma_start(
                    out=wbo[:, :, :], in_=b_out[g * 256:(g + 1) * 256, :].rearrange("(c p) n -> p c n", p=128)),
                    writes=[wbo_b])
                for iq in range(NPT):
                    qa, qab = ph.rotbuf("QA", QA)
                    qb, qbb = ph.rotbuf("QB", QB)
                    useB = iq >= 8
                    for r4 in range(4):
                        fcq = 2 * g + r4 // 2
                        pq = (r4 % 2) * 64
                        ph.dma("sp", lambda e, r4=r4, fcq=fcq, pq=pq, qa=qa, iq=iq: e.dma_start(
                            out=qa[0:64, r4 * 128:(r4 + 1) * 128], in_=qt_s[fcq, pq:pq + 64, iq * 128:(iq + 1) * 128]),
                            writes=[qab])
                        if useB:
                            ph.dma("sp", lambda e, r4=r4, fcq=fcq, pq=pq, qb=qb, iq=iq: e.dma_start(
                                out=qb[0:64, r4 * 128:(r4 + 1) * 128], in_=qt_s[fcq, pq:pq + 64, iq * 128:(iq + 1) * 128]),
                                writes=[qbb])
                    nch = (32 * iq + 30) // 128 + 1
                    oc, ocb = obank()
                    for cidx in range(nch):
                        M = 127 if cidx == 3 else 128
                        sbk, sbb = mmbank()
                        ph.op("pe", lambda e, sbk=sbk, cidx=cidx, M=M, qa=qa: e.matmul(
                            sbk[0:M, :], lhsT=KCT[:, cidx * 128:cidx * 128 + M], rhs=qa[0:64, :], start=True, stop=True),
                            reads=[KCT_b, qab], writes=[sbb])
                        pt, ptb = ph.rotbuf("PT", PT)
                        ph.op("act", lambda e, sbk=sbk, pt=pt, M=M: e.activation(out=pt[0:M, :], in_=sbk[0:M, :], func=AF.Exp),
                              reads=[sbb], writes=[ptb])
                        if cidx >= nch - 2:
                            cm, cmb = ph.rotbuf("cmk", cmk)
                            cst = float(31 + 2048 * cidx - 512 * iq)
                            ph.op("dve", lambda e, cm=cm, cst=cst: e.tensor_scalar(
                                out=stmp[:], in0=T16[:], scalar1=meta_t[:, 1:2], scalar2=cst, op0=ALU.subtract, op1=ALU.add),
                                reads=[k_b], writes=[score_b])
                            ph.op("dve", lambda e, cm=cm: e.tensor_scalar(out=cm[:], in0=stmp[:], scalar1=0.0, scalar2=None,
                                                                         op0=ALU.is_le), reads=[score_b], writes=[cmb])
                            ph.op("dve", lambda e, cm=cm, pt=pt, M=M: e.tensor_tensor(
                                out=pt[0:M, :].rearrange("p (r q) -> p r q", q=128),
                                in0=pt[0:M, :].rearrange("p (r q) -> p r q", q=128),
                                in1=cm[0:M, :].unsqueeze(1).to_broadcast([M, 4, 128]), op=ALU.mult),
                                reads=[cmb, ptb], writes=[ptb])
                        ph.op("pe", lambda e, oc=oc, cidx=cidx, M=M, pt=pt, nch=nch: e.matmul(
                            oc[0:65, :], lhsT=VC[0:M, cidx, :], rhs=pt[0:M, :], start=(cidx == 0), stop=(cidx == nch - 1)),
                            reads=[VC_b, ptb], writes=[ocb])
                        for r4 in range(4):
                            ph.op("pe", lambda e, cidx=cidx, M=M, pt=pt, r4=r4, nch=nch: e.matmul(
                                rbank[:, r4 * 128:(r4 + 1) * 128], lhsT=pt[0:M, r4 * 128:(r4 + 1) * 128],
                                rhs=aggm[0:M, cidx, :], start=(cidx == 0 and r4 == 0), stop=(cidx == nch - 1 and r4 == 3),
                                skip_group_check=True), reads=[ptb, k_b], writes=[rbank_b])
                    rc, rcb = finish(oc, ocb, 0, g, iq, True)
                    if prev_c1 is not None:
                        prev_c1()
                    ow, owb = obank()
                    cs = [c for c in range(4 * iq - 4, 4 * iq + 4) if c >= 0]
                    pend = []

                    def win_pv(item, ow=ow, owb=owb, cs=cs):
                        c, pt, ptb = item
                        ph.op("pe", lambda e, ow=ow, c=c, pt=pt, cs=cs: e.matmul(
                            ow[0:65, :], lhsT=VW[:, c, :], rhs=pt[:, :], start=(c == cs[0]), stop=(c == cs[-1])),
                            reads=[VW_b, ptb], writes=[owb])
                    for c in cs:
                        dd = c - (4 * iq - 4)
                        sbk, sbb = mmbank()
                        ph.op("pe", lambda e, sbk=sbk, c=c, qa=qa: e.matmul(
                            sbk[:, :], lhsT=KW[0:64, c * 128:(c + 1) * 128], rhs=qa[0:64, :], start=True, stop=True),
                            reads=[KW_b, qab], writes=[sbb])
                        pt, ptb = ph.rotbuf("PT", PT)
                        ph.op("act", lambda e, sbk=sbk, pt=pt: e.activation(out=pt[:, :], in_=sbk[:, :], func=AF.Exp),
                              reads=[sbb], writes=[ptb])
                        ph.op("dve", lambda e, pt=pt, dd=dd: e.tensor_tensor(
                            out=pt[:, :].rearrange("p (r q) -> p r q", q=128),
                            in0=pt[:, :].rearrange("p (r q) -> p r q", q=128),
                            in1=wmask[:, dd, :].unsqueeze(1).to_broadcast([128, 4, 128]), op=ALU.mult),
                            reads=[k_b, ptb], writes=[ptb])
                        pend.append((c, pt, ptb))
                        if len(pend) > 2:
                            win_pv(pend.pop(0))
                    while pend:
                        win_pv(pend.pop(0))
                    if prev_c2 is not None:
                        prev_c2()
                    ph.op("dve", lambda e, rc=rc: e.tensor_scalar(out=score[:], in0=rbank[:, 0:128], scalar1=rc[:, 0:1],
                                                                  scalar2=None, op0=ALU.mult),
                          reads=[rbank_b, rcb], writes=[score_b])
                    for r4 in range(1, 4):
                        ph.op("dve", lambda e, rc=rc, r4=r4: e.scalar_tensor_tensor(
                            out=score[:], in0=rbank[:, r4 * 128:(r4 + 1) * 128], scalar=rc[:, r4:r4 + 1], in1=score[:],
                            op0=ALU.mult, op1=ALU.add), reads=[rbank_b, rcb, score_b], writes=[score_b])
                    ph.op("dve", lambda e, iq=iq: e.tensor_tensor(out=score[:], in0=score[:], in1=addm[:, iq, :], op=ALU.add),
                          reads=[score_b, k_b], writes=[score_b])
                    ph.op("dve", lambda e: e.max(out=m8[:, 0:8], in_=score[:]), reads=[score_b], writes=[score_b])
                    ph.op("dve", lambda e: e.match_replace(out=stmp[:], in_to_replace=m8[:, 0:8], in_values=score[:],
                                                           imm_value=-3.0e9), reads=[score_b], writes=[score_b])
                    ph.op("dve", lambda e: e.max(out=m8[:, 8:16], in_=stmp[:]), reads=[score_b], writes=[score_b])
                    ph.op("dve", lambda e: e.tensor_scalar(out=m8[:, 15:16], in0=m8[:, 15:16], scalar1=-1.0e8, scalar2=None,
                                                           op0=ALU.max), reads=[score_b], writes=[score_b])
                    ph.op("dve", lambda e: e.tensor_scalar(
                        out=Xm[:, :].rearrange("p (a b) -> p a b", b=128)[:, :, 64:128],
                        in0=score[:, :].rearrange("p (a b) -> p a b", b=64), scalar1=m8[:, 15:16], scalar2=NEGM,
                        op0=ALU.is_lt, op1=ALU.mult), reads=[score_b], writes=[Xm_b])
                    trk, trkb = trbank()
                    for a in range(2):
                        ph.op("pe", lambda e, a=a, trk=trk: e.transpose(out=trk[:, a * 128:(a + 1) * 128],
                                                                       in_=Xm[:, a * 128:(a + 1) * 128], identity=ident[:, :]),
                              reads=[Xm_b, ident_b], writes=[trkb])
                    ph.op("act", lambda e, trk=trk, qa=qa: e.activation(
                        out=qa[64:128, :].rearrange("p (r q) -> p r q", q=128),
                        in_=trk[64:128, 0:128].unsqueeze(1).to_broadcast([64, 4, 128]), func=AF.Copy),
                        reads=[trkb], writes=[qab])
                    if useB:
                        ph.op("act", lambda e, trk=trk, qb=qb: e.activation(
                            out=qb[64:128, :].rearrange("p (r q) -> p r q", q=128),
                            in_=trk[64:128, 128:256].unsqueeze(1).to_broadcast([64, 4, 128]), func=AF.Copy),
                            reads=[trkb], writes=[qbb])
                    finish(ow, owb, 2, g, iq, False)
                    osl, oslb = obank()
                    nsel = 4 * iq + 4
                    pend = []

                    def sel_pv(item, osl=osl, oslb=oslb, nsel=nsel):
                        c, pt, ptb = item
                        ph.op("pe", lambda e, osl=osl, c=c, pt=pt, nsel=nsel: e.matmul(
                            osl[0:65, :], lhsT=VS[:, c, :], rhs=pt[:, :], start=(c == 0), stop=(c == nsel - 1)),
                            reads=[VS_b, ptb], writes=[oslb])
                    for c in range(nsel):
                        sbk, sbb = mmbank()
                        qq, qqb = (qa, qab) if c < 32 else (qb, qbb)
                        ph.op("pe", lambda e, sbk=sbk, c=c, qq=qq: e.matmul(
                            sbk[:, :], lhsT=KE[:, c * 128:(c + 1) * 128], rhs=qq[:, :], start=True, stop=True),
                            reads=[KE_b, KEe_b, qqb], writes=[sbb])
                        pt, ptb = ph.rotbuf("PT", PT)
                        ph.op("act", lambda e, sbk=sbk, pt=pt: e.activation(out=pt[:, :], in_=sbk[:, :], func=AF.Exp),
                              reads=[sbb], writes=[ptb])
                        if c >= 4 * iq:
                            jd = c - 4 * iq
                            ph.op("dve", lambda e, pt=pt, jd=jd: e.tensor_tensor(
                                out=pt[:, :].rearrange("p (r q) -> p r q", q=128),
                                in0=pt[:, :].rearrange("p (r q) -> p r q", q=128),
                                in1=dmask[:, jd, :].unsqueeze(1).to_broadcast([128, 4, 128]), op=ALU.mult),
                                reads=[k_b, ptb], writes=[ptb])
                        pend.append((c, pt, ptb))
                        if len(pend) > 2:
                            sel_pv(pend.pop(0))
                    while pend:
                        sel_pv(pend.pop(0))
                    prev_c1 = (lambda osl=osl, oslb=oslb, g=g, iq=iq: finish(osl, oslb, 1, g, iq, False, last=True))
                    prev_c2 = (lambda g=g, iq=iq: outproj(g, iq))
                prev_c1()
                prev_c2()
                prev_c1 = prev_c2 = None
            ph.emit()

        def sample_phase():
          es4 = contextlib.ExitStack()
          with es4:
            sb4 = lambda name, shape, dt=F32: es4.enter_context(nc.sbuf_tensor(f"p4_{name}", list(shape), dt))
            NEGM = -30000.0
            BIGV = 1.0e9
            ptb_i = sb4("ptb_i", [128, 256], I32)
            ptb_f = sb4("ptb_f", [128, 256])
            idx_i = sb4("idx_i", [128, 256], I32)
            itmp = sb4("itmp", [128, 128], I32)
            pcol = sb4("pcol", [128, 1])
            k_b = Buf("sconst")
            ta = sb4("ta", [128, 128]); tb = sb4("tb", [128, 128]); tmp_b = Buf("stmp")
            aggS = sb4("aggS", [128, 34], BF16)
            addmS = sb4("addmS", [8, 33])
            sel4 = sb4("sel4", [32, 8])
            maskN = sb4("maskN", [128, 16, 32], BF16)
            maskW0 = sb4("maskW0", [128, 32], BF16)
            w1bd = sb4("w1bd", [128, 2, 32, 128], BF16)
            pe_t = sb4("pe_t", [128, 2, 32], BF16)
            w2_t = sb4("w2_t", [128, 2, 64], BF16)
            peb = sb4("peb", [128, 2])
            cw_b = Buf("scmpw")
            wbo = sb4("wbo", [128, 2, D], BF16); wbo_b = Buf("swbo")
            QG = sb4("QG", [64, 4, 4, 128], BF16); QG_b = Buf("QG")
            skvb, skvb_b = skv, skv_b
            KnT = sb4("KnT", [64, 2, 4, 128], BF16); KnT_b = Buf("KnT")
            VnA = sb4("VnA", [128, 2, 4, 65], BF16); VnA_b = Buf("VnA")
            KsE = sb4("KsE", [128, 4, 2048], BF16); KsE_b = Buf("KsE"); KsEe_b = Buf("KsEe")
            VsA = sb4("VsA", [128, 16, 4, 65], BF16); VsA_b = Buf("VsA")
            CX = sb4("CX", [128, 4, 2048], BF16); CX_b = Buf("CX")
            KwT = sb4("KwT", [64, 4, 512], BF16); KwT_b = Buf("KwT")
            VwA = sb4("VwA", [128, 4, 4, 65], BF16); VwA_b = Buf("VwA")
            wsb = sb4("wsb", [128, 4, 512], BF16); wsb_b = Buf("wsb")
            KCs = sb4("KCs", [64, 2, 4, 128], BF16); KCs_bs = [Buf("KCs0"), Buf("KCs1")]
            VCs = sb4("VCs", [128, 2, 4, 65], BF16); VCs_bs = [Buf("VCs0"), Buf("VCs1")]
            hidc = sb4("hidc", [128, 128], BF16); hidc_b = Buf("shidc")
            pgs = [sb4(f"pg{i}", [128, 512], BF16) for i in range(4)]
            QSa = [sb4(f"QSa{i}", [128, 32], BF16) for i in range(2)]
            PTs = [sb4(f"PTs{i}", [128, 512], BF16) for i in range(2)]
            PTw = [sb4(f"PTw{i}", [128, 128], BF16) for i in range(2)]
            PTc = [sb4(f"PTc{i}", [128, 32], BF16) for i in range(2)]
            pnA = sb4("pnA", [128, 2, 4, 512], BF16); pnA_b = Buf("pnA")
            Rn = sb4("Rn", [32, 34]); Rn_b = Buf("Rn")
            scs = sb4("scs", [8, 128]); scs_b = Buf("scs")
            m8 = sb4("sm8", [8, 16])
            Xs = sb4("Xs", [8, 128], BF16); Xs_b = Buf("Xs")
            OTall = sb4("OTall", [65, 3, 4, 512], BF16); OT_b = Buf("OTall")
            rl = [sb4(f"rl{i}", [128, 8]) for i in range(3)]
            acc = sb4("acc", [128, 256]); acc_b = Buf("sacc")
            acct = sb4("acct", [128, 256]); acct_b = Buf("sacct")
            accb = sb4("accb", [128, 256], BF16)
            accT = sb4("accT", [128, 2, 128], BF16); accT_b = Buf("saccT")

            def mmbank():
                return ph.rotbuf("mm", mmb[0:3])

            def obank():
                return ph.rotbuf("ob", mmb[3:5])
            rbank, rbank_b = mmb[5], Buf("mmR")

            def trbank():
                return ph.rotbuf("tr", trb)

            def iota_f(dst_ap, pattern, cm, base=0, shape=None):
                iv = itmp[:] if shape is None else itmp[0:shape[0], 0:shape[1]]
                ph.op("pool", lambda e: e.iota(iv, pattern=pattern, base=base, channel_multiplier=cm), writes=[tmp_b])
                ph.op("pool", lambda e: e.tensor_copy(out=dst_ap, in_=iv), reads=[tmp_b], writes=[k_b])

            iota_f(pcol[:, :], [[0, 1]], 1, shape=(128, 1))
            ph.dma("sp", lambda e: e.dma_start(out=ptb_i[:], in_=ptab.partition_broadcast(128)), writes=[k_b])
            ph.op("pool", lambda e: e.tensor_copy(out=ptb_f[:], in_=ptb_i[:]), reads=[k_b], writes=[k_b])
            ph.op("dve", lambda e: e.tensor_scalar(out=ptb_f[:], in0=ptb_f[:], scalar1=128.0, scalar2=pcol[:, 0:1],
                                                   op0=ALU.mult, op1=ALU.add), reads=[k_b], writes=[k_b])
            ph.op("pool", lambda e: e.tensor_copy(out=idx_i[:], in_=ptb_f[:]), reads=[k_b], writes=[k_b])
            iota_f(ta[:, 0:33], [[-4, 33]], 1, shape=(128, 33))
            ph.op("dve", lambda e: e.tensor_scalar(out=tb[:, 0:33], in0=ta[:, 0:33], scalar1=-1.0, scalar2=None, op0=ALU.is_ge),
                  reads=[k_b], writes=[tmp_b])
            ph.op("dve", lambda e: e.scalar_tensor_tensor(out=tb[:, 0:33], in0=ta[:, 0:33], scalar=3.0, in1=tb[:, 0:33],
                                                          op0=ALU.is_le, op1=ALU.mult), reads=[k_b, tmp_b], writes=[tmp_b])
            ph.op("dve", lambda e: e.tensor_scalar(out=tb[:, 64:97], in0=ta[:, 0:33], scalar1=0.0, scalar2=None, op0=ALU.is_ge),
                  reads=[k_b], writes=[tmp_b])
            ph.op("dve", lambda e: e.scalar_tensor_tensor(out=tb[:, 64:97], in0=ta[:, 0:33], scalar=2.0, in1=tb[:, 64:97],
                                                          op0=ALU.is_le, op1=ALU.mult), reads=[k_b, tmp_b], writes=[tmp_b])
            ph.op("dve", lambda e: e.tensor_tensor(out=aggS[:, 0:33], in0=tb[:, 0:33], in1=tb[:, 64:97], op=ALU.add),
                  reads=[tmp_b], writes=[k_b])
            ph.op("pool", lambda e: e.memset(aggS[:, 33:34], 1.0), writes=[k_b])
            iota_f(ta[0:8, 0:33], [[1, 33]], 0, shape=(8, 33))
            ph.op("dve", lambda e: e.tensor_scalar(out=tb[0:8, 0:33], in0=ta[0:8, 0:33], scalar1=0.0, scalar2=None,
                                                   op0=ALU.is_equal), reads=[k_b], writes=[tmp_b])
            ph.op("dve", lambda e: e.scalar_tensor_tensor(out=tb[0:8, 0:33], in0=ta[0:8, 0:33], scalar=31.0, in1=tb[0:8, 0:33],
                                                          op0=ALU.is_ge, op1=ALU.add), reads=[k_b, tmp_b], writes=[tmp_b])
            ph.op("dve", lambda e: e.tensor_scalar(out=addmS[:, :], in0=tb[0:8, 0:33], scalar1=BIGV, scalar2=None, op0=ALU.mult),
                  reads=[tmp_b], writes=[k_b])
            iota_f(ta[0:32, 0:8], [[-1, 8]], 1, shape=(32, 8))
            ph.op("dve", lambda e: e.tensor_scalar(out=sel4[:, :], in0=ta[0:32, 0:8], scalar1=0.0, scalar2=None, op0=ALU.is_equal),
                  reads=[k_b], writes=[k_b])
            for r4 in range(1, 4):
                ph.op("dve", lambda e, r4=r4: e.scalar_tensor_tensor(out=sel4[:, :], in0=ta[0:32, 0:8], scalar=float(8 * r4),
                                                                     in1=sel4[:, :], op0=ALU.is_equal, op1=ALU.add),
                      reads=[k_b], writes=[k_b])
            iota_f(ta[:, :], [[8, 16], [1, 8]], -1)
            ph.op("dve", lambda e: e.tensor_scalar(out=ta[:], in0=ta[:], scalar1=0.0, scalar2=None, op0=ALU.is_ge),
                  reads=[k_b], writes=[k_b])
            iota_f(tb[:, :], [[8, 16], [0, 8]], -1)
            ph.op("dve", lambda e: e.scalar_tensor_tensor(out=tb[:], in0=tb[:], scalar=0.0, in1=ta[:], op0=ALU.is_le,
                                                          op1=ALU.mult), reads=[k_b], writes=[k_b])
            ph.op("dve", lambda e: e.tensor_copy(
                out=maskN[:, :, :].rearrange("p s (r t) -> p s r t", t=8),
                in_=tb[:, :].rearrange("p (s t) -> p s t", t=8).unsqueeze(2).to_broadcast([128, 16, 4, 8])),
                reads=[k_b], writes=[k_b])
            iota_f(ta[:, 0:32], [[0, 4], [-1, 8]], 1, shape=(128, 32))
            ph.op("dve", lambda e: e.tensor_scalar(out=maskW0[:, :], in0=ta[:, 0:32], scalar1=0.0, scalar2=None, op0=ALU.is_ge),
                  reads=[k_b], writes=[k_b])
            ph.op("pool", lambda e: e.memset(KsE[64:128, :, :], 1.0), writes=[KsEe_b])
            ph.op("pool", lambda e: e.affine_select(out=KsE[64:128, :, :], in_=KsE[64:128, :, :],
                                                    pattern=[[0, 4], [1, 32], [0, 64]], compare_op=ALU.is_equal,
                                                    fill=0.0, base=0, channel_multiplier=-1),
                  reads=[KsEe_b], writes=[KsEe_b])
            ph.op("pool", lambda e: e.memset(VsA[:, :, :, 64:65], 1.0), writes=[VsA_b])
            ph.op("pool", lambda e: e.memset(VwA[:, :, :, 64:65], 1.0), writes=[VwA_b])
            ph.op("pool", lambda e: e.memset(VCs[:, :, :, 64:65], 1.0), writes=VCs_bs)
            ph.op("pool", lambda e: e.memset(VnA[:, :, :, 64:65], 1.0), writes=[VnA_b])
            ph.op("pool", lambda e: e.memset(Xs[:], 0.0), writes=[Xs_b])
            ph.op("pool", lambda e: e.memset(w1bd[:, :, :, :], 0.0), writes=[cw_b])
            for k in range(2):
                for c2 in range(2):
                    ph.dma("pool", lambda e, k=k, c2=c2: e.dma_start(
                        out=w1bd[c2 * 64:(c2 + 1) * 64, k, :, c2 * 64:(c2 + 1) * 64],
                        in_=cmp_w1[k].rearrange("j d h -> d j h")), reads=[cw_b], writes=[cw_b])
                    ph.dma("pool", lambda e, k=k, c2=c2: e.dma_start(
                        out=pe_t[c2 * 64:(c2 + 1) * 64, k, :], in_=cmp_pe[k].rearrange("j d -> d j"),
                        allow_slow_non_contiguous=True), writes=[cw_b])
                    ph.dma("pool", lambda e, k=k, c2=c2: e.dma_start(out=w2_t[c2 * 64:(c2 + 1) * 64, k, :], in_=cmp_w2[k]),
                           writes=[cw_b])
            for k in range(2):
                bk, bkb = mmbank()
                for j in range(32):
                    ph.op("pe", lambda e, k=k, j=j, bk=bk: e.matmul(
                        bk[:, 0:1], lhsT=w1bd[:, k, j, :], rhs=pe_t[:, k, j:j + 1], start=(j == 0), stop=(j == 31)),
                        reads=[cw_b], writes=[bkb])
                ph.op("act", lambda e, k=k, bk=bk: e.activation(out=peb[:, k:k + 1], in_=bk[:, 0:1], func=AF.Copy),
                      reads=[bkb], writes=[cw_b])
            for r4 in range(4):
                hh = r4 % 2
                ph.dma("sp", lambda e, r4=r4, hh=hh: e.dma_start(
                    out=QG[:, r4, :, :],
                    in_=qt_s.rearrange("(g f) p c -> p g f c", f=2)[hh * 64:(hh + 1) * 64, :, r4 // 2, NPT * 128:NT * 128]),
                    writes=[QG_b])
            for w in range(2):
                trk, trkb = trbank()
                for g in range(4):
                    ph.op("pe", lambda e, w=w, g=g, trk=trk: e.transpose(out=trk[0:64, g * 128:(g + 1) * 128],
                                                                         in_=skvb[:, w, g * 64:(g + 1) * 64], identity=ident[:, :]),
                          reads=[skvb_b, ident_b], writes=[trkb])
                ph.op("act", lambda e, w=w, trk=trk: e.activation(
                    out=KnT[:, w, :, :], in_=trk[0:64, 0:512].rearrange("p (g c) -> p g c", c=128), func=AF.Copy),
                    reads=[trkb], writes=[KnT_b])
                ph.op("dve", lambda e, w=w: e.tensor_copy(out=VnA[:, w, :, 0:64],
                                                          in_=skvb[:, w, 256:512].rearrange("p (g d) -> p g d", d=64)),
                      reads=[skvb_b], writes=[VnA_b])

            maskN2 = maskN[:, :, :].rearrange("p s (r t) -> p s r t", t=8)[:, :, 0, :]
            for g in range(4):
                for w in range(2):
                    sn, snb = mmbank()
                    ph.op("pe", lambda e, sn=sn, w=w, g=g: e.matmul(
                        sn[:, :].rearrange("p (r c) -> p r c", c=128), lhsT=KnT[:, w, g, :], rhs=QG[:, :, g, :],
                        start=True, stop=True), reads=[KnT_b, QG_b], writes=[snb])
                    ph.op("act", lambda e, sn=sn, w=w, g=g: e.activation(out=pnA[:, w, g, :], in_=sn[:, :], func=AF.Exp),
                          reads=[snb], writes=[pnA_b])
                    ph.op("dve", lambda e, w=w, g=g: e.tensor_tensor(
                        out=pnA[:, w, g, :].rearrange("p (r s t) -> p r s t", r=4, t=8),
                        in0=pnA[:, w, g, :].rearrange("p (r s t) -> p r s t", r=4, t=8),
                        in1=maskN2.unsqueeze(1).to_broadcast([128, 4, 16, 8]), op=ALU.mult),
                        reads=[pnA_b, k_b], writes=[pnA_b])

            def gather(pool_ap, col):
                pg, pgb = ph.rotbuf("pg", pgs)
                ph.dma("pool", lambda e: e.indirect_dma_start(
                    out=pg[:, :], out_offset=None, in_=pool_ap[:, :],
                    in_offset=bass.IndirectOffsetOnAxis(ap=idx_i[:, col:col + 1], axis=0)), reads=[k_b], writes=[pgb])
                return pg, pgb

            def prep_slc(s):
                for jp in range(8):
                    trk, trkb = trbank()
                    for a in range(2):
                        j = 2 * jp + a
                        pg, pgb = gather(pslc, s * 16 + j)
                        for g in range(4):
                            ph.op("pe", lambda e, a=a, g=g, trk=trk, pg=pg: e.transpose(
                                out=trk[0:64, (a * 4 + g) * 128:(a * 4 + g + 1) * 128], in_=pg[:, g * 64:(g + 1) * 64],
                                identity=ident[:, :]), reads=[pgb, ident_b], writes=[trkb])
                        ph.op("pool", lambda e, j=j, pg=pg: e.tensor_copy(
                            out=VsA[:, j, :, 0:64], in_=pg[:, 256:512].rearrange("p (g d) -> p g d", d=64)),
                            reads=[pgb], writes=[VsA_b])
                    ph.op("act", lambda e, jp=jp, trk=trk: e.activation(
                        out=KsE[0:64, :, jp * 256:(jp + 1) * 256].rearrange("p g (a t) -> p g a t", t=128),
                        in_=trk[0:64, :].rearrange("p (a g t) -> p g a t", a=2, g=4), func=AF.Copy),
                        reads=[trkb], writes=[KsE_b])
            def prep_cmp(s):
                for jp in range(8):
                    trk, trkb = trbank()
                    for a in range(2):
                        j = 2 * jp + a
                        pg, pgb = gather(pcmp, s * 16 + j)
                        for q4 in range(4):
                            ph.op("pe", lambda e, a=a, q4=q4, trk=trk, pg=pg: e.transpose(
                                out=trk[:, (a * 4 + q4) * 128:(a * 4 + q4 + 1) * 128], in_=pg[:, q4 * 128:(q4 + 1) * 128],
                                identity=ident[:, :]), reads=[pgb, ident_b], writes=[trkb])
                    ph.op("dve", lambda e, jp=jp, trk=trk: e.tensor_copy(
                        out=CX[:, :, jp * 256:(jp + 1) * 256].rearrange("p q (a t) -> p q a t", t=128),
                        in_=trk[:, :].rearrange("p (a q t) -> p q a t", a=2, q=4)),
                        reads=[trkb], writes=[CX_b])
            def prep_win(s):
                ph.dma("pool", lambda e, s=s: e.dma_start(
                    out=wsb[:, :, :], in_=swin[s * 512:(s + 1) * 512, :].rearrange("(c p) n -> p c n", p=128)),
                    writes=[wsb_b])
                for half in range(2):
                    trk, trkb = trbank()
                    for c2 in range(2):
                        c = half * 2 + c2
                        for g in range(4):
                            ph.op("pe", lambda e, c=c, c2=c2, g=g, trk=trk: e.transpose(
                                out=trk[0:64, (c2 * 4 + g) * 128:(c2 * 4 + g + 1) * 128], in_=wsb[:, c, g * 64:(g + 1) * 64],
                                identity=ident[:, :]), reads=[wsb_b, ident_b], writes=[trkb])
                    ph.op("act", lambda e, half=half, trk=trk: e.activation(
                        out=KwT[:, :, half * 256:(half + 1) * 256].rearrange("p g (a t) -> p g a t", t=128),
                        in_=trk[0:64, :].rearrange("p (a g t) -> p g a t", a=2, g=4), func=AF.Copy),
                        reads=[trkb], writes=[KwT_b])
                ph.op("pool", lambda e: e.tensor_copy(
                    out=VwA[:, :, :, 0:64], in_=wsb[:, :, 256:512].rearrange("p c (g d) -> p c g d", d=64)),
                    reads=[wsb_b], writes=[VwA_b])
            def compress_s(s, gp_arg):
                for gp in [gp_arg]:
                    for k in range(2):
                        bk, bkb = mmbank()
                        for j in range(32):
                            ph.op("pe", lambda e, k=k, j=j, bk=bk, gp=gp: e.matmul(
                                bk[:, 0:127], lhsT=w1bd[:, k, j, :], rhs=CX[:, k * 2 + gp, j:j + 2017:16],
                                start=(j == 0), stop=(j == 31)), reads=[cw_b, CX_b], writes=[bkb])
                        ph.op("act", lambda e, k=k, bk=bk: e.activation(out=hidc[:, 0:127], in_=bk[:, 0:127], func=AF.Silu,
                                                                        bias=peb[:, k:k + 1]),
                              reads=[bkb, cw_b], writes=[hidc_b])
                        for g2 in range(2):
                            g = gp * 2 + g2
                            po = g2 * 64
                            b2, b2b = mmbank()
                            if k == 0:
                                ph.op("pe", lambda e, b2=b2, po=po: e.matmul(
                                    b2[0:64, 0:127], lhsT=w2_t[po:po + 64, 0, :], rhs=hidc[po:po + 64, 0:127],
                                    start=True, stop=True), reads=[cw_b, hidc_b], writes=[b2b])
                                ph.op("act", lambda e, b2=b2, g=g: e.activation(out=KCs[:, s % 2, g, 0:127], in_=b2[0:64, 0:127],
                                                                                func=AF.Copy), reads=[b2b], writes=[KCs_bs[s % 2]])
                            else:
                                ph.op("pe", lambda e, b2=b2, po=po: e.matmul(
                                    b2[0:127, 0:64], lhsT=hidc[po:po + 64, 0:127], rhs=w2_t[po:po + 64, 1, :],
                                    start=True, stop=True), reads=[cw_b, hidc_b], writes=[b2b])
                                ph.op("act", lambda e, b2=b2, g=g: e.activation(out=VCs[0:127, s % 2, g, 0:64], in_=b2[0:127, 0:64],
                                                                                func=AF.Copy), reads=[b2b], writes=[VCs_bs[s % 2]])
            def attn_s(s, glist):
                for g in glist:
                    qs, qsb = ph.rotbuf("QSa", QSa)
                    ph.op("act", lambda e, qs=qs, g=g, s=s: e.activation(
                        out=qs[0:64, :].rearrange("p (r t) -> p r t", t=8), in_=QG[:, :, g, s * 8:(s + 1) * 8], func=AF.Copy),
                        reads=[QG_b], writes=[qsb])
                    sbk, sbb = mmbank()
                    ph.op("pe", lambda e, sbk=sbk, g=g, qs=qs: e.matmul(sbk[0:127, 0:32], lhsT=KCs[:, s % 2, g, 0:127], rhs=qs[0:64, :],
                                                                      start=True, stop=True), reads=[KCs_bs[s % 2], qsb], writes=[sbb])
                    pc, pcb = ph.rotbuf("PTc", PTc)
                    ph.op("act", lambda e, sbk=sbk, pc=pc: e.activation(out=pc[0:127, :], in_=sbk[0:127, 0:32], func=AF.Exp),
                          reads=[sbb], writes=[pcb])
                    oc, ocb = obank()
                    ph.op("pe", lambda e, oc=oc, g=g, pc=pc: e.matmul(oc[0:65, 0:32], lhsT=VCs[0:127, s % 2, g, :], rhs=pc[0:127, :],
                                                                    start=True, stop=True), reads=[VCs_bs[s % 2], pcb], writes=[ocb])
                    ph.op("act", lambda e, oc=oc, g=g, s=s: e.activation(
                        out=OTall[0:65, 0, g, :].rearrange("p (r c) -> p r c", c=128)[:, :, s * 8:(s + 1) * 8],
                        in_=oc[0:65, 0:32].rearrange("p (r t) -> p r t", t=8), func=AF.Copy), reads=[ocb], writes=[OT_b])
                    ph.op("pe", lambda e, pc=pc: e.matmul(rbank[0:32, 0:34], lhsT=pc[0:127, :], rhs=aggS[0:127, :],
                                                          start=True, stop=True), reads=[pcb, k_b], writes=[rbank_b])
                    ph.op("dve", lambda e: e.reciprocal(out=Rn[:, 33:34], in_=rbank[0:32, 33:34]), reads=[rbank_b], writes=[Rn_b])
                    ph.op("dve", lambda e: e.tensor_scalar(out=Rn[:, 0:33], in0=rbank[0:32, 0:33], scalar1=Rn[:, 33:34],
                                                           scalar2=None, op0=ALU.mult), reads=[rbank_b, Rn_b], writes=[Rn_b])
                    ib, ibb = mmbank()
                    ph.op("pe", lambda e, ib=ib: e.matmul(ib[0:8, 0:33], lhsT=sel4[:, :], rhs=Rn[:, 0:33], start=True, stop=True),
                          reads=[Rn_b, k_b], writes=[ibb])
                    ph.op("dve", lambda e, ib=ib: e.tensor_tensor(out=scs[:, 0:33], in0=ib[0:8, 0:33], in1=addmS[:, :], op=ALU.add),
                          reads=[ibb, k_b], writes=[scs_b])
                    ph.op("dve", lambda e: e.max(out=m8[:, 0:8], in_=scs[:, 0:33]), reads=[scs_b], writes=[scs_b])
                    ph.op("dve", lambda e: e.match_replace(out=scs[:, 64:97], in_to_replace=m8[:, 0:8],
                                                           in_values=scs[:, 0:33], imm_value=-3.0e9),
                          reads=[scs_b], writes=[scs_b])
                    ph.op("dve", lambda e: e.max(out=m8[:, 8:16], in_=scs[:, 64:97]), reads=[scs_b], writes=[scs_b])
                    ph.op("dve", lambda e: e.tensor_scalar(out=Xs[:, 64:97], in0=scs[:, 0:33], scalar1=m8[:, 15:16], scalar2=NEGM,
                                                           op0=ALU.is_lt, op1=ALU.mult), reads=[scs_b], writes=[Xs_b])
                    trk, trkb = trbank()
                    ph.op("pe", lambda e, trk=trk: e.transpose(out=trk[:, 0:8], in_=Xs[0:8, :], identity=ident[0:8, 0:8]),
                          reads=[Xs_b, ident_b], writes=[trkb])
                    ph.op("act", lambda e, trk=trk, qs=qs: e.activation(
                        out=qs[64:128, :].rearrange("p (r t) -> p r t", t=8),
                        in_=trk[64:128, 0:8].unsqueeze(1).to_broadcast([64, 4, 8]), func=AF.Copy), reads=[trkb], writes=[qsb])
                    sbk, sbb = mmbank()
                    for c in range(16):
                        ph.op("pe", lambda e, sbk=sbk, c=c, g=g, qs=qs: e.matmul(
                            sbk[:, c * 32:(c + 1) * 32], lhsT=KsE[:, g, c * 128:(c + 1) * 128], rhs=qs[:, :],
                            start=True, stop=True, skip_group_check=True), reads=[KsE_b, KsEe_b, qsb], writes=[sbb])
                    pt, ptb = ph.rotbuf("PTs", PTs)
                    ph.op("act", lambda e, sbk=sbk, pt=pt: e.activation(out=pt[:, :], in_=sbk[:, :], func=AF.Exp),
                          reads=[sbb], writes=[ptb])
                    osl, oslb = obank()
                    for c in range(16):
                        ph.op("pe", lambda e, osl=osl, c=c, g=g, pt=pt: e.matmul(
                            osl[0:65, 0:32], lhsT=VsA[:, c, g, :], rhs=pt[:, c * 32:(c + 1) * 32], start=(c == 0), stop=False),
                            reads=[VsA_b, ptb], writes=[oslb])
                    pn_s = pnA[:, 0, g, :].rearrange("p (r c) -> p r c", c=128)[:, :, s * 8:(s + 1) * 8]
                    pn_w = pnA[:, 1, g, :].rearrange("p (r c) -> p r c", c=128)[:, :, s * 8:(s + 1) * 8]
                    pnb_w = pnA_b
                    ph.op("pe", lambda e, osl=osl, g=g, pn_s=pn_s: e.matmul(
                        osl[0:65, 0:32].rearrange("p (r t) -> p r t", t=8), lhsT=VnA[:, 0, g, :], rhs=pn_s,
                        start=False, stop=True), reads=[VnA_b, pnA_b], writes=[oslb])
                    ph.op("act", lambda e, osl=osl, g=g, s=s: e.activation(
                        out=OTall[0:65, 1, g, :].rearrange("p (r c) -> p r c", c=128)[:, :, s * 8:(s + 1) * 8],
                        in_=osl[0:65, 0:32].rearrange("p (r t) -> p r t", t=8), func=AF.Copy), reads=[oslb], writes=[OT_b])
                    sbk, sbb = mmbank()
                    for c in range(4):
                        ph.op("pe", lambda e, sbk=sbk, c=c, g=g, qs=qs: e.matmul(
                            sbk[:, c * 32:(c + 1) * 32], lhsT=KwT[:, g, c * 128:(c + 1) * 128], rhs=qs[0:64, :],
                            start=True, stop=True, skip_group_check=True), reads=[KwT_b, qsb], writes=[sbb])
                    pw, pwb = ph.rotbuf("PTw", PTw)
                    ph.op("act", lambda e, sbk=sbk, pw=pw: e.activation(out=pw[:, :], in_=sbk[:, 0:128], func=AF.Exp),
                          reads=[sbb], writes=[pwb])
                    ph.op("dve", lambda e, pw=pw: e.tensor_tensor(out=pw[:, 0:32], in0=pw[:, 0:32], in1=maskW0[:, :], op=ALU.mult),
                          reads=[pwb, k_b], writes=[pwb])
                    ow, owb = obank()
                    for c in range(4):
                        ph.op("pe", lambda e, ow=ow, c=c, g=g, pw=pw: e.matmul(
                            ow[0:65, 0:32], lhsT=VwA[:, c, g, :], rhs=pw[:, c * 32:(c + 1) * 32], start=(c == 0), stop=False),
                            reads=[VwA_b, pwb], writes=[owb])
                    ph.op("pe", lambda e, ow=ow, g=g, pn_w=pn_w: e.matmul(
                        ow[0:65, 0:32].rearrange("p (r t) -> p r t", t=8), lhsT=VnA[:, 1, g, :], rhs=pn_w,
                        start=False, stop=True), reads=[VnA_b, pnb_w], writes=[owb])
                    ph.op("act", lambda e, ow=ow, g=g, s=s: e.activation(
                        out=OTall[0:65, 2, g, :].rearrange("p (r c) -> p r c", c=128)[:, :, s * 8:(s + 1) * 8],
                        in_=ow[0:65, 0:32].rearrange("p (r t) -> p r t", t=8), func=AF.Copy), reads=[owb], writes=[OT_b])

            prep_cmp(0)
            compress_s(0, 0)
            compress_s(0, 1)
            for s in range(16):
                prep_slc(s)
                prep_win(s)
                if s + 1 < 16:
                    prep_cmp(s + 1)
                attn_s(s, [0])
                if s + 1 < 16:
                    compress_s(s + 1, 0)
                attn_s(s, [1, 2])
                if s + 1 < 16:
                    compress_s(s + 1, 1)
                attn_s(s, [3])
            iq = NPT
            for g in range(4):
                ph.dma("pool", lambda e, g=g: e.dma_start(
                    out=wbo[:, :, :], in_=b_out[g * 256:(g + 1) * 256, :].rearrange("(c p) n -> p c n", p=128)),
                    writes=[wbo_b])
                for branch in range(3):
                    tb_, tbb = trbank()
                    for r4 in range(4):
                        ph.op("pe", lambda e, r4=r4, tb_=tb_, g=g, branch=branch: e.transpose(
                            out=tb_[:, r4 * 66:r4 * 66 + 65], in_=OTall[0:65, branch, g, r4 * 128:(r4 + 1) * 128],
                            identity=ident[0:65, 0:65]), reads=[OT_b, ident_b], writes=[tbb])
                    tv = tb_[:, 0:264].rearrange("p (r c) -> p r c", c=66)
                    r_, rb = ph.rotbuf("rl", rl)
                    ph.op("dve", lambda e, r_=r_, tv=tv: e.tensor_scalar(out=r_[:, 0:4], in0=tv[:, :, 64], scalar1=1e-30, scalar2=None,
                                                                         op0=ALU.max), reads=[tbb], writes=[rb])
                    ph.op("dve", lambda e, r_=r_: e.reciprocal(out=r_[:, 0:4], in_=r_[:, 0:4]), reads=[rb], writes=[rb])
                    gv = gates[:, iq, g * 12:(g + 1) * 12].rearrange("p (r b) -> p r b", b=3)[:, :, branch]
                    ph.op("dve", lambda e, r_=r_, gv=gv: e.tensor_tensor(out=r_[:, 4:8], in0=r_[:, 0:4], in1=gv, op=ALU.mult),
                          reads=[rb, gates_b], writes=[rb])
                    glb = r_[:, 4:8].unsqueeze(2).to_broadcast([128, 4, 64])
                    if branch == 0:
                        ph.op("dve", lambda e, tv=tv, glb=glb: e.tensor_tensor(
                            out=acc[:, :].rearrange("p (r d) -> p r d", d=64), in0=tv[:, :, 0:64], in1=glb, op=ALU.mult),
                            reads=[tbb, rb], writes=[acc_b])
                    else:
                        ph.op("dve", lambda e, tv=tv, glb=glb: e.tensor_tensor(
                            out=acct[:, :].rearrange("p (r d) -> p r d", d=64), in0=tv[:, :, 0:64], in1=glb, op=ALU.mult),
                            reads=[tbb, rb], writes=[acct_b])
                        ph.op("dve", lambda e: e.tensor_tensor(out=acc[:, :], in0=acc[:, :], in1=acct[:, :], op=ALU.add),
                              reads=[acct_b, acc_b], writes=[acc_b])
                ph.op("act", lambda e: e.activation(out=accb[:, :], in_=acc[:, :], func=AF.Copy), reads=[acc_b], writes=[acct_b])
                trk, trkb = trbank()
                for a in range(2):
                    ph.op("pe", lambda e, a=a, trk=trk: e.transpose(out=trk[:, a * 128:(a + 1) * 128],
                                                                   in_=accb[:, a * 128:(a + 1) * 128], identity=ident[:, :]),
                          reads=[acct_b, ident_b], writes=[trkb])
                ph.op("act", lambda e, trk=trk: e.activation(out=accT[:, :, :], in_=trk[:, 0:256].rearrange("p (a q) -> p a q", q=128),
                                                             func=AF.Copy), reads=[trkb], writes=[accT_b])
                for half in range(2):
                    bo, bob = mmbank()
                    for a in range(2):
                        ph.op("pe", lambda e, a=a, bo=bo, half=half: e.matmul(
                            bo[:, :], lhsT=accT[:, a, :], rhs=wbo[:, a, half * 512:(half + 1) * 512],
                            start=(a == 0), stop=(a == 1)), reads=[accT_b, wbo_b], writes=[bob])
                    ph.op("dve", lambda e, bo=bo, half=half: e.tensor_tensor(
                        out=h[:, iq, half * 512:(half + 1) * 512], in0=bo[:, :],
                        in1=h[:, iq, half * 512:(half + 1) * 512], op=ALU.add), reads=[bob, hb[iq]], writes=[hb[iq]])
            ph.emit()

        if mode == "samp_test":
            ph.op("pool", lambda e: e.memset(identf[:], 0.0), writes=[ident_b])
            ph.op("pool", lambda e: e.affine_select(out=identf[:], in_=identf[:], pattern=[[-1, 128]],
                                                    compare_op=ALU.not_equal, fill=1.0, base=0,
                                                    channel_multiplier=1), reads=[ident_b], writes=[ident_b])
            ph.op("pool", lambda e: e.tensor_copy(out=ident[:], in_=identf[:]), reads=[ident_b], writes=[ident_b])
            ph.dma("sp", lambda e: e.dma_start(out=gates[:, :, :].rearrange("p a b -> p (a b)"), in_=gates_in[:, :]),
                   writes=[gates_b])
            ph.dma("sp", lambda e: e.dma_start(out=skv[:, :, :], in_=skv_in[:, :, :]), writes=[skv_b])
            ph.dma("sp", lambda e: e.dma_start(out=h[:, NPT, :], in_=xs[:, :]), writes=[hb[NPT]])
            sample_phase()
            ph.dma("sp", lambda e: e.dma_start(out=y_o[NPT * 128:NT * 128, :], in_=h[:, NPT, :]), reads=[hb[NPT]])
            ph.emit()
        elif mode == "attn_test":
            ph.op("pool", lambda e: e.memset(identf[:], 0.0), writes=[ident_b])
            ph.op("pool", lambda e: e.affine_select(out=identf[:], in_=identf[:], pattern=[[-1, 128]],
                                                    compare_op=ALU.not_equal, fill=1.0, base=0,
                                                    channel_multiplier=1), reads=[ident_b], writes=[ident_b])
            ph.op("pool", lambda e: e.tensor_copy(out=ident[:], in_=identf[:]), reads=[ident_b], writes=[ident_b])
            ph.dma("sp", lambda e: e.dma_start(out=gates[:, :, :].rearrange("p a b -> p (a b)"), in_=gates_in[:, :]),
                   writes=[gates_b])
            for ti in range(NT):
                src = xp[ti * 128:(ti + 1) * 128, :] if ti < NPT else xs[:, :]
                ph.dma("sp", lambda e, ti=ti, src=src: e.dma_start(out=h[:, ti, :], in_=src), writes=[hb[ti]])
            attn_phase()
            for ti in range(NT):
                ph.dma("sp", lambda e, ti=ti: e.dma_start(out=y_o[ti * 128:(ti + 1) * 128, :], in_=h[:, ti, :]),
                       reads=[hb[ti]])
            ph.emit()
        elif mode == "dense_only":
            dense_phase("p1", "p1", groups)
            dense_phase("p5", "p5", groups)
        else:
            dense_phase("p1", "p1", groups)
            rg = [[0, 1, 2, 3], [4, 5, 6, 7]]
            for c in range(4):
                ph.dma("pool", lambda e, c=c: e.collective_compute(
                    "AllGather", ALU.bypass, replica_groups=rg,
                    ins=[ftl[c].rearrange("a p t -> (a p) t").opt()],
                    outs=[gftl[c].rearrange("r a p t -> (r a p) t").opt()]), semq="cc", inc=1)
            for c in range(2):
                ph.dma("pool", lambda e, c=c: e.collective_compute(
                    "AllGather", ALU.bypass, replica_groups=rg,
                    ins=[vtl[c].opt()], outs=[gvtl[c].rearrange("r t d -> (r t) d").opt()]), semq="cc", inc=1)
            sample_phase()
            attn_phase()
            dense_phase("p5", "p5", groups)
    return nc


_NC_CACHE = {}


def kernel(x_prompt, x_sample, cache_cmp_kv, cache_slc_kv, state_win_kv, state_conv, page_table,
           norm_w, final_norm_w, a_in_w, a_conv_w, a_out_w, b_in_w, b_out_w, kv_norm_w, kv_w,
           cmp_pe, cmp_w1, cmp_w2, ffn_in_w, ffn_out_w):
    f = lambda a: np.ascontiguousarray(np.asarray(a, dtype=np.float32))
    x_prompt = f(x_prompt); x_sample = f(x_sample)
    if "nc" not in _NC_CACHE:
        _NC_CACHE["nc"] = build_nc()
    nc = _NC_CACHE["nc"]
    norms = np.stack([f(norm_w)[0, 0], f(norm_w)[0, 1], f(norm_w)[1, 0], f(norm_w)[1, 1],
                      f(final_norm_w), f(kv_norm_w)], 0)
    nwc = np.ascontiguousarray(norms.reshape(6, 8, 128).transpose(2, 0, 1).reshape(128, 48))
    convw = np.ascontiguousarray(f(a_conv_w)[0].reshape(3, 8, 128).transpose(2, 0, 1).reshape(128, 24))
    fnw = f(final_norm_w).reshape(1, D)
    shared = dict(nwc=nwc, fnw=fnw, convw=convw, a_in=f(a_in_w)[0], a_out=f(a_out_w)[0], ffn_in=f(ffn_in_w),
                  ffn_out=f(ffn_out_w), kv_w=f(kv_w), b_in=f(b_in_w)[0], b_out=f(b_out_w)[0],
                  cmp_w1=f(cmp_w1), cmp_w2=f(cmp_w2), cmp_pe=f(cmp_pe),
                  pcmp=f(cache_cmp_kv).reshape(-1, 512), pslc=f(cache_slc_kv).reshape(-1, 512))
    ptab_all = np.ascontiguousarray(np.asarray(page_table, dtype=np.int32))
    swin_all = f(state_win_kv).reshape(128, 512, 512)
    sconv_all = f(state_conv)[0]
    in_maps = []
    for c in range(NCORES):
        n, cc = c // 4, c % 4
        xs_seq = x_prompt[n].reshape(64, 128, D)
        xp = np.ascontiguousarray(xs_seq[cc::4].reshape(NPT * 128, D))
        xh = np.zeros((32, D), np.float32)
        for i in range(NPT):
            qt = 4 * i + cc
            if qt > 0:
                xh[2 * i:2 * i + 2] = x_prompt[n, qt * 128 - 2:qt * 128]
        m = dict(shared)
        meta = np.zeros((128, 32), np.float32)
        pidx = np.arange(128)
        meta[:, 0] = 2 * cc + (pidx >= 64)
        meta[:, 1] = 128 * cc
        for j in range(4):
            meta[:, 2 + j] = 1.0 if j != cc else 0.0
        for d in range(8):
            rel = d - 4 - cc
            meta[:, 6 + d] = 1.0 if -4 < rel < 0 else 0.0
            meta[:, 14 + d] = 1.0 if rel == 0 else 0.0
            meta[:, 22 + d] = 1.0 if rel == -4 else 0.0
        m["meta"] = meta
        m["ptab"] = np.ascontiguousarray(ptab_all[16 * c:16 * c + 16].reshape(1, 256))
        m.update(xp=xp, xh=xh, xs=np.ascontiguousarray(x_sample[16 * c:16 * c + 16].reshape(128, D)),
                 sconv=np.ascontiguousarray(sconv_all[16 * c:16 * c + 16].reshape(32, D)),
                 swin=np.ascontiguousarray(swin_all[16 * c:16 * c + 16].reshape(16 * 512, 512)))
        in_maps.append(m)
    res = run_bass_kernel_spmd(nc, in_maps, core_ids=list(range(NCORES))).results

    y_prompt = np.zeros((2, 8192, D), np.float32)
    y_sample = np.zeros((128, 8, D), np.float32)
    conv_p = np.zeros((1, 2, 2, D), np.float32)
    conv_s = np.zeros((1, 128, 2, D), np.float32)
    cmp_p = np.zeros((2, 8192, 2, 4, 64), np.float32)
    slc_p = np.zeros((2, 8192, 2, 4, 64), np.float32)
    cmp_s = np.zeros((128, 8, 2, 4, 64), np.float32)
    slc_s = np.zeros((128, 8, 2, 4, 64), np.float32)
    win_p = np.zeros((2, 512, 2, 4, 64), np.float32)
    win_s = np.zeros((128, 512, 2, 4, 64), np.float32)
    for c in range(NCORES):
        n, cc = c // 4, c % 4
        r = res[c]
        yv = y_prompt[n].reshape(64, 128, D)
        yv[cc::4] = r["y_o"][:NPT * 128].reshape(NPT, 128, D)
        y_sample[16 * c:16 * c + 16] = r["y_o"][NPT * 128:].reshape(16, 8, D)
        cmp_p[n].reshape(64, 128, 512)[cc::4] = r["cmp_o"][:NPT * 128].reshape(NPT, 128, 512)
        slc_p[n].reshape(64, 128, 512)[cc::4] = r["slc_o"][:NPT * 128].reshape(NPT, 128, 512)
        cmp_s[16 * c:16 * c + 16] = r["cmp_o"][NPT * 128:].reshape(16, 8, 2, 4, 64)
        slc_s[16 * c:16 * c + 16] = r["slc_o"][NPT * 128:].reshape(16, 8, 2, 4, 64)
        conv_s[0, 16 * c:16 * c + 16] = r["conv_o"][0:32].reshape(16, 2, D)
        if cc == 3:
            conv_p[0, n] = r["conv_o"][32:34]
        win_p[n, cc * 128:(cc + 1) * 128] = r["winp_o"].reshape(128, 2, 4, 64)
        win_s[16 * c:16 * c + 16] = r["wins_o"].reshape(16, 512, 2, 4, 64)
    return (y_prompt, y_sample, conv_p, conv_s, cmp_p, cmp_s, slc_p, slc_s, win_p, win_s)
```

```python
import contextlib
import os
DBG = os.environ.get('KDBG', '')
import numpy as np
import concourse.bass as bass
import concourse.mybir as mybir
from concourse.bass_utils import run_bass_kernel_spmd

F32 = mybir.dt.float32
BF16 = mybir.dt.bfloat16
I32 = mybir.dt.int32
AF = mybir.ActivationFunctionType
ALU = mybir.AluOpType

NCORES = 8
D = 1024
NT = 17
NPT = 16
DFF = 2816
NJ = 22
EPS = 1e-6


class Buf:
    __slots__ = ("name", "last_w", "readers")

    def __init__(self, name):
        self.name = name
        self.last_w = None
        self.readers = {}


class Phase:
    ENGS = ["pe", "act", "dve", "pool", "sp"]
    NDMA = 12

    def __init__(self, nc, es, tag):
        self.nc = nc
        self.tag = tag
        self.csem = {e: es.enter_context(nc.semaphore(f"{tag}_{e}")) for e in ["pe", "act", "dve", "pool"]}
        self.dsem = {q: [es.enter_context(nc.semaphore(f"{tag}_{q}d{i}")) for i in range(self.NDMA)]
                     for q in ["sp", "pool", "cc"]}
        self.ccnt = {e: 0 for e in ["pe", "act", "dve", "pool"]}
        self.dcnt = {q: [0] * self.NDMA for q in ["sp", "pool", "cc"]}
        self.dnext = {q: 0 for q in ["sp", "pool", "cc"]}
        self.waited = {e: {} for e in self.ENGS}
        self.prog = {e: [] for e in self.ENGS}
        self.rot = {}

    def sem(self, key):
        if key[0] == "c":
            return self.csem[key[1]]
        return self.dsem[key[1]][key[2]]

    def _deps(self, eng, reads, writes):
        deps = {}

        def add(k, v):
            if deps.get(k, 0) < v:
                deps[k] = v

        for r in reads:
            if r.last_w is not None:
                add(*r.last_w)
        for w in writes:
            if w.last_w is not None:
                add(*w.last_w)
            for k, v in w.readers.items():
                add(k, v)
        waits = []
        for k, v in deps.items():
            if eng == "pe" and k == ("c", "pe"):
                continue
            if self.waited[eng].get(k, 0) < v:
                self.waited[eng][k] = v
                waits.append((k, v))
        return waits

    def _commit(self, key, val, reads, writes):
        for r in reads:
            if r.readers.get(key, 0) < val:
                r.readers[key] = val
        for w in writes:
            w.last_w = (key, val)
            w.readers = {}

    def op(self, eng, fn, reads=(), writes=()):
        excl = [r for r in reads if r.name.startswith(("mm", "tr"))]
        if excl and eng != "pe":
            reads = [r for r in reads if r not in excl]
            writes = list(writes) + excl
        waits = self._deps(eng, reads, writes)
        self.ccnt[eng] += 1
        key = ("c", eng)
        self.prog[eng].append((waits, fn, key, 1))
        self._commit(key, self.ccnt[eng], reads, writes)

    def dma(self, q, fn, reads=(), writes=(), semq=None, inc=16):
        waits = self._deps(q, reads, writes)
        sq = semq or q
        slot = self.dnext[sq]
        self.dnext[sq] = (slot + 1) % self.NDMA
        key = ("d", sq, slot)
        prev = self.dcnt[sq][slot]
        if prev > 0 and self.waited[q].get(key, 0) < prev:
            self.waited[q][key] = prev
            waits.append((key, prev))
        self.dcnt[sq][slot] = prev + inc
        self.prog[q].append((waits, fn, key, inc))
        self._commit(key, prev + inc, reads, writes)

    def rotbuf(self, name, tensors):
        if name not in self.rot:
            self.rot[name] = [0, [(t, Buf(f"{name}{i}")) for i, t in enumerate(tensors)]]
        st = self.rot[name]
        r = st[1][st[0] % len(st[1])]
        st[0] += 1
        return r

    def emit(self):
        nc = self.nc
        finals = [(("c", e), self.ccnt[e]) for e in self.ccnt if self.ccnt[e] > 0]
        for q in self.dcnt:
            for s in range(self.NDMA):
                if self.dcnt[q][s] > 0:
                    finals.append((("d", q, s), self.dcnt[q][s]))
        prog = self.prog
        self.prog = {e: [] for e in self.ENGS}
        self.rot = {}
        with nc.Block() as block:
            for e, reg in [("pe", block.tensor), ("act", block.scalar), ("dve", block.vector),
                           ("pool", block.gpsimd), ("sp", block.sync)]:
                def body(engine, e=e):
                    for waits, fn, key, inc in prog[e]:
                        for k, v in waits:
                            engine.wait_ge(self.sem(k), v)
                        ins = fn(engine)
                        if ins is not None:
                            ins.then_inc(self.sem(key), inc)
                    for k, v in finals:
                        engine.wait_ge(self.sem(k), v)
                        if self.waited[e].get(k, 0) < v:
                            self.waited[e][k] = v
                reg(body)


def build_nc(mode="full", groups=None, npool=2560):
    nc = bass.Bass("TRN2", target_bir_lowering=False)
    din = lambda name, shape, dt=F32: nc.dram_tensor(name, list(shape), dt, kind="ExternalInput").ap()
    dout = lambda name, shape, dt=F32: nc.dram_tensor(name, list(shape), dt, kind="ExternalOutput").ap()
    xp = din("xp", [NPT * 128, D])
    xh = din("xh", [32, D])
    xs = din("xs", [128, D])
    sconv = din("sconv", [32, D])
    swin = din("swin", [16 * 512, 512])
    nwc = din("nwc", [128, 48])
    fnw = din("fnw", [1, D])
    convw = din("convw", [128, 24])
    a_in = din("a_in", [D, 3 * D])
    a_out = din("a_out", [D, D])
    ffn_in = din("ffn_in", [2, D, 2 * DFF])
    ffn_out = din("ffn_out", [2, DFF, D])
    kv_w = din("kv_w", [D, 1536])
    b_in = din("b_in", [D, 1072])
    b_out = din("b_out", [D, D])
    cmp_w1 = din("cmp_w1", [2, 32, 64, 64])
    cmp_w2 = din("cmp_w2", [2, 64, 64])
    cmp_pe = din("cmp_pe", [2, 32, 64])
    meta = din("meta", [128, 32])
    pcmp = din("pcmp", [npool * 128, 512])
    pslc = din("pslc", [npool * 128, 512])
    ptab = din("ptab", [1, 256], I32)

    y_o = dout("y_o", [NT * 128, D])
    conv_o = dout("conv_o", [34, D])
    cmp_o = dout("cmp_o", [NT * 128, 512])
    slc_o = dout("slc_o", [NT * 128, 512])
    winp_o = dout("winp_o", [128, 512])
    wins_o = dout("wins_o", [16 * 512, 512])

    ftl = [nc.dram_tensor(f"ft_s{c}", [2, 128, NPT * 128], BF16).ap() for c in range(4)]
    vtl = [nc.dram_tensor(f"vt_s{c}", [NPT * 128, 256], BF16).ap() for c in range(2)]
    if mode == "samp_test":
        skv_in = din("skv_in", [128, 2, 512], BF16)
    if mode in ("attn_test", "samp_test"):
        qt_s = din("qt_s", [8, 128, NT * 128], BF16)
        gft = din("gft", [4, 8, 128, NPT * 128], BF16)
        gvt = din("gvt", [4, 2, NPT * 128, 256], BF16)
        gftl = [gft[:, 2 * c:2 * c + 2] for c in range(4)]
        gvtl = [gvt[:, c] for c in range(2)]
        gates_in = din("gates_in", [128, NT * 48])
    else:
        qt_s = nc.dram_tensor("qt_s", [8, 128, NT * 128], BF16).ap()
        gftl = [nc.dram_tensor(f"gft{c}", [4, 2, 128, NPT * 128], BF16).ap() for c in range(4)]
        gvtl = [nc.dram_tensor(f"gvt{c}", [4, NPT * 128, 256], BF16).ap() for c in range(2)]

    es = contextlib.ExitStack()
    with es:
        sb = lambda name, shape, dt=F32: es.enter_context(nc.sbuf_tensor(name, list(shape), dt))
        ps = lambda name, shape, dt=F32: es.enter_context(nc.psum_tensor(name, list(shape), dt))
        h = sb("h", [128, NT, D])
        hb = [Buf(f"h{i}") for i in range(NT)]
        gates = sb("gates", [128, NT, 48])
        gates_b = Buf("gates")
        skv = sb("skv", [128, 2, 512], BF16)
        skv_b = Buf("skv")
        ident = sb("ident", [128, 128], BF16)
        identf = sb("identf", [128, 128], F32)
        ident_b = Buf("ident")
        nwc_t = sb("nwc_t", [128, 48])
        convw_t = sb("convw_t", [128, 24])
        fnw_t = sb("fnw_t", [128, D])
        const_b = Buf("consts")

        mmb = [ps(f"mm{i}", [128, 512]) for i in range(6)]
        trb = [ps(f"tr{i}", [128, 1024], BF16) for i in range(2)]

        ph = Phase(nc, es, "k")

        def dense_phase(tag, stage, groups):
          es1 = contextlib.ExitStack()
          with es1:
            sb1 = lambda name, shape, dt=F32: es1.enter_context(nc.sbuf_tensor(f"{tag}_{name}", list(shape), dt))
            xhalo = sb1("xhalo", [32, D])
            xhalo_b = Buf("xhalo")
            sconv_t = sb1("sconv_t", [34, D])
            sconvT = sb1("sconvT", [128, 8, 32])
            sconvT_b = Buf("sconvT")
            vh = sb1("vh", [128, 8, 32])
            vh_b = Buf("vh")
            cstage = sb1("cstage", [128, 8, 34])
            cstage_b = Buf("cstage")
            cout = sconv_t
            xT = sb1("xT", [128, 8, 544], BF16)
            xT_b = Buf("xT")
            junk = sb1("junk", [128, D], BF16)
            junk_b = Buf("junk")
            xr = [sb1(f"xr{i}", [128, D], BF16) for i in range(2)]
            ss = [sb1(f"ss{i}", [128, 2]) for i in range(4)]
            wp = [sb1(f"wp{i}", [128, 8, 384], BF16) for i in range(3)]
            wres = [sb1(f"wres{i}", [128, 6, D], BF16) for i in range(2)]
            csb = [sb1(f"csb{i}", [128, 512]) for i in range(2)]
            vbuf = [sb1(f"vbuf{i}", [128, 520]) for i in range(2)]
            t1b = [sb1(f"t1b{i}", [128, 512]) for i in range(2)]
            bcT = sb1("bcT", [128, 8, 512], BF16)
            bcT_b = Buf("bcT")
            hidT = bcT
            hidT_b = bcT_b
            xT2 = bcT
            xT2_b = bcT_b
            kvrow = [sb1(f"kvrow{i}", [128, 512]) for i in range(3)]
            vst = [sb1(f"vst{i}", [128, 256], BF16) for i in range(2)]
            fst = [sb1(f"fst{i}", [128, 512], BF16) for i in range(3)]

            def mmbank():
                return ph.rotbuf("mm", mmb)

            def trbank():
                return ph.rotbuf("tr", trb)

            sconv_b = Buf("sconv")
            wins_b = Buf("wins")
            if stage == "p1":
                ph.op("pool", lambda e: e.memset(identf[:], 0.0), writes=[ident_b])
                ph.op("pool", lambda e: e.affine_select(out=identf[:], in_=identf[:], pattern=[[-1, 128]],
                                                        compare_op=ALU.not_equal, fill=1.0, base=0,
                                                        channel_multiplier=1),
                      reads=[ident_b], writes=[ident_b])
                ph.op("pool", lambda e: e.tensor_copy(out=ident[:], in_=identf[:]), reads=[ident_b], writes=[ident_b])
                ph.dma("sp", lambda e: e.dma_start(out=nwc_t[:], in_=nwc[:, :]), writes=[const_b])
                ph.dma("sp", lambda e: e.dma_start(out=convw_t[:], in_=convw[:, :]), writes=[const_b])
                ph.dma("sp", lambda e: e.dma_start(out=fnw_t[:], in_=fnw.partition_broadcast(128)), writes=[const_b])
                ph.dma("sp", lambda e: e.dma_start(out=xhalo[:], in_=xh[:, :]), writes=[xhalo_b])
                ph.dma("sp", lambda e: e.dma_start(out=sconv_t[0:32, :], in_=sconv[:, :]), writes=[sconv_b])
                ph.dma("sp", lambda e: e.dma_start(
                    out=wins_o.rearrange("(s r) c -> s r c", r=512)[:, 0:504, :],
                    in_=swin.rearrange("(s r) c -> s r c", r=512)[:, 8:512, :]), writes=[wins_b])
                for ti in range(NT):
                    src = xp[ti * 128:(ti + 1) * 128, :] if ti < NPT else xs[:, :]
                    ph.dma("sp", lambda e, ti=ti, src=src: e.dma_start(out=h[:, ti, :], in_=src), writes=[hb[ti]])
                for half in range(2):
                    bank, bb = mmbank()
                    for jj in range(4):
                        j = half * 4 + jj
                        ph.op("pe", lambda e, j=j, jj=jj, bank=bank: e.transpose(
                            out=bank[:, jj * 32:(jj + 1) * 32], in_=sconv_t[0:32, j * 128:(j + 1) * 128],
                            identity=identf[0:32, 0:32]), reads=[sconv_b, ident_b], writes=[bb])
                    ph.op("act", lambda e, half=half, bank=bank: e.activation(
                        out=sconvT[:, half * 4:(half + 1) * 4, :],
                        in_=bank[:, 0:128].rearrange("p (j c) -> p j c", c=32), func=AF.Copy),
                        reads=[bb], writes=[sconvT_b])

            def rstd_of(src_ap, src_b, P):
                st, stb = ph.rotbuf("ss", ss)
                ph.op("act", lambda e: e.activation(out=junk[0:P, :], in_=src_ap, func=AF.Square,
                                                    accum_out=st[0:P, 0:1]),
                      reads=[src_b], writes=[junk_b, stb])
                ph.op("dve", lambda e: e.tensor_scalar(out=st[0:P, 1:2], in0=st[0:P, 0:1], scalar1=1.0 / D,
                                                       scalar2=EPS, op0=ALU.mult, op1=ALU.add),
                      reads=[stb], writes=[stb])
                ph.op("act", lambda e: e.activation(out=st[0:P, 1:2], in_=st[0:P, 1:2], func=AF.Sqrt),
                      reads=[stb], writes=[stb])
                ph.op("dve", lambda e: e.reciprocal(out=st[0:P, 1:2], in_=st[0:P, 1:2]),
                      reads=[stb], writes=[stb])
                return st, stb

            def norm_T(src_ap, src_b, P, dsts):
                st, stb = rstd_of(src_ap, src_b, P)
                x_r, xrb = ph.rotbuf("xr", xr)
                ph.op("dve", lambda e: e.tensor_scalar(out=x_r[0:P, :], in0=src_ap, scalar1=st[0:P, 1:2],
                                                       scalar2=None, op0=ALU.mult),
                      reads=[src_b, stb], writes=[xrb])
                bank, bb = trbank()
                for k in range(8):
                    ph.op("pe", lambda e, k=k: e.transpose(out=bank[:, k * 128:k * 128 + P],
                                                           in_=x_r[0:P, k * 128:(k + 1) * 128],
                                                           identity=ident[0:P, 0:P]),
                          reads=[xrb, ident_b], writes=[bb])
                for (dt_, db, c0, ni) in dsts:
                    ph.op("dve", lambda e, dt_=dt_, c0=c0, ni=ni: e.tensor_tensor(
                        out=dt_[:, :, c0:c0 + P],
                        in0=bank[:, :].rearrange("p (k c) -> p k c", c=128)[:, :, 0:P],
                        in1=nwc_t[:, ni * 8:(ni + 1) * 8].unsqueeze(2).to_broadcast([128, 8, P]),
                        op=ALU.mult), reads=[bb, const_b], writes=[db])

            def splits_of(n):
                return [(0, min(512, n))] + ([(512, n)] if n > 512 else [])

            def ffn(l, tiles, ni):
                n = 128 * len(tiles)
                for t, ti in enumerate(tiles):
                    norm_T(h[:, ti, :], hb[ti], 128, [(xT, xT_b, t * 128, ni)])
                for (j0, j1) in [(0, 6), (6, 12), (12, 17), (17, 22)]:
                    nj = j1 - j0
                    wr, wrb = ph.rotbuf("wres", wres)
                    ph.dma("pool", lambda e, wr=wr, j0=j0, j1=j1, nj=nj: e.dma_start(
                        out=wr[:, 0:nj, :],
                        in_=ffn_out[l, j0 * 128:j1 * 128, :].rearrange("(j p) n -> p j n", p=128)),
                        writes=[wrb])
                    for jj in range(nj):
                        j = j0 + jj
                        w, wb = ph.rotbuf("wp", wp)
                        for t2 in range(2):
                            ph.dma("pool", lambda e, w=w, j=j, t2=t2: e.dma_start(
                                out=w[:, :, t2 * 128:(t2 + 1) * 128],
                                in_=ffn_in[l, :, t2 * DFF + j * 128:t2 * DFF + (j + 1) * 128].rearrange("(k p) f -> p k f", p=128)),
                                writes=[wb])
                        bg, bgb = mmbank()
                        bu, bub = mmbank()
                        for k in range(8):
                            ph.op("pe", lambda e, k=k, w=w, bg=bg: e.matmul(
                                bg[:, 0:n], lhsT=w[:, k, 0:128], rhs=xT[:, k, 0:n], start=(k == 0), stop=(k == 7)),
                                reads=[wb, xT_b], writes=[bgb])
                        for k in range(8):
                            ph.op("pe", lambda e, k=k, w=w, bu=bu: e.matmul(
                                bu[:, 0:n], lhsT=w[:, k, 128:256], rhs=xT[:, k, 0:n], start=(k == 0), stop=(k == 7)),
                                reads=[wb, xT_b], writes=[bub])
                        sg, sgb = ph.rotbuf("csb", csb)
                        ph.op("act", lambda e, sg=sg, bg=bg: e.activation(out=sg[:, 0:n], in_=bg[:, 0:n], func=AF.Silu),
                              reads=[bgb], writes=[sgb])
                        ph.op("dve", lambda e, sg=sg, bu=bu, jj=jj: e.tensor_tensor(
                            out=hidT[:, jj, 0:n], in0=bu[:, 0:n], in1=sg[:, 0:n], op=ALU.mult),
                            reads=[bub, sgb], writes=[hidT_b])
                    for t, ti in enumerate(tiles):
                        for half in range(2):
                            bo, bob = mmbank()
                            for jj in range(nj):
                                ph.op("pe", lambda e, jj=jj, bo=bo, wr=wr, t=t, half=half, nj=nj: e.matmul(
                                    bo[:, :], lhsT=hidT[:, jj, t * 128:(t + 1) * 128],
                                    rhs=wr[:, jj, half * 512:(half + 1) * 512], start=(jj == 0), stop=(jj == nj - 1)),
                                    reads=[hidT_b, wrb], writes=[bob])
                            ph.op("dve", lambda e, bo=bo, ti=ti, half=half: e.tensor_tensor(
                                out=h[:, ti, half * 512:(half + 1) * 512], in0=bo[:, :],
                                in1=h[:, ti, half * 512:(half + 1) * 512], op=ALU.add),
                                reads=[bob, hb[ti]], writes=[hb[ti]])
            def mixer(gi, tiles, with_halo):
                sample = tiles[0] == NPT
                n_main = 128 * len(tiles)
                for t, ti in enumerate(tiles):
                    norm_T(h[:, ti, :], hb[ti], 128, [(xT, xT_b, t * 128, 0)])
                if with_halo:
                    norm_T(xhalo[0:32, :], xhalo_b, 32, [(xT, xT_b, 512, 0)])
                spl = ([(512, 544)] if with_halo else []) + [(0, n_main)]
                for j in range(8):
                    w, wb = ph.rotbuf("wp", wp)
                    for t3 in range(3):
                        ph.dma("pool", lambda e, w=w, j=j, t3=t3: e.dma_start(
                            out=w[:, :, t3 * 128:(t3 + 1) * 128],
                            in_=a_in[:, t3 * D + j * 128:t3 * D + (j + 1) * 128].rearrange("(k p) f -> p k f", p=128)),
                            writes=[wb])
                    for (c0, c1) in spl:
                        n = c1 - c0
                        halo = c0 == 512
                        banks = []
                        for t3 in range(3):
                            if halo and t3 == 0:
                                banks.append((None, None))
                                continue
                            bk, bkb = mmbank()
                            for k in range(8):
                                ph.op("pe", lambda e, k=k, w=w, bk=bk, t3=t3, c0=c0, c1=c1, n=n: e.matmul(
                                    bk[:, 0:n], lhsT=w[:, k, t3 * 128:(t3 + 1) * 128], rhs=xT[:, k, c0:c1],
                                    start=(k == 0), stop=(k == 7)), reads=[wb, xT_b], writes=[bkb])
                            banks.append((bk, bkb))
                        (pb, pbb), (pc, pcb), (pu, pub) = banks
                        cs, csb_ = ph.rotbuf("csb", csb)
                        ph.op("act", lambda e, cs=cs, pc=pc, n=n: e.activation(out=cs[:, 0:n], in_=pc[:, 0:n], func=AF.Copy),
                              reads=[pcb], writes=[csb_])
                        if halo:
                            ph.op("dve", lambda e, cs=cs, pu=pu, j=j: e.tensor_tensor(
                                out=vh[:, j, :], in0=pu[:, 0:32], in1=cs[:, 0:32], op=ALU.mult),
                                reads=[pub, csb_], writes=[vh_b])
                            continue
                        vb, vbb = ph.rotbuf("vbuf", vbuf)
                        if sample:
                            nb, L = 16, 8
                        else:
                            nb, L = len(tiles), 128
                        W = L + 2
                        vv = vb[:, 0:nb * W].rearrange("p (b c) -> p b c", c=W)
                        ph.op("dve", lambda e, vv=vv, pu=pu, cs=cs, L=L, n=n: e.tensor_tensor(
                            out=vv[:, :, 2:2 + L], in0=pu[:, 0:n].rearrange("p (b c) -> p b c", c=L),
                            in1=cs[:, 0:n].rearrange("p (b c) -> p b c", c=L), op=ALU.mult),
                            reads=[pub, csb_], writes=[vbb])
                        if sample:
                            ph.op("pool", lambda e, vv=vv, j=j: e.tensor_copy(
                                out=vv[:, :, 0:2], in_=sconvT[:, j, :].rearrange("p (s r) -> p s r", r=2)),
                                reads=[sconvT_b], writes=[vbb])
                            ph.op("pool", lambda e, vv=vv, j=j: e.tensor_copy(
                                out=cstage[:, j, 0:32].rearrange("p (s r) -> p s r", r=2), in_=vv[:, :, 8:10]),
                                reads=[vbb], writes=[cstage_b])
                        else:
                            g0 = tiles[0]
                            ph.op("pool", lambda e, vv=vv, j=j, g0=g0, nb=nb: e.tensor_copy(
                                out=vv[:, :, 0:2], in_=vh[:, j, 2 * g0:2 * (g0 + nb)].rearrange("p (s r) -> p s r", r=2)),
                                reads=[vh_b], writes=[vbb])
                            if tiles[-1] == NPT - 1:
                                ph.op("pool", lambda e, vv=vv, j=j, nb=nb: e.tensor_copy(
                                    out=cstage[:, j, 32:34], in_=vv[:, nb - 1, 128:130]),
                                    reads=[vbb], writes=[cstage_b])
                        t1, t1bb = ph.rotbuf("t1b", t1b)
                        t1v = t1[:, 0:n].rearrange("p (b c) -> p b c", c=L)
                        cw0, cw1, cw2 = [convw_t[:, tap * 8 + j:tap * 8 + j + 1] for tap in range(3)]
                        ph.op("dve", lambda e, t1v=t1v, vv=vv, L=L, cw0=cw0: e.tensor_scalar(
                            out=t1v, in0=vv[:, :, 0:L], scalar1=cw0, scalar2=None, op0=ALU.mult),
                            reads=[vbb, const_b], writes=[t1bb])
                        ph.op("dve", lambda e, t1v=t1v, vv=vv, L=L, cw1=cw1: e.scalar_tensor_tensor(
                            out=t1v, in0=vv[:, :, 1:1 + L], scalar=cw1, in1=t1v, op0=ALU.mult, op1=ALU.add),
                            reads=[vbb, const_b, t1bb], writes=[t1bb])
                        ph.op("dve", lambda e, t1v=t1v, vv=vv, L=L, cw2=cw2: e.scalar_tensor_tensor(
                            out=t1v, in0=vv[:, :, 2:2 + L], scalar=cw2, in1=t1v, op0=ALU.mult, op1=ALU.add),
                            reads=[vbb, const_b, t1bb], writes=[t1bb])
                        ph.op("dve", lambda e, t1=t1, pb=pb, j=j, n=n: e.tensor_tensor(
                            out=bcT[:, j, 0:n], in0=pb[:, 0:n], in1=t1[:, 0:n], op=ALU.mult),
                            reads=[pbb, t1bb], writes=[bcT_b])
                for half in range(2):
                    wr, wrb = ph.rotbuf("wres", wres)
                    wr_v = wr[:, :, :].rearrange("p a b -> p (a b)")[:, 0:4096].rearrange("p (j n) -> p j n", j=8)
                    ph.dma("pool", lambda e, wr_v=wr_v, half=half: e.dma_start(
                        out=wr_v, in_=a_out[:, half * 512:(half + 1) * 512].rearrange("(j p) n -> p j n", p=128)),
                        writes=[wrb])
                    for t, ti in enumerate(tiles):
                        bo, bob = mmbank()
                        for j in range(8):
                            ph.op("pe", lambda e, j=j, bo=bo, wr_v=wr_v, t=t: e.matmul(
                                bo[:, :], lhsT=bcT[:, j, t * 128:(t + 1) * 128],
                                rhs=wr_v[:, j, :], start=(j == 0), stop=(j == 7)),
                                reads=[bcT_b, wrb], writes=[bob])
                        ph.op("dve", lambda e, bo=bo, ti=ti, half=half: e.tensor_tensor(
                            out=h[:, ti, half * 512:(half + 1) * 512], in0=bo[:, :],
                            in1=h[:, ti, half * 512:(half + 1) * 512], op=ALU.add),
                            reads=[bob, hb[ti]], writes=[hb[ti]])

            def kvside(tiles):
                sample = tiles[0] == NPT
                n = 128 * len(tiles)
                t0 = tiles[0]
                for t, ti in enumerate(tiles):
                    norm_T(h[:, ti, :], hb[ti], 128, [(xT, xT_b, t * 128, 5), (xT2, xT2_b, t * 128, 2)])
                fm_cols = {0: [0, 128, 256, 384], 1: [0, 128], 2: [0, 128]}
                fm_idx = {0: [0, 1, 2, 3], 1: [4, 5], 2: [6, 7]}
                for c in range(3):
                    wr, wrb = ph.rotbuf("wres", wres)
                    wv = wr[:, :, :].rearrange("p a b -> p (a b)")[:, 0:4096].rearrange("p (k n) -> p k n", k=8)
                    ph.dma("pool", lambda e, wv=wv, c=c: e.dma_start(
                        out=wv, in_=kv_w[:, c * 512:(c + 1) * 512].rearrange("(k p) n -> p k n", p=128)),
                        writes=[wrb])
                    for t, ti in enumerate(tiles):
                        bk, bkb = mmbank()
                        for k in range(8):
                            ph.op("pe", lambda e, k=k, bk=bk, t=t, wv=wv: e.matmul(
                                bk[:, :], lhsT=xT[:, k, t * 128:(t + 1) * 128], rhs=wv[:, k, :],
                                start=(k == 0), stop=(k == 7)), reads=[xT_b, wrb], writes=[bkb])
                        kr, krb = ph.rotbuf("kvrow", kvrow)
                        ph.op("act", lambda e, kr=kr, bk=bk: e.activation(out=kr[:, :], in_=bk[:, :], func=AF.Copy),
                              reads=[bkb], writes=[krb])
                        if c == 0:
                            ph.dma("sp", lambda e, kr=kr, ti=ti: e.dma_start(
                                out=cmp_o[ti * 128:(ti + 1) * 128, :], in_=kr[:, :]), reads=[krb])
                        elif c == 1:
                            ph.dma("sp", lambda e, kr=kr, ti=ti: e.dma_start(
                                out=slc_o[ti * 128:(ti + 1) * 128, :], in_=kr[:, :]), reads=[krb])
                            if sample:
                                ph.op("pool", lambda e, kr=kr: e.tensor_copy(out=skv[:, 0, :], in_=kr[:, :]),
                                      reads=[krb], writes=[skv_b])
                        else:
                            if ti == NPT - 1:
                                ph.dma("sp", lambda e, kr=kr: e.dma_start(out=winp_o[:, :], in_=kr[:, :]), reads=[krb])
                            if sample:
                                ph.op("pool", lambda e, kr=kr: e.tensor_copy(out=skv[:, 1, :], in_=kr[:, :]),
                                      reads=[krb], writes=[skv_b])
                                for s in range(16):
                                    ph.dma("sp", lambda e, kr=kr, s=s: e.dma_start(
                                        out=wins_o[s * 512 + 504:s * 512 + 512, :],
                                        in_=kr[s * 8:(s + 1) * 8, :]), reads=[krb, wins_b], writes=[wins_b])
                        if c >= 1 and not sample:
                            vs, vsb = ph.rotbuf("vst", vst)
                            ph.op("dve", lambda e, vs=vs, bk=bk: e.tensor_copy(out=vs[:, :], in_=bk[:, 256:512]),
                                  reads=[bkb], writes=[vsb])
                            if 'noscr' not in DBG:
                              ph.dma("sp", lambda e, vs=vs, ti=ti, c=c: e.dma_start(
                                out=vtl[c - 1][ti * 128:(ti + 1) * 128, :], in_=vs[:, :]), reads=[vsb])
                    if not sample:
                        for col0, fi in zip(fm_cols[c], fm_idx[c]):
                            bk, bkb = mmbank()
                            for k in range(8):
                                ph.op("pe", lambda e, k=k, bk=bk, col0=col0, wv=wv: e.matmul(
                                    bk[:, 0:n], lhsT=wv[:, k, col0:col0 + 128], rhs=xT[:, k, 0:n],
                                    start=(k == 0), stop=(k == 7)), reads=[xT_b, wrb], writes=[bkb])
                            fs, fsb = ph.rotbuf("fst", fst)
                            ph.op("act", lambda e, fs=fs, bk=bk: e.activation(out=fs[:, 0:n], in_=bk[:, 0:n], func=AF.Copy),
                                  reads=[bkb], writes=[fsb])
                            if 'noscr' not in DBG:
                              ph.dma("sp", lambda e, fs=fs, fi=fi: e.dma_start(
                                out=ftl[fi // 2][fi % 2, :, t0 * 128:t0 * 128 + n], in_=fs[:, 0:n]), reads=[fsb])
                for c in range(3):
                    ncol = 512 if c < 2 else 48
                    wr, wrb = ph.rotbuf("wres", wres)
                    wv = wr[:, :, :].rearrange("p a b -> p (a b)")[:, 0:8 * ncol].rearrange("p (k n) -> p k n", k=8)
                    ph.dma("pool", lambda e, wv=wv, c=c, ncol=ncol: e.dma_start(
                        out=wv, in_=b_in[:, c * 512:c * 512 + ncol].rearrange("(k p) n -> p k n", p=128)),
                        writes=[wrb])
                    if c < 2:
                        for f4 in range(4):
                            fi = c * 4 + f4
                            bk, bkb = mmbank()
                            for k in range(8):
                                ph.op("pe", lambda e, k=k, bk=bk, f4=f4, wv=wv: e.matmul(
                                    bk[:, 0:n], lhsT=wv[:, k, f4 * 128:(f4 + 1) * 128], rhs=xT2[:, k, 0:n],
                                    start=(k == 0), stop=(k == 7)), reads=[xT2_b, wrb], writes=[bkb])
                            fs, fsb = ph.rotbuf("fst", fst)
                            ph.op("act", lambda e, fs=fs, bk=bk: e.activation(out=fs[:, 0:n], in_=bk[:, 0:n],
                                                                             func=AF.Copy, scale=0.125),
                                  reads=[bkb], writes=[fsb])
                            if 'noscr' not in DBG:
                              ph.dma("sp", lambda e, fs=fs, fi=fi: e.dma_start(
                                out=qt_s[fi, :, t0 * 128:t0 * 128 + n], in_=fs[:, 0:n]), reads=[fsb])
                    else:
                        for t, ti in enumerate(tiles):
                            bk, bkb = mmbank()
                            for k in range(8):
                                ph.op("pe", lambda e, k=k, bk=bk, t=t, wv=wv: e.matmul(
                                    bk[:, 0:48], lhsT=xT2[:, k, t * 128:(t + 1) * 128], rhs=wv[:, k, :],
                                    start=(k == 0), stop=(k == 7)), reads=[xT2_b, wrb], writes=[bkb])
                            ph.op("act", lambda e, bk=bk, ti=ti: e.activation(out=gates[:, ti, :], in_=bk[:, 0:48],
                                                                              func=AF.Sigmoid),
                                  reads=[bkb], writes=[gates_b])

            def final_out(tiles):
                for ti in tiles:
                    st, stb = rstd_of(h[:, ti, :], hb[ti], 128)
                    ph.op("dve", lambda e, ti=ti, st=st: e.scalar_tensor_tensor(
                        out=h[:, ti, :], in0=h[:, ti, :], scalar=st[:, 1:2], in1=fnw_t[:, :], op0=ALU.mult, op1=ALU.mult),
                        reads=[hb[ti], stb, const_b], writes=[hb[ti]])
                    ph.dma("sp", lambda e, ti=ti: e.dma_start(out=y_o[ti * 128:(ti + 1) * 128, :], in_=h[:, ti, :]),
                           reads=[hb[ti]])

            if stage == "p1":
                for gi, tiles in enumerate(groups):
                    mixer(gi, tiles, with_halo=(tiles[0] == 0))
                    ffn(0, tiles, 1)
                    kvside(tiles)
                for half in range(2):
                    bank, bb = mmbank()
                    for jj in range(4):
                        j = half * 4 + jj
                        ph.op("pe", lambda e, j=j, jj=jj, bank=bank: e.transpose(
                            out=bank[0:34, jj * 128:(jj + 1) * 128], in_=cstage[:, j, :], identity=identf[:, :]),
                            reads=[cstage_b, ident_b], writes=[bb])
                    ph.op("act", lambda e, half=half, bank=bank: e.activation(
                        out=cout[0:34, half * 512:(half + 1) * 512], in_=bank[0:34, :], func=AF.Copy),
                        reads=[bb], writes=[sconv_b])
                ph.dma("sp", lambda e: e.dma_start(out=conv_o[:, :], in_=cout[0:34, :]), reads=[sconv_b])
            else:
                for gi, tiles in enumerate(groups):
                    ffn(1, tiles, 3)
                    final_out(tiles)
            ph.emit()

        if groups is None:
            groups = [[0, 1, 2, 3], [4, 5, 6, 7], [8, 9, 10, 11], [12, 13, 14, 15], [16]]
        def attn_phase():
          es3 = contextlib.ExitStack()
          with es3:
            sb3 = lambda name, shape, dt=F32: es3.enter_context(nc.sbuf_tensor(f"p3_{name}", list(shape), dt))
            NEGM = -30000.0
            BIGV = 1.0e9
            KE = sb3("KE", [128, 8192], BF16); KE_b = Buf("KE"); KEe_b = Buf("KEe")
            KW = sb3("KW", [128, 8192], BF16); KW_b = Buf("KW")
            VS = sb3("VS", [128, 64, 65], BF16); VS_b = Buf("VS")
            VW = sb3("VW", [128, 64, 65], BF16); VW_b = Buf("VW")
            KCT = sb3("KCT", [64, 512], BF16); KCT_b = Buf("KCT")
            VC = sb3("VC", [128, 4, 65], BF16); VC_b = Buf("VC")
            hidc = sb3("hidc", [64, 512], BF16); hidc_b = Buf("hidc")
            w1a = sb3("w1a", [128, 2, 32, 64], BF16)
            w1b = sb3("w1b", [128, 2, 16, 64], BF16)
            pe_t = sb3("pe_t", [128, 2, 16], BF16)
            w2_t = sb3("w2_t", [64, 2, 64], BF16)
            peb = sb3("peb", [64, 2])
            cw_b = Buf("cmpw")
            wbo = sb3("wbo", [128, 2, D], BF16); wbo_b = Buf("wbo")
            meta_t = sb3("meta_t", [128, 32])
            itmp = sb3("itmp", [128, 128], I32)
            jrow = sb3("jrow", [128, 128])
            T16 = sb3("T16", [128, 128])
            dqp = sb3("dqp", [128, 128])
            tri01 = sb3("tri01", [128, 128])
            atri01 = sb3("atri01", [128, 128])
            j0row = sb3("j0row", [128, 128])
            aggm = sb3("aggm", [128, 4, 128], BF16)
            dmask = sb3("dmask", [128, 4, 128], BF16)
            wmask = sb3("wmask", [128, 8, 128], BF16)
            addm = sb3("addm", [128, NPT, 128])
            k_b = Buf("aconst")
            ta = sb3("ta", [128, 128]); tb = sb3("tb", [128, 128]); tmp_b = Buf("tatb")
            Xm = sb3("Xm", [128, 256], BF16); Xm_b = Buf("Xm")
            QA = [sb3(f"QA{i}", [128, 512], BF16) for i in range(2)]
            QB = [sb3(f"QB{i}", [128, 512], BF16) for i in range(2)]
            PT = [sb3(f"PT{i}", [128, 512], BF16) for i in range(6)]
            cmk = [sb3(f"cmk{i}", [128, 128], BF16) for i in range(2)]
            osb = [sb3(f"osb{i}", [65, 512], BF16) for i in range(2)]
            rl = [sb3(f"rl{i}", [128, 8]) for i in range(3)]
            accs = [(sb3(f"acc{i}", [128, 256]), Buf(f"acc{i}")) for i in range(2)]
            acct = sb3("acct", [128, 256]); acct_b = Buf("acct")
            accbs = [(sb3(f"accb{i}", [128, 256], BF16), Buf(f"accb{i}")) for i in range(2)]
            accT = sb3("accT", [128, 2, 128], BF16); accT_b = Buf("accT")
            score = sb3("score", [128, 128]); score_b = Buf("score")
            stmp = sb3("stmp", [128, 128])
            m8 = sb3("m8", [128, 16])

            def mmbank():
                return ph.rotbuf("mm", mmb[0:3])

            def obank():
                return ph.rotbuf("ob", mmb[3:5])
            rbank, rbank_b = mmb[5], Buf("mmR")

            def trbank():
                return ph.rotbuf("tr", trb)

            ph.dma("sp", lambda e: e.dma_start(out=meta_t[:], in_=meta[:, :]), writes=[k_b])
            def iota_f(dst, pattern, cm, base=0):
                ph.op("pool", lambda e: e.iota(itmp[:], pattern=pattern, base=base, channel_multiplier=cm), writes=[tmp_b])
                ph.op("pool", lambda e: e.tensor_copy(out=dst[:], in_=itmp[:]), reads=[tmp_b], writes=[k_b])
            iota_f(jrow, [[1, 128]], 0)
            iota_f(T16, [[-1, 128]], 16)
            iota_f(dqp, [[1, 128]], -1)
            ph.op("dve", lambda e: e.tensor_scalar(out=tri01[:], in0=dqp[:], scalar1=0.0, scalar2=None, op0=ALU.is_ge),
                  reads=[k_b], writes=[k_b])
            ph.op("dve", lambda e: e.tensor_scalar(out=atri01[:], in0=dqp[:], scalar1=0.0, scalar2=None, op0=ALU.is_le),
                  reads=[k_b], writes=[k_b])
            ph.op("dve", lambda e: e.tensor_scalar(out=j0row[:], in0=jrow[:], scalar1=0.0, scalar2=None, op0=ALU.is_equal),
                  reads=[k_b], writes=[k_b])
            for cidx in range(4):
                iota_f(ta, [[-4, 128]], 1, base=128 * cidx)
                ph.op("dve", lambda e: e.tensor_scalar(out=tb[:], in0=ta[:], scalar1=-1.0, scalar2=None, op0=ALU.is_ge),
                      reads=[k_b], writes=[tmp_b])
                ph.op("dve", lambda e: e.scalar_tensor_tensor(out=tb[:], in0=ta[:], scalar=3.0, in1=tb[:], op0=ALU.is_le,
                                                              op1=ALU.mult), reads=[k_b, tmp_b], writes=[tmp_b])
                ph.op("dve", lambda e: e.tensor_scalar(out=stmp[:], in0=ta[:], scalar1=0.0, scalar2=None, op0=ALU.is_ge),
                      reads=[k_b], writes=[score_b])
                ph.op("dve", lambda e: e.scalar_tensor_tensor(out=stmp[:], in0=ta[:], scalar=2.0, in1=stmp[:], op0=ALU.is_le,
                                                              op1=ALU.mult), reads=[k_b, score_b], writes=[score_b])
                ph.op("dve", lambda e, cidx=cidx: e.tensor_tensor(out=aggm[:, cidx, :], in0=tb[:], in1=stmp[:], op=ALU.add),
                      reads=[tmp_b, score_b], writes=[k_b])
            for j in range(4):
                ph.op("dve", lambda e, j=j: e.tensor_scalar(out=dmask[:, j, :], in0=tri01[:], scalar1=meta_t[:, 2 + j:3 + j],
                                                            scalar2=None, op0=ALU.max), reads=[k_b], writes=[k_b])
            for d in range(8):
                ph.op("dve", lambda e, d=d: e.tensor_scalar(out=ta[:], in0=tri01[:], scalar1=meta_t[:, 14 + d:15 + d],
                                                            scalar2=meta_t[:, 6 + d:7 + d], op0=ALU.mult, op1=ALU.add),
                      reads=[k_b], writes=[tmp_b])
                ph.op("dve", lambda e, d=d: e.scalar_tensor_tensor(out=wmask[:, d, :], in0=atri01[:],
                                                                   scalar=meta_t[:, 22 + d:23 + d], in1=ta[:],
                                                                   op0=ALU.mult, op1=ALU.add),
                      reads=[k_b, tmp_b], writes=[k_b])
            for iq in range(NPT):
                ph.op("dve", lambda e, iq=iq: e.tensor_scalar(out=ta[:], in0=jrow[:], scalar1=meta_t[:, 0:1],
                                                              scalar2=float(8 * iq), op0=ALU.subtract, op1=ALU.subtract),
                      reads=[k_b], writes=[tmp_b])
                ph.op("dve", lambda e: e.tensor_scalar(out=tb[:], in0=ta[:], scalar1=0.0, scalar2=None, op0=ALU.is_le),
                      reads=[tmp_b], writes=[tmp_b])
                ph.op("dve", lambda e: e.scalar_tensor_tensor(out=ta[:], in0=ta[:], scalar=-1.0, in1=tb[:], op0=ALU.is_ge,
                                                              op1=ALU.mult), reads=[tmp_b], writes=[tmp_b])
                ph.op("dve", lambda e: e.tensor_tensor(out=ta[:], in0=ta[:], in1=j0row[:], op=ALU.max),
                      reads=[tmp_b, k_b], writes=[tmp_b])
                ph.op("dve", lambda e: e.tensor_tensor(out=ta[:], in0=ta[:], in1=tb[:], op=ALU.add),
                      reads=[tmp_b], writes=[tmp_b])
                ph.op("dve", lambda e, iq=iq: e.tensor_scalar(out=addm[:, iq, :], in0=ta[:], scalar1=-1.0, scalar2=BIGV,
                                                              op0=ALU.add, op1=ALU.mult), reads=[tmp_b], writes=[k_b])
            ph.op("pool", lambda e: e.memset(KE[64:128, :], 1.0), writes=[KEe_b])
            ph.op("pool", lambda e: e.affine_select(out=KE[64:128, :], in_=KE[64:128, :],
                                                    pattern=[[0, 2], [1, 64], [0, 64]], compare_op=ALU.is_equal,
                                                    fill=0.0, base=0, channel_multiplier=-1),
                  reads=[KEe_b], writes=[KEe_b])
            ph.op("pool", lambda e: e.memset(VS[:, :, 64:65], 1.0), writes=[VS_b])
            ph.op("pool", lambda e: e.memset(VW[:, :, 64:65], 1.0), writes=[VW_b])
            ph.op("pool", lambda e: e.memset(VC[:, :, 64:65], 1.0), writes=[VC_b])
            ph.op("pool", lambda e: e.memset(Xm[:], 0.0), writes=[Xm_b])
            ph.op("pool", lambda e: e.memset(hidc[:, 511:512], 0.0), writes=[hidc_b])
            for k in range(2):
                for c2 in range(2):
                    ph.dma("pool", lambda e, k=k, c2=c2: e.dma_start(
                        out=w1a[c2 * 64:(c2 + 1) * 64, k, :, :], in_=cmp_w1[k].rearrange("j d h -> d j h")), writes=[cw_b])
                    ph.dma("pool", lambda e, k=k, c2=c2: e.dma_start(
                        out=w1b[c2 * 64:(c2 + 1) * 64, k, :, :],
                        in_=cmp_w1[k].rearrange("(jj j2) d h -> j2 d jj h", j2=2)[c2]), writes=[cw_b])
                    ph.dma("pool", lambda e, k=k, c2=c2: e.dma_start(
                        out=pe_t[c2 * 64:(c2 + 1) * 64, k, :],
                        in_=cmp_pe[k].rearrange("(jj j2) d -> j2 d jj", j2=2)[c2], allow_slow_non_contiguous=True),
                        writes=[cw_b])
                ph.dma("pool", lambda e, k=k: e.dma_start(out=w2_t[:, k, :], in_=cmp_w2[k]), writes=[cw_b])
            for k in range(2):
                bk, bkb = mmbank()
                for jj in range(16):
                    ph.op("pe", lambda e, k=k, jj=jj, bk=bk: e.matmul(
                        bk[0:64, 0:1], lhsT=w1b[:, k, jj, :], rhs=pe_t[:, k, jj:jj + 1], start=(jj == 0), stop=(jj == 15)),
                        reads=[cw_b], writes=[bkb])
                ph.op("act", lambda e, k=k, bk=bk: e.activation(out=peb[:, k:k + 1], in_=bk[0:64, 0:1], func=AF.Copy),
                      reads=[bkb], writes=[cw_b])

            def load_ft(dst, dst_b, fc, p0, np_, dp0):
                for r in range(4):
                    ph.dma("sp", lambda e, r=r: e.dma_start(
                        out=dst[dp0:dp0 + np_, :].rearrange("p (i r c) -> p r i c", r=4, c=128)[:, r],
                        in_=gftl[fc // 2][r, fc % 2, p0:p0 + np_, :].rearrange("p (i c) -> p i c", c=128)), writes=[dst_b])

            def load_v(dst, dst_b, which, g):
                for r in range(4):
                    ph.dma("sp", lambda e, r=r: e.dma_start(
                        out=dst[:, :, 0:64].rearrange("p (i r) d -> p r i d", r=4)[:, r],
                        in_=gvtl[which][r].rearrange("(i p) d -> p i d", p=128)[:, :, g * 64:(g + 1) * 64]),
                        writes=[dst_b])

            def finish(ob, obb, branch, g, iq, first, last=False):
                acc, acc_b = accs[iq % 2]
                accb, accb_b = accbs[iq % 2]
                os_, osb_b = ph.rotbuf("osb", osb)
                ph.op("act", lambda e: e.activation(out=os_[0:65, :], in_=ob[0:65, :], func=AF.Copy),
                      reads=[obb], writes=[osb_b])
                tb_, tbb = trbank()
                for r4 in range(4):
                    ph.op("pe", lambda e, r4=r4: e.transpose(out=tb_[:, r4 * 66:r4 * 66 + 65],
                                                             in_=os_[0:65, r4 * 128:(r4 + 1) * 128],
                                                             identity=ident[0:65, 0:65]),
                          reads=[osb_b, ident_b], writes=[tbb])
                tv = tb_[:, 0:264].rearrange("p (r c) -> p r c", c=66)
                r_, rb = ph.rotbuf("rl", rl)
                ph.op("dve", lambda e: e.tensor_scalar(out=r_[:, 0:4], in0=tv[:, :, 64], scalar1=1e-30, scalar2=None,
                                                       op0=ALU.max), reads=[tbb], writes=[rb])
                ph.op("dve", lambda e: e.reciprocal(out=r_[:, 0:4], in_=r_[:, 0:4]), reads=[rb], writes=[rb])
                gv = gates[:, iq, g * 12:(g + 1) * 12].rearrange("p (r b) -> p r b", b=3)[:, :, branch]
                ph.op("dve", lambda e: e.tensor_tensor(out=r_[:, 4:8], in0=r_[:, 0:4], in1=gv, op=ALU.mult),
                      reads=[rb, gates_b], writes=[rb])
                glb = r_[:, 4:8].unsqueeze(2).to_broadcast([128, 4, 64])
                if first:
                    ph.op("dve", lambda e: e.tensor_tensor(out=acc[:, :].rearrange("p (r d) -> p r d", d=64),
                                                           in0=tv[:, :, 0:64], in1=glb, op=ALU.mult),
                          reads=[tbb, rb], writes=[acc_b])
                else:
                    ph.op("dve", lambda e: e.tensor_tensor(out=acct[:, :].rearrange("p (r d) -> p r d", d=64),
                                                           in0=tv[:, :, 0:64], in1=glb, op=ALU.mult),
                          reads=[tbb, rb], writes=[acct_b])
                    if last:
                        ph.op("dve", lambda e: e.tensor_tensor(out=accb[:, :], in0=acc[:, :], in1=acct[:, :], op=ALU.add),
                              reads=[acct_b, acc_b], writes=[accb_b])
                    else:
                        ph.op("dve", lambda e: e.tensor_tensor(out=acc[:, :], in0=acc[:, :], in1=acct[:, :], op=ALU.add),
                              reads=[acct_b, acc_b], writes=[acc_b])
                return r_, rb

            def outproj(g, iq):
                accb, accb_b = accbs[iq % 2]
                trk, trkb = trbank()
                for a in range(2):
                    ph.op("pe", lambda e, a=a, trk=trk: e.transpose(out=trk[:, a * 128:(a + 1) * 128],
                                                                   in_=accb[:, a * 128:(a + 1) * 128], identity=ident[:, :]),
                          reads=[accb_b, ident_b], writes=[trkb])
                ph.op("act", lambda e, trk=trk: e.activation(out=accT[:, :, :],
                                                             in_=trk[:, 0:256].rearrange("p (a q) -> p a q", q=128),
                                                             func=AF.Copy), reads=[trkb], writes=[accT_b])
                for half in range(2):
                    bo, bob = mmbank()
                    for a in range(2):
                        ph.op("pe", lambda e, a=a, bo=bo, half=half: e.matmul(
                            bo[:, :], lhsT=accT[:, a, :], rhs=wbo[:, a, half * 512:(half + 1) * 512],
                            start=(a == 0), stop=(a == 1)), reads=[accT_b, wbo_b], writes=[bob])
                    ph.op("dve", lambda e, bo=bo, half=half, iq=iq: e.tensor_tensor(
                        out=h[:, iq, half * 512:(half + 1) * 512], in0=bo[:, :],
                        in1=h[:, iq, half * 512:(half + 1) * 512], op=ALU.add),
                        reads=[bob, hb[iq]], writes=[hb[iq]])

            prev_c1 = prev_c2 = None
            for g in range(4):
                po = (g % 2) * 64
                for k in range(2):
                    load_ft(KW, KW_b, k * 2 + g // 2, 0, 128, 0)
                    bk, bkb = mmbank()
                    for j in range(32):
                        ph.op("pe", lambda e, k=k, j=j, bk=bk, po=po: e.matmul(
                            bk[0:64, 0:511], lhsT=w1a[po:po + 64, k, j, :], rhs=KW[po:po + 64, j:j + 8161:16],
                            start=(j == 0), stop=(j == 31)), reads=[cw_b, KW_b], writes=[bkb])
                    ph.op("act", lambda e, k=k, bk=bk: e.activation(out=hidc[:, 0:511], in_=bk[0:64, 0:511], func=AF.Silu,
                                                                    bias=peb[:, k:k + 1]),
                          reads=[bkb, cw_b], writes=[hidc_b])
                    if k == 0:
                        b2, b2b = mmbank()
                        ph.op("pe", lambda e, b2=b2: e.matmul(b2[0:64, 0:511], lhsT=w2_t[:, 0, :], rhs=hidc[:, 0:511],
                                                              start=True, stop=True), reads=[cw_b, hidc_b], writes=[b2b])
                        ph.op("act", lambda e, b2=b2: e.activation(out=KCT[:, 0:511], in_=b2[0:64, 0:511], func=AF.Copy),
                              reads=[b2b], writes=[KCT_b])
                    else:
                        b2, b2b = mmbank()
                        for cidx in range(4):
                            M = 128
                            ph.op("pe", lambda e, b2=b2, cidx=cidx, M=M: e.matmul(
                                b2[0:M, cidx * 64:(cidx + 1) * 64], lhsT=hidc[:, cidx * 128:cidx * 128 + M],
                                rhs=w2_t[:, 1, :], start=True, stop=True), reads=[cw_b, hidc_b], writes=[b2b])
                        ph.op("act", lambda e, b2=b2: e.activation(
                            out=VC[:, :, 0:64], in_=b2[:, 0:256].rearrange("p (c d) -> p c d", d=64), func=AF.Copy),
                            reads=[b2b], writes=[VC_b])
                load_ft(KE, KE_b, 4 + g // 2, po, 64, 0)
                load_ft(KW, KW_b, 6 + g // 2, po, 64, 0)
                load_v(VS, VS_b, 0, g)
                load_v(VW, VW_b, 1, g)
                ph.dma("pool", lambda e, g=g: e.dma_start(
                    out=wbo[:, :, :], in_=b_out[g * 256:(g + 1) * 256, :].rearrange("(c p) n -> p c n", p=128)),
                    writes=[wbo_b])
                for iq in range(NPT):
                    qa, qab = ph.rotbuf("QA", QA)
                    qb, qbb = ph.rotbuf("QB", QB)
                    useB = iq >= 8
                    for r4 in range(4):
                        fcq = 2 * g + r4 // 2
                        pq = (r4 % 2) * 64
                        ph.dma("sp", lambda e, r4=r4, fcq=fcq, pq=pq, qa=qa, iq=iq: e.dma_start(
                            out=qa[0:64, r4 * 128:(r4 + 1) * 128], in_=qt_s[fcq, pq:pq + 64, iq * 128:(iq + 1) * 128]),
                            writes=[qab])
                        if useB:
                            ph.dma("sp", lambda e, r4=r4, fcq=fcq, pq=pq, qb=qb, iq=iq: e.dma_start(
                                out=qb[0:64, r4 * 128:(r4 + 1) * 128], in_=qt_s[fcq, pq:pq + 64, iq * 128:(iq + 1) * 128]),
                                writes=[qbb])
                    nch = (32 * iq + 30) // 128 + 1
                    oc, ocb = obank()
                    for cidx in range(nch):
                        M = 127 if cidx == 3 else 128
                        sbk, sbb = mmbank()
                        ph.op("pe", lambda e, sbk=sbk, cidx=cidx, M=M, qa=qa: e.matmul(
                            sbk[0:M, :], lhsT=KCT[:, cidx * 128:cidx * 128 + M], rhs=qa[0:64, :], start=True, stop=True),
                            reads=[KCT_b, qab], writes=[sbb])
                        pt, ptb = ph.rotbuf("PT", PT)
                        ph.op("act", lambda e, sbk=sbk, pt=pt, M=M: e.activation(out=pt[0:M, :], in_=sbk[0:M, :], func=AF.Exp),
                              reads=[sbb], writes=[ptb])
                        if cidx >= nch - 2:
                            cm, cmb = ph.rotbuf("cmk", cmk)
                            cst = float(31 + 2048 * cidx - 512 * iq)
                            ph.op("dve", lambda e, cm=cm, cst=cst: e.tensor_scalar(
                                out=stmp[:], in0=T16[:], scalar1=meta_t[:, 1:2], scalar2=cst, op0=ALU.subtract, op1=ALU.add),
                                reads=[k_b], writes=[score_b])
                            ph.op("dve", lambda e, cm=cm: e.tensor_scalar(out=cm[:], in0=stmp[:], scalar1=0.0, scalar2=None,
                                                                         op0=ALU.is_le), reads=[score_b], writes=[cmb])
                            ph.op("dve", lambda e, cm=cm, pt=pt, M=M: e.tensor_tensor(
                                out=pt[0:M, :].rearrange("p (r q) -> p r q", q=128),
                                in0=pt[0:M, :].rearrange("p (r q) -> p r q", q=128),
                                in1=cm[0:M, :].unsqueeze(1).to_broadcast([M, 4, 128]), op=ALU.mult),
                                reads=[cmb, ptb], writes=[ptb])
                        ph.op("pe", lambda e, oc=oc, cidx=cidx, M=M, pt=pt, nch=nch: e.matmul(
                            oc[0:65, :], lhsT=VC[0:M, cidx, :], rhs=pt[0:M, :], start=(cidx == 0), stop=(cidx == nch - 1)),
                            reads=[VC_b, ptb], writes=[ocb])
                        for r4 in range(4):
                            ph.op("pe", lambda e, cidx=cidx, M=M, pt=pt, r4=r4, nch=nch: e.matmul(
                                rbank[:, r4 * 128:(r4 + 1) * 128], lhsT=pt[0:M, r4 * 128:(r4 + 1) * 128],
                                rhs=aggm[0:M, cidx, :], start=(cidx == 0 and r4 == 0), stop=(cidx == nch - 1 and r4 == 3),
                                skip_group_check=True), reads=[ptb, k_b], writes=[rbank_b])
                    rc, rcb = finish(oc, ocb, 0, g, iq, True)
                    if prev_c1 is not None:
                        prev_c1()
                    ow, owb = obank()
                    cs = [c for c in range(4 * iq - 4, 4 * iq + 4) if c >= 0]
                    pend = []

                    def win_pv(item, ow=ow, owb=owb, cs=cs):
                        c, pt, ptb = item
                        ph.op("pe", lambda e, ow=ow, c=c, pt=pt, cs=cs: e.matmul(
                            ow[0:65, :], lhsT=VW[:, c, :], rhs=pt[:, :], start=(c == cs[0]), stop=(c == cs[-1])),
                            reads=[VW_b, ptb], writes=[owb])
                    for c in cs:
                        dd = c - (4 * iq - 4)
                        sbk, sbb = mmbank()
                        ph.op("pe", lambda e, sbk=sbk, c=c, qa=qa: e.matmul(
                            sbk[:, :], lhsT=KW[0:64, c * 128:(c + 1) * 128], rhs=qa[0:64, :], start=True, stop=True),
                            reads=[KW_b, qab], writes=[sbb])
                        pt, ptb = ph.rotbuf("PT", PT)
                        ph.op("act", lambda e, sbk=sbk, pt=pt: e.activation(out=pt[:, :], in_=sbk[:, :], func=AF.Exp),
                              reads=[sbb], writes=[ptb])
                        ph.op("dve", lambda e, pt=pt, dd=dd: e.tensor_tensor(
                            out=pt[:, :].rearrange("p (r q) -> p r q", q=128),
                            in0=pt[:, :].rearrange("p (r q) -> p r q", q=128),
                            in1=wmask[:, dd, :].unsqueeze(1).to_broadcast([128, 4, 128]), op=ALU.mult),
                            reads=[k_b, ptb], writes=[ptb])
                        pend.append((c, pt, ptb))
                        if len(pend) > 2:
                            win_pv(pend.pop(0))
                    while pend:
                        win_pv(pend.pop(0))
                    ph.op("dve", lambda e, rc=rc: e.tensor_scalar(out=score[:], in0=rbank[:, 0:128], scalar1=rc[:, 0:1],
                                                                  scalar2=None, op0=ALU.mult),
                          reads=[rbank_b, rcb], writes=[score_b])
                    for r4 in range(1, 4):
                        ph.op("dve", lambda e, rc=rc, r4=r4: e.scalar_tensor_tensor(
                            out=score[:], in0=rbank[:, r4 * 128:(r4 + 1) * 128], scalar=rc[:, r4:r4 + 1], in1=score[:],
                            op0=ALU.mult, op1=ALU.add), reads=[rbank_b, rcb, score_b], writes=[score_b])
                    ph.op("dve", lambda e, iq=iq: e.tensor_tensor(out=score[:], in0=score[:], in1=addm[:, iq, :], op=ALU.add),
                          reads=[score_b, k_b], writes=[score_b])
                    ph.op("dve", lambda e: e.max(out=m8[:, 0:8], in_=score[:]), reads=[score_b], writes=[score_b])
                    ph.op("dve", lambda e: e.match_replace(out=stmp[:], in_to_replace=m8[:, 0:8], in_values=score[:],
                                                           imm_value=-3.0e9), reads=[score_b], writes=[score_b])
                    ph.op("dve", lambda e: e.max(out=m8[:, 8:16], in_=stmp[:]), reads=[score_b], writes=[score_b])
                    ph.op("dve", lambda e: e.tensor_scalar(out=m8[:, 15:16], in0=m8[:, 15:16], scalar1=-1.0e8, scalar2=None,
                                                           op0=ALU.max), reads=[score_b], writes=[score_b])
                    ph.op("dve", lambda e: e.tensor_scalar(
                        out=Xm[:, :].rearrange("p (a b) -> p a b", b=128)[:, :, 64:128],
                        in0=score[:, :].rearrange("p (a b) -> p a b", b=64), scalar1=m8[:, 15:16], scalar2=NEGM,
                        op0=ALU.is_lt, op1=ALU.mult), reads=[score_b], writes=[Xm_b])
                    trk, trkb = trbank()
                    for a in range(2):
                        ph.op("pe", lambda e, a=a, trk=trk: e.transpose(out=trk[:, a * 128:(a + 1) * 128],
                                                                       in_=Xm[:, a * 128:(a + 1) * 128], identity=ident[:, :]),
                              reads=[Xm_b, ident_b], writes=[trkb])
                    ph.op("act", lambda e, trk=trk, qa=qa: e.activation(
                        out=qa[64:128, :].rearrange("p (r q) -> p r q", q=128),
                        in_=trk[64:128, 0:128].unsqueeze(1).to_broadcast([64, 4, 128]), func=AF.Copy),
                        reads=[trkb], writes=[qab])
                    if useB:
                        ph.op("act", lambda e, trk=trk, qb=qb: e.activation(
                            out=qb[64:128, :].rearrange("p (r q) -> p r q", q=128),
                            in_=trk[64:128, 128:256].unsqueeze(1).to_broadcast([64, 4, 128]), func=AF.Copy),
                            reads=[trkb], writes=[qbb])
                    finish(ow, owb, 2, g, iq, False)
                    if prev_c2 is not None:
                        prev_c2()
                    osl, oslb = obank()
                    nsel = 4 * iq + 4
                    pend = []

                    def sel_pv(item, osl=osl, oslb=oslb, nsel=nsel):
                        c, pt, ptb = item
                        ph.op("pe", lambda e, osl=osl, c=c, pt=pt, nsel=nsel: e.matmul(
                            osl[0:65, :], lhsT=VS[:, c, :], rhs=pt[:, :], start=(c == 0), stop=(c == nsel - 1)),
                            reads=[VS_b, ptb], writes=[oslb])
                    for c in range(nsel):
                        sbk, sbb = mmbank()
                        qq, qqb = (qa, qab) if c < 32 else (qb, qbb)
                        ph.op("pe", lambda e, sbk=sbk, c=c, qq=qq: e.matmul(
                            sbk[:, :], lhsT=KE[:, c * 128:(c + 1) * 128], rhs=qq[:, :], start=True, stop=True),
                            reads=[KE_b, KEe_b, qqb], writes=[sbb])
                        pt, ptb = ph.rotbuf("PT", PT)
                        ph.op("act", lambda e, sbk=sbk, pt=pt: e.activation(out=pt[:, :], in_=sbk[:, :], func=AF.Exp),
                              reads=[sbb], writes=[ptb])
                        if c >= 4 * iq:
                            jd = c - 4 * iq
                            ph.op("dve", lambda e, pt=pt, jd=jd: e.tensor_tensor(
                                out=pt[:, :].rearrange("p (r q) -> p r q", q=128),
                                in0=pt[:, :].rearrange("p (r q) -> p r q", q=128),
                                in1=dmask[:, jd, :].unsqueeze(1).to_broadcast([128, 4, 128]), op=ALU.mult),
                                reads=[k_b, ptb], writes=[ptb])
                        pend.append((c, pt, ptb))
                        if len(pend) > 2:
                            sel_pv(pend.pop(0))
                    while pend:
                        sel_pv(pend.pop(0))
                    prev_c1 = (lambda osl=osl, oslb=oslb, g=g, iq=iq: finish(osl, oslb, 1, g, iq, False, last=True))
                    prev_c2 = (lambda g=g, iq=iq: outproj(g, iq))
                prev_c1()
                prev_c2()
                prev_c1 = prev_c2 = None
            ph.emit()

        def sample_phase():
          es4 = contextlib.ExitStack()
          with es4:
            sb4 = lambda name, shape, dt=F32: es4.enter_context(nc.sbuf_tensor(f"p4_{name}", list(shape), dt))
            NEGM = -30000.0
            BIGV = 1.0e9
            ptb_i = sb4("ptb_i", [128, 256], I32)
            ptb_f = sb4("ptb_f", [128, 256])
            idx_i = sb4("idx_i", [128, 256], I32)
            itmp = sb4("itmp", [128, 128], I32)
            pcol = sb4("pcol", [128, 1])
            k_b = Buf("sconst")
            ta = sb4("ta", [128, 128]); tb = sb4("tb", [128, 128]); tmp_b = Buf("stmp")
            aggS = sb4("aggS", [128, 34], BF16)
            addmS = sb4("addmS", [8, 33])
            sel4 = sb4("sel4", [32, 8])
            maskN = sb4("maskN", [128, 16, 32], BF16)
            maskW0 = sb4("maskW0", [128, 32], BF16)
            w1bd = sb4("w1bd", [128, 2, 32, 128], BF16)
            pe_t = sb4("pe_t", [128, 2, 32], BF16)
            w2_t = sb4("w2_t", [128, 2, 64], BF16)
            peb = sb4("peb", [128, 2])
            cw_b = Buf("scmpw")
            wbo = sb4("wbo", [128, 2, D], BF16); wbo_b = Buf("swbo")
            QG = sb4("QG", [64, 4, 4, 128], BF16); QG_b = Buf("QG")
            skvb, skvb_b = skv, skv_b
            KnT = sb4("KnT", [64, 2, 4, 128], BF16); KnT_b = Buf("KnT")
            VnA = sb4("VnA", [128, 2, 4, 65], BF16); VnA_b = Buf("VnA")
            KsE = sb4("KsE", [128, 4, 2048], BF16); KsE_b = Buf("KsE"); KsEe_b = Buf("KsEe")
            VsA = sb4("VsA", [128, 16, 4, 65], BF16); VsA_b = Buf("VsA")
            CX = sb4("CX", [128, 4, 2048], BF16); CX_b = Buf("CX")
            KwT = sb4("KwT", [64, 4, 512], BF16); KwT_b = Buf("KwT")
            VwA = sb4("VwA", [128, 4, 4, 65], BF16); VwA_b = Buf("VwA")
            wsb = sb4("wsb", [128, 4, 512], BF16); wsb_b = Buf("wsb")
            KCs = sb4("KCs", [64, 4, 128], BF16); KCs_b = Buf("KCs")
            VCs = sb4("VCs", [128, 4, 65], BF16); VCs_b = Buf("VCs")
            hidc = sb4("hidc", [128, 128], BF16); hidc_b = Buf("shidc")
            pgs = [sb4(f"pg{i}", [128, 512], BF16) for i in range(4)]
            QSa = [sb4(f"QSa{i}", [128, 32], BF16) for i in range(2)]
            PTs = [sb4(f"PTs{i}", [128, 512], BF16) for i in range(2)]
            PTw = [sb4(f"PTw{i}", [128, 128], BF16) for i in range(2)]
            PTc = [sb4(f"PTc{i}", [128, 32], BF16) for i in range(2)]
            pnA = sb4("pnA", [128, 2, 4, 512], BF16); pnA_b = Buf("pnA")
            Rn = sb4("Rn", [32, 34]); Rn_b = Buf("Rn")
            scs = sb4("scs", [8, 128]); scs_b = Buf("scs")
            m8 = sb4("sm8", [8, 16])
            Xs = sb4("Xs", [8, 128], BF16); Xs_b = Buf("Xs")
            OTall = sb4("OTall", [65, 3, 4, 512], BF16); OT_b = Buf("OTall")
            rl = [sb4(f"rl{i}", [128, 8]) for i in range(3)]
            acc = sb4("acc", [128, 256]); acc_b = Buf("sacc")
            acct = sb4("acct", [128, 256]); acct_b = Buf("sacct")
            accb = sb4("accb", [128, 256], BF16)
            accT = sb4("accT", [128, 2, 128], BF16); accT_b = Buf("saccT")

            def mmbank():
                return ph.rotbuf("mm", mmb[0:3])

            def obank():
                return ph.rotbuf("ob", mmb[3:5])
            rbank, rbank_b = mmb[5], Buf("mmR")

            def trbank():
                return ph.rotbuf("tr", trb)

            def iota_f(dst_ap, pattern, cm, base=0, shape=None):
                iv = itmp[:] if shape is None else itmp[0:shape[0], 0:shape[1]]
                ph.op("pool", lambda e: e.iota(iv, pattern=pattern, base=base, channel_multiplier=cm), writes=[tmp_b])
                ph.op("pool", lambda e: e.tensor_copy(out=dst_ap, in_=iv), reads=[tmp_b], writes=[k_b])

            iota_f(pcol[:, :], [[0, 1]], 1, shape=(128, 1))
            ph.dma("sp", lambda e: e.dma_start(out=ptb_i[:], in_=ptab.partition_broadcast(128)), writes=[k_b])
            ph.op("pool", lambda e: e.tensor_copy(out=ptb_f[:], in_=ptb_i[:]), reads=[k_b], writes=[k_b])
            ph.op("dve", lambda e: e.tensor_scalar(out=ptb_f[:], in0=ptb_f[:], scalar1=128.0, scalar2=pcol[:, 0:1],
                                                   op0=ALU.mult, op1=ALU.add), reads=[k_b], writes=[k_b])
            ph.op("pool", lambda e: e.tensor_copy(out=idx_i[:], in_=ptb_f[:]), reads=[k_b], writes=[k_b])
            iota_f(ta[:, 0:33], [[-4, 33]], 1, shape=(128, 33))
            ph.op("dve", lambda e: e.tensor_scalar(out=tb[:, 0:33], in0=ta[:, 0:33], scalar1=-1.0, scalar2=None, op0=ALU.is_ge),
                  reads=[k_b], writes=[tmp_b])
            ph.op("dve", lambda e: e.scalar_tensor_tensor(out=tb[:, 0:33], in0=ta[:, 0:33], scalar=3.0, in1=tb[:, 0:33],
                                                          op0=ALU.is_le, op1=ALU.mult), reads=[k_b, tmp_b], writes=[tmp_b])
            ph.op("dve", lambda e: e.tensor_scalar(out=tb[:, 64:97], in0=ta[:, 0:33], scalar1=0.0, scalar2=None, op0=ALU.is_ge),
                  reads=[k_b], writes=[tmp_b])
            ph.op("dve", lambda e: e.scalar_tensor_tensor(out=tb[:, 64:97], in0=ta[:, 0:33], scalar=2.0, in1=tb[:, 64:97],
                                                          op0=ALU.is_le, op1=ALU.mult), reads=[k_b, tmp_b], writes=[tmp_b])
            ph.op("dve", lambda e: e.tensor_tensor(out=aggS[:, 0:33], in0=tb[:, 0:33], in1=tb[:, 64:97], op=ALU.add),
                  reads=[tmp_b], writes=[k_b])
            ph.op("pool", lambda e: e.memset(aggS[:, 33:34], 1.0), writes=[k_b])
            iota_f(ta[0:8, 0:33], [[1, 33]], 0, shape=(8, 33))
            ph.op("dve", lambda e: e.tensor_scalar(out=tb[0:8, 0:33], in0=ta[0:8, 0:33], scalar1=0.0, scalar2=None,
                                                   op0=ALU.is_equal), reads=[k_b], writes=[tmp_b])
            ph.op("dve", lambda e: e.scalar_tensor_tensor(out=tb[0:8, 0:33], in0=ta[0:8, 0:33], scalar=31.0, in1=tb[0:8, 0:33],
                                                          op0=ALU.is_ge, op1=ALU.add), reads=[k_b, tmp_b], writes=[tmp_b])
            ph.op("dve", lambda e: e.tensor_scalar(out=addmS[:, :], in0=tb[0:8, 0:33], scalar1=BIGV, scalar2=None, op0=ALU.mult),
                  reads=[tmp_b], writes=[k_b])
            iota_f(ta[0:32, 0:8], [[-1, 8]], 1, shape=(32, 8))
            ph.op("dve", lambda e: e.tensor_scalar(out=sel4[:, :], in0=ta[0:32, 0:8], scalar1=0.0, scalar2=None, op0=ALU.is_equal),
                  reads=[k_b], writes=[k_b])
            for r4 in range(1, 4):
                ph.op("dve", lambda e, r4=r4: e.scalar_tensor_tensor(out=sel4[:, :], in0=ta[0:32, 0:8], scalar=float(8 * r4),
                                                                     in1=sel4[:, :], op0=ALU.is_equal, op1=ALU.add),
                      reads=[k_b], writes=[k_b])
            iota_f(ta[:, :], [[8, 16], [1, 8]], -1)
            ph.op("dve", lambda e: e.tensor_scalar(out=ta[:], in0=ta[:], scalar1=0.0, scalar2=None, op0=ALU.is_ge),
                  reads=[k_b], writes=[k_b])
            iota_f(tb[:, :], [[8, 16], [0, 8]], -1)
            ph.op("dve", lambda e: e.scalar_tensor_tensor(out=tb[:], in0=tb[:], scalar=0.0, in1=ta[:], op0=ALU.is_le,
                                                          op1=ALU.mult), reads=[k_b], writes=[k_b])
            ph.op("dve", lambda e: e.tensor_copy(
                out=maskN[:, :, :].rearrange("p s (r t) -> p s r t", t=8),
                in_=tb[:, :].rearrange("p (s t) -> p s t", t=8).unsqueeze(2).to_broadcast([128, 16, 4, 8])),
                reads=[k_b], writes=[k_b])
            iota_f(ta[:, 0:32], [[0, 4], [-1, 8]], 1, shape=(128, 32))
            ph.op("dve", lambda e: e.tensor_scalar(out=maskW0[:, :], in0=ta[:, 0:32], scalar1=0.0, scalar2=None, op0=ALU.is_ge),
                  reads=[k_b], writes=[k_b])
            ph.op("pool", lambda e: e.memset(KsE[64:128, :, :], 1.0), writes=[KsEe_b])
            ph.op("pool", lambda e: e.affine_select(out=KsE[64:128, :, :], in_=KsE[64:128, :, :],
                                                    pattern=[[0, 4], [1, 32], [0, 64]], compare_op=ALU.is_equal,
                                                    fill=0.0, base=0, channel_multiplier=-1),
                  reads=[KsEe_b], writes=[KsEe_b])
            ph.op("pool", lambda e: e.memset(VsA[:, :, :, 64:65], 1.0), writes=[VsA_b])
            ph.op("pool", lambda e: e.memset(VwA[:, :, :, 64:65], 1.0), writes=[VwA_b])
            ph.op("pool", lambda e: e.memset(VCs[:, :, 64:65], 1.0), writes=[VCs_b])
            ph.op("pool", lambda e: e.memset(VnA[:, :, :, 64:65], 1.0), writes=[VnA_b])
            ph.op("pool", lambda e: e.memset(Xs[:], 0.0), writes=[Xs_b])
            ph.op("pool", lambda e: e.memset(w1bd[:, :, :, :], 0.0), writes=[cw_b])
            for k in range(2):
                for c2 in range(2):
                    ph.dma("pool", lambda e, k=k, c2=c2: e.dma_start(
                        out=w1bd[c2 * 64:(c2 + 1) * 64, k, :, c2 * 64:(c2 + 1) * 64],
                        in_=cmp_w1[k].rearrange("j d h -> d j h")), reads=[cw_b], writes=[cw_b])
                    ph.dma("pool", lambda e, k=k, c2=c2: e.dma_start(
                        out=pe_t[c2 * 64:(c2 + 1) * 64, k, :], in_=cmp_pe[k].rearrange("j d -> d j"),
                        allow_slow_non_contiguous=True), writes=[cw_b])
                    ph.dma("pool", lambda e, k=k, c2=c2: e.dma_start(out=w2_t[c2 * 64:(c2 + 1) * 64, k, :], in_=cmp_w2[k]),
                           writes=[cw_b])
            for k in range(2):
                bk, bkb = mmbank()
                for j in range(32):
                    ph.op("pe", lambda e, k=k, j=j, bk=bk: e.matmul(
                        bk[:, 0:1], lhsT=w1bd[:, k, j, :], rhs=pe_t[:, k, j:j + 1], start=(j == 0), stop=(j == 31)),
                        reads=[cw_b], writes=[bkb])
                ph.op("act", lambda e, k=k, bk=bk: e.activation(out=peb[:, k:k + 1], in_=bk[:, 0:1], func=AF.Copy),
                      reads=[bkb], writes=[cw_b])
            for r4 in range(4):
                hh = r4 % 2
                ph.dma("sp", lambda e, r4=r4, hh=hh: e.dma_start(
                    out=QG[:, r4, :, :],
                    in_=qt_s.rearrange("(g f) p c -> p g f c", f=2)[hh * 64:(hh + 1) * 64, :, r4 // 2, NPT * 128:NT * 128]),
                    writes=[QG_b])
            for w in range(2):
                trk, trkb = trbank()
                for g in range(4):
                    ph.op("pe", lambda e, w=w, g=g, trk=trk: e.transpose(out=trk[0:64, g * 128:(g + 1) * 128],
                                                                         in_=skvb[:, w, g * 64:(g + 1) * 64], identity=ident[:, :]),
                          reads=[skvb_b, ident_b], writes=[trkb])
                ph.op("act", lambda e, w=w, trk=trk: e.activation(
                    out=KnT[:, w, :, :], in_=trk[0:64, 0:512].rearrange("p (g c) -> p g c", c=128), func=AF.Copy),
                    reads=[trkb], writes=[KnT_b])
                ph.op("dve", lambda e, w=w: e.tensor_copy(out=VnA[:, w, :, 0:64],
                                                          in_=skvb[:, w, 256:512].rearrange("p (g d) -> p g d", d=64)),
                      reads=[skvb_b], writes=[VnA_b])

            maskN2 = maskN[:, :, :].rearrange("p s (r t) -> p s r t", t=8)[:, :, 0, :]
            for g in range(4):
                for w in range(2):
                    sn, snb = mmbank()
                    ph.op("pe", lambda e, sn=sn, w=w, g=g: e.matmul(
                        sn[:, :].rearrange("p (r c) -> p r c", c=128), lhsT=KnT[:, w, g, :], rhs=QG[:, :, g, :],
                        start=True, stop=True), reads=[KnT_b, QG_b], writes=[snb])
                    ph.op("act", lambda e, sn=sn, w=w, g=g: e.activation(out=pnA[:, w, g, :], in_=sn[:, :], func=AF.Exp),
                          reads=[snb], writes=[pnA_b])
                    ph.op("dve", lambda e, w=w, g=g: e.tensor_tensor(
                        out=pnA[:, w, g, :].rearrange("p (r s t) -> p r s t", r=4, t=8),
                        in0=pnA[:, w, g, :].rearrange("p (r s t) -> p r s t", r=4, t=8),
                        in1=maskN2.unsqueeze(1).to_broadcast([128, 4, 16, 8]), op=ALU.mult),
                        reads=[pnA_b, k_b], writes=[pnA_b])

            def gather(pool_ap, col):
                pg, pgb = ph.rotbuf("pg", pgs)
                ph.dma("pool", lambda e: e.indirect_dma_start(
                    out=pg[:, :], out_offset=None, in_=pool_ap[:, :],
                    in_offset=bass.IndirectOffsetOnAxis(ap=idx_i[:, col:col + 1], axis=0)), reads=[k_b], writes=[pgb])
                return pg, pgb

            for s in range(16):
                for jp in range(8):
                    trk, trkb = trbank()
                    for a in range(2):
                        j = 2 * jp + a
                        pg, pgb = gather(pslc, s * 16 + j)
                        for g in range(4):
                            ph.op("pe", lambda e, a=a, g=g, trk=trk, pg=pg: e.transpose(
                                out=trk[0:64, (a * 4 + g) * 128:(a * 4 + g + 1) * 128], in_=pg[:, g * 64:(g + 1) * 64],
                                identity=ident[:, :]), reads=[pgb, ident_b], writes=[trkb])
                        ph.op("pool", lambda e, j=j, pg=pg: e.tensor_copy(
                            out=VsA[:, j, :, 0:64], in_=pg[:, 256:512].rearrange("p (g d) -> p g d", d=64)),
                            reads=[pgb], writes=[VsA_b])
                    ph.op("act", lambda e, jp=jp, trk=trk: e.activation(
                        out=KsE[0:64, :, jp * 256:(jp + 1) * 256].rearrange("p g (a t) -> p g a t", t=128),
                        in_=trk[0:64, :].rearrange("p (a g t) -> p g a t", a=2, g=4), func=AF.Copy),
                        reads=[trkb], writes=[KsE_b])
                for jp in range(8):
                    trk, trkb = trbank()
                    for a in range(2):
                        j = 2 * jp + a
                        pg, pgb = gather(pcmp, s * 16 + j)
                        for q4 in range(4):
                            ph.op("pe", lambda e, a=a, q4=q4, trk=trk, pg=pg: e.transpose(
                                out=trk[:, (a * 4 + q4) * 128:(a * 4 + q4 + 1) * 128], in_=pg[:, q4 * 128:(q4 + 1) * 128],
                                identity=ident[:, :]), reads=[pgb, ident_b], writes=[trkb])
                    ph.op("dve", lambda e, jp=jp, trk=trk: e.tensor_copy(
                        out=CX[:, :, jp * 256:(jp + 1) * 256].rearrange("p q (a t) -> p q a t", t=128),
                        in_=trk[:, :].rearrange("p (a q t) -> p q a t", a=2, q=4)),
                        reads=[trkb], writes=[CX_b])
                ph.dma("pool", lambda e, s=s: e.dma_start(
                    out=wsb[:, :, :], in_=swin[s * 512:(s + 1) * 512, :].rearrange("(c p) n -> p c n", p=128)),
                    writes=[wsb_b])
                for half in range(2):
                    trk, trkb = trbank()
                    for c2 in range(2):
                        c = half * 2 + c2
                        for g in range(4):
                            ph.op("pe", lambda e, c=c, c2=c2, g=g, trk=trk: e.transpose(
                                out=trk[0:64, (c2 * 4 + g) * 128:(c2 * 4 + g + 1) * 128], in_=wsb[:, c, g * 64:(g + 1) * 64],
                                identity=ident[:, :]), reads=[wsb_b, ident_b], writes=[trkb])
                    ph.op("act", lambda e, half=half, trk=trk: e.activation(
                        out=KwT[:, :, half * 256:(half + 1) * 256].rearrange("p g (a t) -> p g a t", t=128),
                        in_=trk[0:64, :].rearrange("p (a g t) -> p g a t", a=2, g=4), func=AF.Copy),
                        reads=[trkb], writes=[KwT_b])
                ph.op("pool", lambda e: e.tensor_copy(
                    out=VwA[:, :, :, 0:64], in_=wsb[:, :, 256:512].rearrange("p c (g d) -> p c g d", d=64)),
                    reads=[wsb_b], writes=[VwA_b])
                for gp in range(2):
                    for k in range(2):
                        bk, bkb = mmbank()
                        for j in range(32):
                            ph.op("pe", lambda e, k=k, j=j, bk=bk, gp=gp: e.matmul(
                                bk[:, 0:127], lhsT=w1bd[:, k, j, :], rhs=CX[:, k * 2 + gp, j:j + 2017:16],
                                start=(j == 0), stop=(j == 31)), reads=[cw_b, CX_b], writes=[bkb])
                        ph.op("act", lambda e, k=k, bk=bk: e.activation(out=hidc[:, 0:127], in_=bk[:, 0:127], func=AF.Silu,
                                                                        bias=peb[:, k:k + 1]),
                              reads=[bkb, cw_b], writes=[hidc_b])
                        for g2 in range(2):
                            g = gp * 2 + g2
                            po = g2 * 64
                            b2, b2b = mmbank()
                            if k == 0:
                                ph.op("pe", lambda e, b2=b2, po=po: e.matmul(
                                    b2[0:64, 0:127], lhsT=w2_t[po:po + 64, 0, :], rhs=hidc[po:po + 64, 0:127],
                                    start=True, stop=True), reads=[cw_b, hidc_b], writes=[b2b])
                                ph.op("act", lambda e, b2=b2, g=g: e.activation(out=KCs[:, g, 0:127], in_=b2[0:64, 0:127],
                                                                                func=AF.Copy), reads=[b2b], writes=[KCs_b])
                            else:
                                ph.op("pe", lambda e, b2=b2, po=po: e.matmul(
                                    b2[0:127, 0:64], lhsT=hidc[po:po + 64, 0:127], rhs=w2_t[po:po + 64, 1, :],
                                    start=True, stop=True), reads=[cw_b, hidc_b], writes=[b2b])
                                ph.op("act", lambda e, b2=b2, g=g: e.activation(out=VCs[0:127, g, 0:64], in_=b2[0:127, 0:64],
                                                                                func=AF.Copy), reads=[b2b], writes=[VCs_b])
                for g in range(4):
                    qs, qsb = ph.rotbuf("QSa", QSa)
                    ph.op("act", lambda e, qs=qs, g=g, s=s: e.activation(
                        out=qs[0:64, :].rearrange("p (r t) -> p r t", t=8), in_=QG[:, :, g, s * 8:(s + 1) * 8], func=AF.Copy),
                        reads=[QG_b], writes=[qsb])
                    sbk, sbb = mmbank()
                    ph.op("pe", lambda e, sbk=sbk, g=g, qs=qs: e.matmul(sbk[0:127, 0:32], lhsT=KCs[:, g, 0:127], rhs=qs[0:64, :],
                                                                      start=True, stop=True), reads=[KCs_b, qsb], writes=[sbb])
                    pc, pcb = ph.rotbuf("PTc", PTc)
                    ph.op("act", lambda e, sbk=sbk, pc=pc: e.activation(out=pc[0:127, :], in_=sbk[0:127, 0:32], func=AF.Exp),
                          reads=[sbb], writes=[pcb])
                    oc, ocb = obank()
                    ph.op("pe", lambda e, oc=oc, g=g, pc=pc: e.matmul(oc[0:65, 0:32], lhsT=VCs[0:127, g, :], rhs=pc[0:127, :],
                                                                    start=True, stop=True), reads=[VCs_b, pcb], writes=[ocb])
                    ph.op("act", lambda e, oc=oc, g=g, s=s: e.activation(
                        out=OTall[0:65, 0, g, :].rearrange("p (r c) -> p r c", c=128)[:, :, s * 8:(s + 1) * 8],
                        in_=oc[0:65, 0:32].rearrange("p (r t) -> p r t", t=8), func=AF.Copy), reads=[ocb], writes=[OT_b])
                    ph.op("pe", lambda e, pc=pc: e.matmul(rbank[0:32, 0:34], lhsT=pc[0:127, :], rhs=aggS[0:127, :],
                                                          start=True, stop=True), reads=[pcb, k_b], writes=[rbank_b])
                    ph.op("dve", lambda e: e.reciprocal(out=Rn[:, 33:34], in_=rbank[0:32, 33:34]), reads=[rbank_b], writes=[Rn_b])
                    ph.op("dve", lambda e: e.tensor_scalar(out=Rn[:, 0:33], in0=rbank[0:32, 0:33], scalar1=Rn[:, 33:34],
                                                           scalar2=None, op0=ALU.mult), reads=[rbank_b, Rn_b], writes=[Rn_b])
                    ib, ibb = mmbank()
                    ph.op("pe", lambda e, ib=ib: e.matmul(ib[0:8, 0:33], lhsT=sel4[:, :], rhs=Rn[:, 0:33], start=True, stop=True),
                          reads=[Rn_b, k_b], writes=[ibb])
                    ph.op("dve", lambda e, ib=ib: e.tensor_tensor(out=scs[:, 0:33], in0=ib[0:8, 0:33], in1=addmS[:, :], op=ALU.add),
                          reads=[ibb, k_b], writes=[scs_b])
                    ph.op("dve", lambda e: e.max(out=m8[:, 0:8], in_=scs[:, 0:33]), reads=[scs_b], writes=[scs_b])
                    ph.op("dve", lambda e: e.match_replace(out=scs[:, 64:97], in_to_replace=m8[:, 0:8],
                                                           in_values=scs[:, 0:33], imm_value=-3.0e9),
                          reads=[scs_b], writes=[scs_b])
                    ph.op("dve", lambda e: e.max(out=m8[:, 8:16], in_=scs[:, 64:97]), reads=[scs_b], writes=[scs_b])
                    ph.op("dve", lambda e: e.tensor_scalar(out=Xs[:, 64:97], in0=scs[:, 0:33], scalar1=m8[:, 15:16], scalar2=NEGM,
                                                           op0=ALU.is_lt, op1=ALU.mult), reads=[scs_b], writes=[Xs_b])
                    trk, trkb = trbank()
                    ph.op("pe", lambda e, trk=trk: e.transpose(out=trk[:, 0:8], in_=Xs[0:8, :], identity=ident[0:8, 0:8]),
                          reads=[Xs_b, ident_b], writes=[trkb])
                    ph.op("act", lambda e, trk=trk, qs=qs: e.activation(
                        out=qs[64:128, :].rearrange("p (r t) -> p r t", t=8),
                        in_=trk[64:128, 0:8].unsqueeze(1).to_broadcast([64, 4, 8]), func=AF.Copy), reads=[trkb], writes=[qsb])
                    sbk, sbb = mmbank()
                    for c in range(16):
                        ph.op("pe", lambda e, sbk=sbk, c=c, g=g, qs=qs: e.matmul(
                            sbk[:, c * 32:(c + 1) * 32], lhsT=KsE[:, g, c * 128:(c + 1) * 128], rhs=qs[:, :],
                            start=True, stop=True, skip_group_check=True), reads=[KsE_b, KsEe_b, qsb], writes=[sbb])
                    pt, ptb = ph.rotbuf("PTs", PTs)
                    ph.op("act", lambda e, sbk=sbk, pt=pt: e.activation(out=pt[:, :], in_=sbk[:, :], func=AF.Exp),
                          reads=[sbb], writes=[ptb])
                    osl, oslb = obank()
                    for c in range(16):
                        ph.op("pe", lambda e, osl=osl, c=c, g=g, pt=pt: e.matmul(
                            osl[0:65, 0:32], lhsT=VsA[:, c, g, :], rhs=pt[:, c * 32:(c + 1) * 32], start=(c == 0), stop=False),
                            reads=[VsA_b, ptb], writes=[oslb])
                    pn_s = pnA[:, 0, g, :].rearrange("p (r c) -> p r c", c=128)[:, :, s * 8:(s + 1) * 8]
                    pn_w = pnA[:, 1, g, :].rearrange("p (r c) -> p r c", c=128)[:, :, s * 8:(s + 1) * 8]
                    pnb_w = pnA_b
                    ph.op("pe", lambda e, osl=osl, g=g, pn_s=pn_s: e.matmul(
                        osl[0:65, 0:32].rearrange("p (r t) -> p r t", t=8), lhsT=VnA[:, 0, g, :], rhs=pn_s,
                        start=False, stop=True), reads=[VnA_b, pnA_b], writes=[oslb])
                    ph.op("act", lambda e, osl=osl, g=g, s=s: e.activation(
                        out=OTall[0:65, 1, g, :].rearrange("p (r c) -> p r c", c=128)[:, :, s * 8:(s + 1) * 8],
                        in_=osl[0:65, 0:32].rearrange("p (r t) -> p r t", t=8), func=AF.Copy), reads=[oslb], writes=[OT_b])
                    sbk, sbb = mmbank()
                    for c in range(4):
                        ph.op("pe", lambda e, sbk=sbk, c=c, g=g, qs=qs: e.matmul(
                            sbk[:, c * 32:(c + 1) * 32], lhsT=KwT[:, g, c * 128:(c + 1) * 128], rhs=qs[0:64, :],
                            start=True, stop=True, skip_group_check=True), reads=[KwT_b, qsb], writes=[sbb])
                    pw, pwb = ph.rotbuf("PTw", PTw)
                    ph.op("act", lambda e, sbk=sbk, pw=pw: e.activation(out=pw[:, :], in_=sbk[:, 0:128], func=AF.Exp),
                          reads=[sbb], writes=[pwb])
                    ph.op("dve", lambda e, pw=pw: e.tensor_tensor(out=pw[:, 0:32], in0=pw[:, 0:32], in1=maskW0[:, :], op=ALU.mult),
                          reads=[pwb, k_b], writes=[pwb])
                    ow, owb = obank()
                    for c in range(4):
                        ph.op("pe", lambda e, ow=ow, c=c, g=g, pw=pw: e.matmul(
                            ow[0:65, 0:32], lhsT=VwA[:, c, g, :], rhs=pw[:, c * 32:(c + 1) * 32], start=(c == 0), stop=False),
                            reads=[VwA_b, pwb], writes=[owb])
                    ph.op("pe", lambda e, ow=ow, g=g, pn_w=pn_w: e.matmul(
                        ow[0:65, 0:32].rearrange("p (r t) -> p r t", t=8), lhsT=VnA[:, 1, g, :], rhs=pn_w,
                        start=False, stop=True), reads=[VnA_b, pnb_w], writes=[owb])
                    ph.op("act", lambda e, ow=ow, g=g, s=s: e.activation(
                        out=OTall[0:65, 2, g, :].rearrange("p (r c) -> p r c", c=128)[:, :, s * 8:(s + 1) * 8],
                        in_=ow[0:65, 0:32].rearrange("p (r t) -> p r t", t=8), func=AF.Copy), reads=[owb], writes=[OT_b])

            iq = NPT
            for g in range(4):
                ph.dma("pool", lambda e, g=g: e.dma_start(
                    out=wbo[:, :, :], in_=b_out[g * 256:(g + 1) * 256, :].rearrange("(c p) n -> p c n", p=128)),
                    writes=[wbo_b])
                for branch in range(3):
                    tb_, tbb = trbank()
                    for r4 in range(4):
                        ph.op("pe", lambda e, r4=r4, tb_=tb_, g=g, branch=branch: e.transpose(
                            out=tb_[:, r4 * 66:r4 * 66 + 65], in_=OTall[0:65, branch, g, r4 * 128:(r4 + 1) * 128],
                            identity=ident[0:65, 0:65]), reads=[OT_b, ident_b], writes=[tbb])
                    tv = tb_[:, 0:264].rearrange("p (r c) -> p r c", c=66)
                    r_, rb = ph.rotbuf("rl", rl)
                    ph.op("dve", lambda e, r_=r_, tv=tv: e.tensor_scalar(out=r_[:, 0:4], in0=tv[:, :, 64], scalar1=1e-30, scalar2=None,
                                                                         op0=ALU.max), reads=[tbb], writes=[rb])
                    ph.op("dve", lambda e, r_=r_: e.reciprocal(out=r_[:, 0:4], in_=r_[:, 0:4]), reads=[rb], writes=[rb])
                    gv = gates[:, iq, g * 12:(g + 1) * 12].rearrange("p (r b) -> p r b", b=3)[:, :, branch]
                    ph.op("dve", lambda e, r_=r_, gv=gv: e.tensor_tensor(out=r_[:, 4:8], in0=r_[:, 0:4], in1=gv, op=ALU.mult),
                          reads=[rb, gates_b], writes=[rb])
                    glb = r_[:, 4:8].unsqueeze(2).to_broadcast([128, 4, 64])
                    if branch == 0:
                        ph.op("dve", lambda e, tv=tv, glb=glb: e.tensor_tensor(
                            out=acc[:, :].rearrange("p (r d) -> p r d", d=64), in0=tv[:, :, 0:64], in1=glb, op=ALU.mult),
                            reads=[tbb, rb], writes=[acc_b])
                    else:
                        ph.op("dve", lambda e, tv=tv, glb=glb: e.tensor_tensor(
                            out=acct[:, :].rearrange("p (r d) -> p r d", d=64), in0=tv[:, :, 0:64], in1=glb, op=ALU.mult),
                            reads=[tbb, rb], writes=[acct_b])
                        ph.op("dve", lambda e: e.tensor_tensor(out=acc[:, :], in0=acc[:, :], in1=acct[:, :], op=ALU.add),
                              reads=[acct_b, acc_b], writes=[acc_b])
                ph.op("act", lambda e: e.activation(out=accb[:, :], in_=acc[:, :], func=AF.Copy), reads=[acc_b], writes=[acct_b])
                trk, trkb = trbank()
                for a in range(2):
                    ph.op("pe", lambda e, a=a, trk=trk: e.transpose(out=trk[:, a * 128:(a + 1) * 128],
                                                                   in_=accb[:, a * 128:(a + 1) * 128], identity=ident[:, :]),
                          reads=[acct_b, ident_b], writes=[trkb])
                ph.op("act", lambda e, trk=trk: e.activation(out=accT[:, :, :], in_=trk[:, 0:256].rearrange("p (a q) -> p a q", q=128),
                                                             func=AF.Copy), reads=[trkb], writes=[accT_b])
                for half in range(2):
                    bo, bob = mmbank()
                    for a in range(2):
                        ph.op("pe", lambda e, a=a, bo=bo, half=half: e.matmul(
                            bo[:, :], lhsT=accT[:, a, :], rhs=wbo[:, a, half * 512:(half + 1) * 512],
                            start=(a == 0), stop=(a == 1)), reads=[accT_b, wbo_b], writes=[bob])
                    ph.op("dve", lambda e, bo=bo, half=half: e.tensor_tensor(
                        out=h[:, iq, half * 512:(half + 1) * 512], in0=bo[:, :],
                        in1=h[:, iq, half * 512:(half + 1) * 512], op=ALU.add), reads=[bob, hb[iq]], writes=[hb[iq]])
            ph.emit()

        if mode == "samp_test":
            ph.op("pool", lambda e: e.memset(identf[:], 0.0), writes=[ident_b])
            ph.op("pool", lambda e: e.affine_select(out=identf[:], in_=identf[:], pattern=[[-1, 128]],
                                                    compare_op=ALU.not_equal, fill=1.0, base=0,
                                                    channel_multiplier=1), reads=[ident_b], writes=[ident_b])
            ph.op("pool", lambda e: e.tensor_copy(out=ident[:], in_=identf[:]), reads=[ident_b], writes=[ident_b])
            ph.dma("sp", lambda e: e.dma_start(out=gates[:, :, :].rearrange("p a b -> p (a b)"), in_=gates_in[:, :]),
                   writes=[gates_b])
            ph.dma("sp", lambda e: e.dma_start(out=skv[:, :, :], in_=skv_in[:, :, :]), writes=[skv_b])
            ph.dma("sp", lambda e: e.dma_start(out=h[:, NPT, :], in_=xs[:, :]), writes=[hb[NPT]])
            sample_phase()
            ph.dma("sp", lambda e: e.dma_start(out=y_o[NPT * 128:NT * 128, :], in_=h[:, NPT, :]), reads=[hb[NPT]])
            ph.emit()
        elif mode == "attn_test":
            ph.op("pool", lambda e: e.memset(identf[:], 0.0), writes=[ident_b])
            ph.op("pool", lambda e: e.affine_select(out=identf[:], in_=identf[:], pattern=[[-1, 128]],
                                                    compare_op=ALU.not_equal, fill=1.0, base=0,
                                                    channel_multiplier=1), reads=[ident_b], writes=[ident_b])
            ph.op("pool", lambda e: e.tensor_copy(out=ident[:], in_=identf[:]), reads=[ident_b], writes=[ident_b])
            ph.dma("sp", lambda e: e.dma_start(out=gates[:, :, :].rearrange("p a b -> p (a b)"), in_=gates_in[:, :]),
                   writes=[gates_b])
            for ti in range(NT):
                src = xp[ti * 128:(ti + 1) * 128, :] if ti < NPT else xs[:, :]
                ph.dma("sp", lambda e, ti=ti, src=src: e.dma_start(out=h[:, ti, :], in_=src), writes=[hb[ti]])
            attn_phase()
            for ti in range(NT):
                ph.dma("sp", lambda e, ti=ti: e.dma_start(out=y_o[ti * 128:(ti + 1) * 128, :], in_=h[:, ti, :]),
                       reads=[hb[ti]])
            ph.emit()
        elif mode == "dense_only":
            dense_phase("p1", "p1", groups)
            dense_phase("p5", "p5", groups)
        else:
            dense_phase("p1", "p1", groups)
            rg = [[0, 1, 2, 3], [4, 5, 6, 7]]
            for c in range(4):
                ph.dma("pool", lambda e, c=c: e.collective_compute(
                    "AllGather", ALU.bypass, replica_groups=rg,
                    ins=[ftl[c].rearrange("a p t -> (a p) t").opt()],
                    outs=[gftl[c].rearrange("r a p t -> (r a p) t").opt()]), semq="cc", inc=1)
            for c in range(2):
                ph.dma("pool", lambda e, c=c: e.collective_compute(
                    "AllGather", ALU.bypass, replica_groups=rg,
                    ins=[vtl[c].opt()], outs=[gvtl[c].rearrange("r t d -> (r t) d").opt()]), semq="cc", inc=1)
            sample_phase()
            attn_phase()
            dense_phase("p5", "p5", groups)
    return nc


_NC_CACHE = {}


def kernel(x_prompt, x_sample, cache_cmp_kv, cache_slc_kv, state_win_kv, state_conv, page_table,
           norm_w, final_norm_w, a_in_w, a_conv_w, a_out_w, b_in_w, b_out_w, kv_norm_w, kv_w,
           cmp_pe, cmp_w1, cmp_w2, ffn_in_w, ffn_out_w):
    f = lambda a: np.ascontiguousarray(np.asarray(a, dtype=np.float32))
    x_prompt = f(x_prompt); x_sample = f(x_sample)
    if "nc" not in _NC_CACHE:
        _NC_CACHE["nc"] = build_nc()
    nc = _NC_CACHE["nc"]
    norms = np.stack([f(norm_w)[0, 0], f(norm_w)[0, 1], f(norm_w)[1, 0], f(norm_w)[1, 1],
                      f(final_norm_w), f(kv_norm_w)], 0)
    nwc = np.ascontiguousarray(norms.reshape(6, 8, 128).transpose(2, 0, 1).reshape(128, 48))
    convw = np.ascontiguousarray(f(a_conv_w)[0].reshape(3, 8, 128).transpose(2, 0, 1).reshape(128, 24))
    fnw = f(final_norm_w).reshape(1, D)
    shared = dict(nwc=nwc, fnw=fnw, convw=convw, a_in=f(a_in_w)[0], a_out=f(a_out_w)[0], ffn_in=f(ffn_in_w),
                  ffn_out=f(ffn_out_w), kv_w=f(kv_w), b_in=f(b_in_w)[0], b_out=f(b_out_w)[0],
                  cmp_w1=f(cmp_w1), cmp_w2=f(cmp_w2), cmp_pe=f(cmp_pe),
                  pcmp=f(cache_cmp_kv).reshape(-1, 512), pslc=f(cache_slc_kv).reshape(-1, 512))
    ptab_all = np.ascontiguousarray(np.asarray(page_table, dtype=np.int32))
    swin_all = f(state_win_kv).reshape(128, 512, 512)
    sconv_all = f(state_conv)[0]
    in_maps = []
    for c in range(NCORES):
        n, cc = c // 4, c % 4
        xs_seq = x_prompt[n].reshape(64, 128, D)
        xp = np.ascontiguousarray(xs_seq[cc::4].reshape(NPT * 128, D))
        xh = np.zeros((32, D), np.float32)
        for i in range(NPT):
            qt = 4 * i + cc
            if qt > 0:
                xh[2 * i:2 * i + 2] = x_prompt[n, qt * 128 - 2:qt * 128]
        m = dict(shared)
        meta = np.zeros((128, 32), np.float32)
        pidx = np.arange(128)
        meta[:, 0] = 2 * cc + (pidx >= 64)
        meta[:, 1] = 128 * cc
        for j in range(4):
            meta[:, 2 + j] = 1.0 if j != cc else 0.0
        for d in range(8):
            rel = d - 4 - cc
            meta[:, 6 + d] = 1.0 if -4 < rel < 0 else 0.0
            meta[:, 14 + d] = 1.0 if rel == 0 else 0.0
            meta[:, 22 + d] = 1.0 if rel == -4 else 0.0
        m["meta"] = meta
        m["ptab"] = np.ascontiguousarray(ptab_all[16 * c:16 * c + 16].reshape(1, 256))
        m.update(xp=xp, xh=xh, xs=np.ascontiguousarray(x_sample[16 * c:16 * c + 16].reshape(128, D)),
                 sconv=np.ascontiguousarray(sconv_all[16 * c:16 * c + 16].reshape(32, D)),
                 swin=np.ascontiguousarray(swin_all[16 * c:16 * c + 16].reshape(16 * 512, 512)))
        in_maps.append(m)
    res = run_bass_kernel_spmd(nc, in_maps, core_ids=list(range(NCORES))).results

    y_prompt = np.zeros((2, 8192, D), np.float32)
    y_sample = np.zeros((128, 8, D), np.float32)
    conv_p = np.zeros((1, 2, 2, D), np.float32)
    conv_s = np.zeros((1, 128, 2, D), np.float32)
    cmp_p = np.zeros((2, 8192, 2, 4, 64), np.float32)
    slc_p = np.zeros((2, 8192, 2, 4, 64), np.float32)
    cmp_s = np.zeros((128, 8, 2, 4, 64), np.float32)
    slc_s = np.zeros((128, 8, 2, 4, 64), np.float32)
    win_p = np.zeros((2, 512, 2, 4, 64), np.float32)
    win_s = np.zeros((128, 512, 2, 4, 64), np.float32)
    for c in range(NCORES):
        n, cc = c // 4, c % 4
        r = res[c]
        yv = y_prompt[n].reshape(64, 128, D)
        yv[cc::4] = r["y_o"][:NPT * 128].reshape(NPT, 128, D)
        y_sample[16 * c:16 * c + 16] = r["y_o"][NPT * 128:].reshape(16, 8, D)
        cmp_p[n].reshape(64, 128, 512)[cc::4] = r["cmp_o"][:NPT * 128].reshape(NPT, 128, 512)
        slc_p[n].reshape(64, 128, 512)[cc::4] = r["slc_o"][:NPT * 128].reshape(NPT, 128, 512)
        cmp_s[16 * c:16 * c + 16] = r["cmp_o"][NPT * 128:].reshape(16, 8, 2, 4, 64)
        slc_s[16 * c:16 * c + 16] = r["slc_o"][NPT * 128:].reshape(16, 8, 2, 4, 64)
        conv_s[0, 16 * c:16 * c + 16] = r["conv_o"][0:32].reshape(16, 2, D)
        if cc == 3:
            conv_p[0, n] = r["conv_o"][32:34]
        win_p[n, cc * 128:(cc + 1) * 128] = r["winp_o"].reshape(128, 2, 4, 64)
        win_s[16 * c:16 * c + 16] = r["wins_o"].reshape(16, 512, 2, 4, 64)
    return (y_prompt, y_sample, conv_p, conv_s, cmp_p, cmp_s, slc_p, slc_s, win_p, win_s)
```
